# Optimizing a Trainium2 kernel written in Bass

```python
import math
import jax, jax.numpy as jnp
from jax import lax
import numpy as np

D_MODEL = 1024
BATCH = 32
SEQ = 2048
DEPTH = 1

DN_HEADS = D_MODEL // 256
DN_DK = 128
DN_DV = 128
CONV_WIDTH = 4
CHUNK = 64
DIFF_HEADS = D_MODEL // 256
DIFF_DQK = 64
DIFF_DV = 2 * DIFF_DQK
ROPE_THETA = 500000.0
ROPE_DIM = DIFF_DQK // 4
Q_BLOCK = 128
D_FF = 4 * D_MODEL
EPS = 1e-6

DN_QK = DN_HEADS * DN_DK
DN_V = DN_HEADS * DN_DV
DN_CONV = 2 * DN_QK + DN_V
DF_QK = DIFF_HEADS * 2 * DIFF_DQK
DF_V = DIFF_HEADS * DIFF_DV
MIX_WIDTH = DN_V + DF_V
D_IN = DN_CONV + DN_V + 2 * DN_HEADS + 2 * DF_QK + DF_V

kernel_name = "hymba_gated_deltanet_diff_attn_layer"


def rmsnorm(x, w):
    x32 = x.astype(jnp.float32)
    y = x32 * lax.rsqrt(jnp.mean(x32 * x32, axis=-1, keepdims=True) + EPS)
    return (y * w.astype(jnp.float32)).astype(x.dtype)


def l2norm(x):
    return x * lax.rsqrt(jnp.sum(x * x, axis=-1, keepdims=True) + 1e-6)


def rotary_tables(positions):
    inv_freq = ROPE_THETA ** (-jnp.arange(0, ROPE_DIM, 2, dtype=jnp.float32) / ROPE_DIM)
    ang = positions.astype(jnp.float32)[:, None] * inv_freq[None, :]
    return jnp.cos(ang), jnp.sin(ang)


def partial_rotary(x, cos, sin):
    half = ROPE_DIM // 2
    c = cos.astype(x.dtype)
    s = sin.astype(x.dtype)
    x1 = x[..., :half]
    x2 = x[..., half:ROPE_DIM]
    return jnp.concatenate([x1 * c - x2 * s, x2 * c + x1 * s, x[..., ROPE_DIM:]], axis=-1)


def causal_depthwise_conv(x, w):
    rhs = w[:, None, :].astype(x.dtype)
    return lax.conv_general_dilated(
        x, rhs, window_strides=(1,), padding=[(CONV_WIDTH - 1, 0)],
        dimension_numbers=("NWC", "WIO", "NWC"), feature_group_count=x.shape[-1])


def gated_delta_rule_chunked(q, k, v, g, beta):
    b, seq, h, dk = q.shape
    dv = v.shape[-1]
    n = seq // CHUNK
    q = l2norm(q) * (dk ** -0.5)
    k = l2norm(k)

    def chunks(t):
        return t.reshape(b, n, CHUNK, h, -1).transpose(0, 3, 1, 2, 4)

    q, k, v = chunks(q), chunks(k), chunks(v)
    g = jnp.cumsum(g.reshape(b, n, CHUNK, h).transpose(0, 3, 1, 2), axis=-1)
    beta = beta.reshape(b, n, CHUNK, h).transpose(0, 3, 1, 2)

    idx = jnp.arange(CHUNK)
    incl = idx[:, None] >= idx[None, :]
    strict = idx[:, None] > idx[None, :]
    decay = jnp.exp(jnp.where(incl, g[..., :, None] - g[..., None, :], -jnp.inf))

    kb = k * beta[..., None]
    a_strict = jnp.einsum("bhnid,bhnjd->bhnij", kb, k) * jnp.where(strict, decay, 0.0)
    lhs = a_strict + jnp.eye(CHUNK, dtype=a_strict.dtype)
    u = lax.linalg.triangular_solve(lhs, v * beta[..., None], left_side=True, lower=True,
                                    unit_diagonal=True)
    w = lax.linalg.triangular_solve(lhs, kb * jnp.exp(g)[..., None], left_side=True,
                                    lower=True, unit_diagonal=True)
    qk = jnp.einsum("bhnid,bhnjd->bhnij", q, k) * decay
    g_last = g[..., -1:]
    q_state = q * jnp.exp(g)[..., None]
    k_state = k * jnp.exp(g_last - g)[..., None]
    chunk_decay = jnp.exp(g_last[..., 0])

    xs = tuple(jnp.moveaxis(t, 2, 0) for t in (q_state, k_state, u, w, qk, chunk_decay))

    def step(state, inp):
        qs, ks, u_n, w_n, qk_n, dec = inp
        v_new = u_n - jnp.einsum("bhcd,bhde->bhce", w_n, state)
        o = jnp.einsum("bhcd,bhde->bhce", qs, state) + jnp.einsum("bhij,bhje->bhie", qk_n, v_new)
        state = state * dec[..., None, None] + jnp.einsum("bhcd,bhce->bhde", ks, v_new)
        return state, o

    s0 = jnp.zeros((b, h, dk, dv), jnp.float32)
    _, o = lax.scan(step, s0, xs)
    return o.transpose(1, 0, 3, 2, 4).reshape(b, seq, h, dv)


def diff_attention(q1, q2, k1, k2, v, lam):
    b, h, seq, _ = q1.shape
    nb = seq // Q_BLOCK
    scale = DIFF_DQK ** -0.5
    kpos = jnp.arange(seq)

    def block(i):
        start = i * Q_BLOCK
        qpos = start + jnp.arange(Q_BLOCK)
        mask = kpos[None, :] <= qpos[:, None]

        def probs(q, k):
            qb = lax.dynamic_slice_in_dim(q, start, Q_BLOCK, axis=2)
            s = jnp.einsum("bhqd,bhkd->bhqk", qb, k).astype(jnp.float32) * scale
            return jax.nn.softmax(jnp.where(mask, s, -jnp.inf), axis=-1)

        att = probs(q1, k1) - lam * probs(q2, k2)
        return jnp.einsum("bhqk,bhkd->bhqd", att.astype(v.dtype), v)

    out = lax.map(block, jnp.arange(nb))
    return out.transpose(1, 2, 0, 3, 4).reshape(b, h, seq, v.shape[-1])


def hybrid_layer(x, cos, sin, layer_idx, attn_norm_w, w_in, conv_w, a_log, dt_bias,
                 dn_norm_w, lambda_q1, lambda_k1, lambda_q2, lambda_k2, diff_norm_w,
                 group_scale, w_out, mlp_norm_w, w_up, w_down):
    f32 = jnp.float32
    b, seq, _ = x.shape
    hn = rmsnorm(x, attn_norm_w)
    proj = hn @ w_in
    cuts = np.cumsum([DN_CONV, DN_V, DN_HEADS, DN_HEADS, DF_QK, DF_QK]).tolist()
    dn_qkv, dn_z, dn_b, dn_a, df_q, df_k, df_v = jnp.split(proj, cuts, axis=-1)

    dn_qkv = jax.nn.silu(causal_depthwise_conv(dn_qkv, conv_w))
    dq, dk, dv = jnp.split(dn_qkv, [DN_QK, 2 * DN_QK], axis=-1)
    dq = dq.reshape(b, seq, DN_HEADS, DN_DK).astype(f32)
    dk = dk.reshape(b, seq, DN_HEADS, DN_DK).astype(f32)
    dv = dv.reshape(b, seq, DN_HEADS, DN_DV).astype(f32)
    beta = jax.nn.sigmoid(dn_b.astype(f32))
    g = -jnp.exp(a_log.astype(f32)) * jax.nn.softplus(dn_a.astype(f32) + dt_bias.astype(f32))
    o_dn = gated_delta_rule_chunked(dq, dk, dv, g, beta)
    z = dn_z.reshape(b, seq, DN_HEADS, DN_DV).astype(f32)
    o_dn = (rmsnorm(o_dn, dn_norm_w) * jax.nn.silu(z)).reshape(b, seq, DN_V).astype(x.dtype)

    q = df_q.reshape(b, seq, DIFF_HEADS, 2, DIFF_DQK).transpose(0, 2, 3, 1, 4)
    k = df_k.reshape(b, seq, DIFF_HEADS, 2, DIFF_DQK).transpose(0, 2, 3, 1, 4)
    q = partial_rotary(q, cos, sin)
    k = partial_rotary(k, cos, sin)
    v = df_v.reshape(b, seq, DIFF_HEADS, DIFF_DV).transpose(0, 2, 1, 3)
    lambda_init = 0.8 - 0.6 * math.exp(-0.3 * layer_idx)
    lam = (jnp.exp(jnp.sum(lambda_q1.astype(f32) * lambda_k1.astype(f32)))
           - jnp.exp(jnp.sum(lambda_q2.astype(f32) * lambda_k2.astype(f32))) + lambda_init)
    o_df = diff_attention(q[:, :, 0], q[:, :, 1], k[:, :, 0], k[:, :, 1], v, lam)
    o_df = rmsnorm(o_df, diff_norm_w) * (1.0 - lambda_init)
    o_df = o_df.transpose(0, 2, 1, 3).reshape(b, seq, DF_V)

    mixed = jnp.concatenate([o_dn, o_df], axis=-1) * group_scale
    x = x + mixed @ w_out

    hm = rmsnorm(x, mlp_norm_w)
    x = x + jnp.square(jax.nn.relu(hm @ w_up)) @ w_down
    return x


def setup_inputs(seed: int = 0) -> dict:
    key = jax.random.key(seed)
    ks = jax.random.split(key, 20)
    f32 = jnp.float32

    def nrm(k, shape, scale):
        return jax.random.normal(k, shape, f32) * scale

    def gain(k, shape):
        return 1.0 + 0.02 * jax.random.normal(k, shape, f32)

    dt = jnp.exp(jax.random.uniform(ks[5], (DEPTH, DN_HEADS), f32, math.log(1e-3), math.log(1e-1)))
    return {
        "x": jax.random.normal(ks[0], (BATCH, SEQ, D_MODEL), f32),
        "positions": jnp.arange(SEQ, dtype=jnp.int32),
        "attn_norm_w": gain(ks[1], (DEPTH, D_MODEL)),
        "w_in": nrm(ks[2], (DEPTH, D_MODEL, D_IN), D_MODEL ** -0.5),
        "conv_w": nrm(ks[3], (DEPTH, CONV_WIDTH, DN_CONV), CONV_WIDTH ** -0.5),
        "a_log": jnp.log(jax.random.uniform(ks[4], (DEPTH, DN_HEADS), f32, 1.0, 16.0)),
        "dt_bias": dt + jnp.log(-jnp.expm1(-dt)),
        "dn_norm_w": gain(ks[6], (DEPTH, DN_DV)),
        "lambda_q1": nrm(ks[7], (DEPTH, DIFF_DQK), 0.1),
        "lambda_k1": nrm(ks[8], (DEPTH, DIFF_DQK), 0.1),
        "lambda_q2": nrm(ks[9], (DEPTH, DIFF_DQK), 0.1),
        "lambda_k2": nrm(ks[10], (DEPTH, DIFF_DQK), 0.1),
        "diff_norm_w": gain(ks[11], (DEPTH, DIFF_DV)),
        "group_scale": gain(ks[12], (DEPTH, MIX_WIDTH)),
        "w_out": nrm(ks[13], (DEPTH, MIX_WIDTH, D_MODEL), MIX_WIDTH ** -0.5),
        "mlp_norm_w": gain(ks[14], (DEPTH, D_MODEL)),
        "w_up": nrm(ks[15], (DEPTH, D_MODEL, D_FF), D_MODEL ** -0.5),
        "w_down": nrm(ks[16], (DEPTH, D_FF, D_MODEL), D_FF ** -0.5),
        "final_norm_w": gain(ks[17], (D_MODEL,)),
    }


def reference(x, positions, attn_norm_w, w_in, conv_w, a_log, dt_bias, dn_norm_w,
              lambda_q1, lambda_k1, lambda_q2, lambda_k2, diff_norm_w, group_scale,
              w_out, mlp_norm_w, w_up, w_down, final_norm_w):
    cos, sin = rotary_tables(positions)
    for l in range(DEPTH):
        x = hybrid_layer(x, cos, sin, l, attn_norm_w[l], w_in[l], conv_w[l], a_log[l],
                         dt_bias[l], dn_norm_w[l], lambda_q1[l], lambda_k1[l],
                         lambda_q2[l], lambda_k2[l], diff_norm_w[l], group_scale[l],
                         w_out[l], mlp_norm_w[l], w_up[l], w_down[l])
    return rmsnorm(x, final_norm_w)
```

```python
import math
import os
import numpy as np
from contextlib import ExitStack
import concourse.bass as bass
import concourse.mybir as mybir
from concourse.bass_utils import run_bass_kernel_spmd

F32 = mybir.dt.float32
BF16 = mybir.dt.bfloat16
I32 = mybir.dt.int32
AF = mybir.ActivationFunctionType
ALU = mybir.AluOpType

D = 1024
DIN = 3592
DFF = 4096
NCORES = 8
EPS = 1e-6
ROPE_THETA = 500000.0
LAMBDA_INIT = 0.8 - 0.6 * math.exp(-0.3 * 0)

EPOCH = 30000
ENGS = ("sync", "scalar", "vector", "gpsimd", "tensor")
COMPUTE = ("scalar", "vector", "gpsimd", "tensor")


class Res:
    __slots__ = ("w", "r", "excl")

    def __init__(self, excl=False):
        self.w = None
        self.r = []
        self.excl = excl


class Prog:
    def __init__(self, nc, n_dma_sems=(("sync", 24), ("scalar", 4), ("gpsimd", 8))):
        self.nc = nc
        self.ops = {e: [] for e in ENGS}
        self.cnt = {e: 0 for e in COMPUTE}
        self.waited = {e: {} for e in ENGS}
        self.dma_pool = {q: n for q, n in n_dma_sems}
        self.dma_uses = {q: [0] * n for q, n in n_dma_sems}
        self.dma_rr = {q: 0 for q, n in n_dma_sems}
        self.semkeys = set()
        self.stack = ExitStack()
        self.out_tokens = []
        self.last_tok = {}

    def sbuf(self, name, shape, dtype):
        return self.stack.enter_context(self.nc.sbuf_tensor(name, list(shape), dtype))

    def psum(self, name, shape, dtype=F32):
        return self.stack.enter_context(self.nc.psum_tensor(name, list(shape), dtype))

    def _need(self, eng, deps):
        best = {}
        for d in deps:
            if d is None:
                continue
            k, v = d
            if best.get(k, 0) < v:
                best[k] = v
        waits = []
        wd = self.waited[eng]
        for k, v in best.items():
            if eng == "tensor" and k[0] == "tensor":
                continue
            if wd.get(k, 0) >= v:
                continue
            wd[k] = v
            waits.append((k, v))
        return waits

    def _deps(self, reads, writes):
        deps = []
        for r in reads:
            deps.append(r.w)
            if r.excl:
                deps.extend(r.r)
        for w in writes:
            deps.append(w.w)
            deps.extend(w.r)
        return deps

    def _mark(self, tok, reads, writes):
        for r in reads:
            if r.excl:
                r.w = tok
                r.r = []
            else:
                r.r.append(tok)
        for w in writes:
            w.w = tok
            w.r = []
        self.last_tok[tok[0]] = tok

    def op(self, eng, fn, reads=(), writes=()):
        waits = self._need(eng, self._deps(reads, writes))
        self.cnt[eng] += 1
        c = self.cnt[eng]
        key = (eng, (c - 1) // EPOCH)
        tok = (key, (c - 1) % EPOCH + 1)
        self.semkeys.add(key)
        self._mark(tok, reads, writes)
        self.ops[eng].append((fn, waits, key, 1))
        return tok

    def dma(self, q, fn, reads=(), writes=(), is_output=False):
        deps = self._deps(reads, writes)
        i = self.dma_rr[q]
        self.dma_rr[q] = (i + 1) % self.dma_pool[q]
        key = ("dma_" + q, i)
        self.semkeys.add(key)
        prev = self.dma_uses[q][i]
        if prev > 0:
            deps.append((key, 16 * prev))
        self.dma_uses[q][i] = prev + 1
        tok = (key, 16 * (prev + 1))
        waits = self._need(q, deps)
        self._mark(tok, reads, writes)
        self.ops[q].append((fn, waits, key, 16))
        if is_output:
            self.out_tokens.append(tok)
        return tok

    def barrier(self):
        toks = list(self.last_tok.values())
        for e in ENGS:
            waits = self._need(e, toks)
            if waits:
                self.ops[e].append((None, waits, None, 0))

    def finish(self, q="sync"):
        waits = self._need(q, self.out_tokens)
        self.ops[q].append((None, waits, None, 0))

    def emit(self):
        nc = self.nc
        sems = {}
        for k in sorted(self.semkeys, key=str):
            sems[k] = self.stack.enter_context(nc.semaphore("s_%s_%d" % (k[0], k[1])))
        with nc.Block() as block:
            def make(engname):
                def body(e):
                    for fn, waits, key, inc in self.ops[engname]:
                        for (k, v) in waits:
                            e.wait_ge(sems[k], v)
                        if fn is not None:
                            fn(e).then_inc(sems[key], inc)
                return body
            block.sync(make("sync"))
            block.scalar(make("scalar"))
            block.vector(make("vector"))
            block.gpsimd(make("gpsimd"))
            block.tensor(make("tensor"))

    def close(self):
        self.stack.close()


class Arena:
    def __init__(self, P, name, kib):
        self.n = kib * 256
        self.t = P.sbuf(name, [128, self.n], F32)
        self.off = 0

    def reset(self, off=0):
        self.off = off

    def alloc(self, shape, dtype):
        n = 1
        for s in shape:
            n *= s
        nb = n * (2 if dtype == BF16 else 4)
        nw = (nb + 3) // 4
        nw = (nw + 7) // 8 * 8
        assert self.off + nw <= self.n, "arena overflow %d + %d > %d" % (self.off, nw, self.n)
        ap = self.t[:, self.off:self.off + nw]
        self.off += nw
        if dtype == BF16:
            ap = ap.bitcast(BF16)[:, 0:n]
        elif dtype == I32:
            ap = ap.bitcast(I32)[:, 0:n]
        else:
            ap = ap[:, 0:n]
        if len(shape) == 2:
            ap = ap.rearrange("p (a b) -> p a b", a=shape[0])
        elif len(shape) == 3:
            ap = ap.rearrange("p (a b c) -> p a b c", a=shape[0], b=shape[1])
        return ap


def build(NB, L, dbg=False, upto=None):
    NT = L // 128
    NG = L // 512
    assert L % 512 == 0
    nc = bass.Bass("TRN2", target_bir_lowering=False)
    x_d = nc.dram_tensor("x", [NB, L, D], F32, kind="ExternalInput").ap()
    pos_d = nc.dram_tensor("positions", [L], I32, kind="ExternalInput").ap()
    anw_d = nc.dram_tensor("attn_norm_w", [D], F32, kind="ExternalInput").ap()
    win_d = nc.dram_tensor("w_in", [D, DIN], F32, kind="ExternalInput").ap()
    convw_d = nc.dram_tensor("conv_w", [4, 1536], F32, kind="ExternalInput").ap()
    alog_d = nc.dram_tensor("a_log", [4], F32, kind="ExternalInput").ap()
    dtb_d = nc.dram_tensor("dt_bias", [4], F32, kind="ExternalInput").ap()
    dnw_d = nc.dram_tensor("dn_norm_w", [128], F32, kind="ExternalInput").ap()
    lq1_d = nc.dram_tensor("lambda_q1", [64], F32, kind="ExternalInput").ap()
    lk1_d = nc.dram_tensor("lambda_k1", [64], F32, kind="ExternalInput").ap()
    lq2_d = nc.dram_tensor("lambda_q2", [64], F32, kind="ExternalInput").ap()
    lk2_d = nc.dram_tensor("lambda_k2", [64], F32, kind="ExternalInput").ap()
    dfw_d = nc.dram_tensor("diff_norm_w", [128], F32, kind="ExternalInput").ap()
    gs_d = nc.dram_tensor("group_scale", [D], F32, kind="ExternalInput").ap()
    wout_d = nc.dram_tensor("w_out", [D, D], F32, kind="ExternalInput").ap()
    mnw_d = nc.dram_tensor("mlp_norm_w", [D], F32, kind="ExternalInput").ap()
    wup_d = nc.dram_tensor("w_up", [D, DFF], F32, kind="ExternalInput").ap()
    wdn_d = nc.dram_tensor("w_down", [DFF, D], F32, kind="ExternalInput").ap()
    fnw_d = nc.dram_tensor("final_norm_w", [D], F32, kind="ExternalInput").ap()
    y_d = nc.dram_tensor("y", [NB, L, D], F32, kind="ExternalOutput").ap()
    if dbg:
        dbg_d = nc.dram_tensor("dbg", [128, 8 * L], F32, kind="ExternalOutput").ap()
    win_s = nc.dram_tensor("win_s", [128, 8, DIN], BF16).ap()
    wout_s = nc.dram_tensor("wout_s", [128, 8, D], BF16).ap()
    wup_s = nc.dram_tensor("wup_s", [128, 8, DFF], BF16).ap()
    wdn_s = nc.dram_tensor("wdn_s", [128, 32, D], BF16).ap()

    P = Prog(nc)

    def early(tag):
        if upto != tag:
            return False
        P.barrier()
        zt = P.sbuf("zt_" + tag, [128, 64], F32)
        rz = Res()
        P.op("vector", lambda e: e.memset(zt[:], 1.0), writes=[rz])
        P.dma("sync", lambda e: e.dma_start(out=dbg_d[:, 0:64], in_=zt[:]), reads=[rz], is_output=True)
        P.finish(); P.emit(); P.close()
        return True
    V, S, G, T = "vector", "scalar", "gpsimd", "tensor"

    ident_bf = P.sbuf("ident_bf", [128, 128], BF16)
    ident_f = P.sbuf("ident_f", [128, 128], F32)
    ones_bf = P.sbuf("ones_bf", [128, 128], BF16)
    ones_f = P.sbuf("ones_f", [128, 128], F32)
    UI = P.sbuf("UI", [128, 128], F32)
    UI_bf = P.sbuf("UI_bf", [128, 128], BF16)
    negSL = P.sbuf("negSL", [128, 128], F32)
    cosT = P.sbuf("cosT", [128, NT, 8], F32)
    sinT = P.sbuf("sinT", [128, NT, 8], F32)
    cvec = P.sbuf("cvec", [128, 64], F32)
    convw = P.sbuf("convw", [128, 12, 4], F32)
    fnw_bc = P.sbuf("fnw_bc", [128, D], F32)
    R_const = Res()
    EPSC = cvec[:, 0:1]
    ONEC = cvec[:, 1:2]
    LAMN = cvec[:, 2:3]
    NEGA = cvec[:, 4:8]
    DTB = cvec[:, 8:12]
    EPSL2 = cvec[:, 12:13]

    main = Arena(P, "main", 199)
    hnT = P.sbuf("hnT", [128, 8, L], BF16) if False else None

    pb = [P.psum("pb%d" % i, [128, 512], F32) for i in range(7)]
    rpb = [Res(True) for _ in range(7)]
    ptr = P.psum("ptr", [128, 1024], BF16)
    rptr = Res(True)

    P.op(G, lambda e: e.memset(ident_f[:], 1.0), writes=[R_const])
    P.op(G, lambda e: e.affine_select(out=ident_f[:], in_=ident_f[:], pattern=[[1, 128]], compare_op=ALU.is_equal,
                                      fill=0.0, base=0, channel_multiplier=-1), reads=[R_const], writes=[R_const])
    P.op(V, lambda e: e.tensor_copy(out=ident_bf[:], in_=ident_f[:]), reads=[R_const], writes=[R_const])
    P.op(G, lambda e: e.memset(ones_f[:], 1.0), writes=[R_const])
    P.op(G, lambda e: e.memset(ones_bf[:], 1.0), writes=[R_const])
    P.op(G, lambda e: e.memset(UI[:], 1.0), writes=[R_const])
    P.op(G, lambda e: e.affine_select(out=UI[:], in_=UI[:], pattern=[[1, 128]], compare_op=ALU.is_ge,
                                      fill=0.0, base=0, channel_multiplier=-1), reads=[R_const], writes=[R_const])
    P.op(V, lambda e: e.tensor_copy(out=UI_bf[:], in_=UI[:]), reads=[R_const], writes=[R_const])
    P.op(G, lambda e: e.memset(negSL[:], -1.0), writes=[R_const])
    P.op(G, lambda e: e.affine_select(out=negSL[:], in_=negSL[:], pattern=[[-1, 128]], compare_op=ALU.is_gt,
                                      fill=0.0, base=0, channel_multiplier=1), reads=[R_const], writes=[R_const])
    P.op(V, lambda e: e.memset(cvec[:], 0.0), writes=[R_const])
    P.op(V, lambda e: e.memset(cvec[:, 0:1], EPS), reads=[R_const], writes=[R_const])
    P.op(V, lambda e: e.memset(cvec[:, 1:2], 1.0), reads=[R_const], writes=[R_const])
    P.op(V, lambda e: e.memset(cvec[:, 12:13], 1e-6), reads=[R_const], writes=[R_const])
    P.dma("sync", lambda e: e.dma_start(out=fnw_bc[:], in_=fnw_d.partition_broadcast(128)), writes=[R_const])
    P.dma("sync", lambda e: e.dma_start(out=cvec[:, 16:20], in_=alog_d.partition_broadcast(128)), reads=[R_const], writes=[R_const])
    P.dma("sync", lambda e: e.dma_start(out=cvec[:, 8:12], in_=dtb_d.partition_broadcast(128)), reads=[R_const], writes=[R_const])
    P.op(S, lambda e: e.activation(out=cvec[:, 20:24], in_=cvec[:, 16:20], func=AF.Exp), reads=[R_const], writes=[R_const])
    P.op(V, lambda e: e.tensor_scalar(out=cvec[:, 4:8], in0=cvec[:, 20:24], scalar1=-1.0, scalar2=None, op0=ALU.mult),
         reads=[R_const], writes=[R_const])

    if early('consts'):
        return nc
    main.reset()
    lt = main.alloc([4, 64], F32)
    R_s = Res()
    for i, d_ in enumerate((lq1_d, lk1_d, lq2_d, lk2_d)):
        P.dma("sync", lambda e, i=i, d_=d_: e.dma_start(out=lt[:, i, :], in_=d_.partition_broadcast(128)), reads=[R_s], writes=[R_s])
    lp = main.alloc([2, 64], F32)
    P.op(V, lambda e: e.tensor_tensor(out=lp[:, 0, :], in0=lt[:, 0, :], in1=lt[:, 1, :], op=ALU.mult), reads=[R_s], writes=[R_s])
    P.op(V, lambda e: e.tensor_tensor(out=lp[:, 1, :], in0=lt[:, 2, :], in1=lt[:, 3, :], op=ALU.mult), reads=[R_s], writes=[R_s])
    P.op(V, lambda e: e.reduce_sum(out=cvec[:, 24:25], in_=lp[:, 0, :], axis=mybir.AxisListType.X), reads=[R_s, R_const], writes=[R_const])
    P.op(V, lambda e: e.reduce_sum(out=cvec[:, 25:26], in_=lp[:, 1, :], axis=mybir.AxisListType.X), reads=[R_s, R_const], writes=[R_const])
    P.op(S, lambda e: e.activation(out=cvec[:, 26:28], in_=cvec[:, 24:26], func=AF.Exp), reads=[R_const], writes=[R_const])
    P.op(V, lambda e: e.tensor_tensor(out=cvec[:, 28:29], in0=cvec[:, 27:28], in1=cvec[:, 26:27], op=ALU.subtract), reads=[R_const], writes=[R_const])
    P.op(V, lambda e: e.tensor_scalar(out=cvec[:, 2:3], in0=cvec[:, 28:29], scalar1=-LAMBDA_INIT, scalar2=None, op0=ALU.add),
         reads=[R_const], writes=[R_const])
    cwr = main.alloc([1536], F32)
    P.dma("sync", lambda e: e.dma_start(out=cwr[0:4, :], in_=convw_d[:, :]), reads=[R_s], writes=[R_s])
    for c in range(12):
        P.op(T, lambda e, c=c: e.matmul(pb[0][:, c * 4:(c + 1) * 4], lhsT=cwr[0:4, c * 128:(c + 1) * 128], rhs=ident_f[0:4, 0:4],
                                         start=True, stop=True), reads=[R_s, R_const], writes=[rpb[0]])
    P.op(V, lambda e: e.tensor_copy(out=convw[:].rearrange("p a b -> p (a b)"), in_=pb[0][:, 0:48]), reads=[rpb[0]], writes=[R_const])
    posi = main.alloc([128], I32)
    posf = main.alloc([128], F32)
    P.dma("sync", lambda e: e.dma_start(out=posi[0:NT, :], in_=pos_d.rearrange("(n p) -> n p", p=128)), reads=[R_s], writes=[R_s])
    P.op(V, lambda e: e.tensor_copy(out=posf[0:NT, :], in_=posi[0:NT, :]), reads=[R_s], writes=[R_s])
    P.op(T, lambda e: e.matmul(pb[1][:, 0:NT], lhsT=posf[0:NT, :], rhs=ident_f[0:NT, 0:NT], start=True, stop=True),
         reads=[R_s, R_const], writes=[rpb[1]])
    post = main.alloc([NT], F32)
    P.op(V, lambda e: e.tensor_copy(out=post, in_=pb[1][:, 0:NT]), reads=[rpb[1]], writes=[R_s])
    invf = main.alloc([8], F32)
    for i in range(8):
        fr = float(np.float32(ROPE_THETA) ** np.float32(-(2.0 * i) / 16.0))
        P.op(V, lambda e, i=i, fr=fr: e.memset(invf[:, i:i + 1], fr), reads=[R_s], writes=[R_s])
    ang = main.alloc([NT, 8], F32)
    P.op(V, lambda e: e.tensor_tensor(out=ang, in0=post.unsqueeze(2).to_broadcast([128, NT, 8]),
                                      in1=invf.unsqueeze(1).to_broadcast([128, NT, 8]), op=ALU.mult), reads=[R_s], writes=[R_s])
    tA = main.alloc([NT, 8], F32)
    tB = main.alloc([NT, 8], F32)
    tI = main.alloc([NT, 8], I32)
    TWO_PI = 2.0 * math.pi
    for (shift, dst) in ((0.0, sinT), (math.pi / 2.0, cosT)):
        P.op(V, lambda e, shift=shift: e.tensor_scalar(out=tA, in0=ang, scalar1=shift, scalar2=None, op0=ALU.add), reads=[R_s], writes=[R_s])
        P.op(V, lambda e: e.tensor_scalar(out=tB, in0=tA, scalar1=1.0 / TWO_PI, scalar2=None, op0=ALU.mult), reads=[R_s], writes=[R_s])
        P.op(V, lambda e: e.tensor_copy(out=tI, in_=tB), reads=[R_s], writes=[R_s])
        P.op(V, lambda e: e.tensor_copy(out=tB, in_=tI), reads=[R_s], writes=[R_s])
        P.op(V, lambda e: e.scalar_tensor_tensor(out=tA, in0=tB, scalar=-TWO_PI, in1=tA, op0=ALU.mult, op1=ALU.add), reads=[R_s], writes=[R_s])
        P.op(V, lambda e: e.tensor_scalar(out=tB, in0=tA, scalar1=math.pi, scalar2=TWO_PI, op0=ALU.is_gt, op1=ALU.mult), reads=[R_s], writes=[R_s])
        P.op(V, lambda e: e.tensor_tensor(out=tA, in0=tA, in1=tB, op=ALU.subtract), reads=[R_s], writes=[R_s])
        P.op(V, lambda e: e.tensor_scalar(out=tB, in0=tA, scalar1=-math.pi, scalar2=TWO_PI, op0=ALU.is_lt, op1=ALU.mult), reads=[R_s], writes=[R_s])
        P.op(V, lambda e: e.tensor_tensor(out=tA, in0=tA, in1=tB, op=ALU.add), reads=[R_s], writes=[R_s])
        P.op(S, lambda e, dst=dst: e.activation(out=dst[:], in_=tA, func=AF.Sin), reads=[R_s, R_const], writes=[R_const])

    if early('setup'):
        return nc
    P.barrier()
    main.reset()
    rowv = P.sbuf("rowv", [128, 8, 4], F32)
    R_rv = Res()
    vraw = main.alloc([3, 128], F32)
    for wi, vd in enumerate((anw_d, mnw_d, gs_d)):
        P.dma("sync", lambda e, wi=wi, vd=vd: e.dma_start(out=vraw[0:8, wi, :], in_=vd.rearrange("(k p) -> k p", p=128)), reads=[R_rv], writes=[R_rv])
    for wi in range(3):
        P.op(T, lambda e, wi=wi: e.matmul(pb[0][:, wi * 8:(wi + 1) * 8], lhsT=vraw[0:8, wi, :], rhs=ident_f[0:8, 0:8], start=True, stop=True),
             reads=[R_rv, R_const], writes=[rpb[0]])
    P.op(V, lambda e: e.tensor_copy(out=rowv[:, :, 0:3], in_=pb[0][:, 0:24].rearrange("p (w k) -> p k w", w=3)), reads=[rpb[0]], writes=[R_rv])
    for k in range(8):
        src = dnw_d if k < 4 else dfw_d
        P.dma("sync", lambda e, k=k, src=src: e.dma_start(out=rowv[:, k, 3:4], in_=src.rearrange("(p o) -> p o", o=1)), reads=[R_rv], writes=[R_rv])
    P.op(V, lambda e: e.tensor_scalar(out=rowv[:, 4:8, 3:4], in0=rowv[:, 4:8, 3:4], scalar1=1.0 - LAMBDA_INIT, scalar2=None, op0=ALU.mult),
         reads=[R_rv], writes=[R_rv])
    P.op(V, lambda e: e.tensor_tensor(out=rowv[:, :, 2:3], in0=rowv[:, :, 2:3], in1=rowv[:, :, 3:4], op=ALU.mult), reads=[R_rv], writes=[R_rv])
    stg_f = [main.alloc([4096], F32) for _ in range(2)]
    stg_b = [main.alloc([4096], BF16) for _ in range(2)]
    R_sf = [Res(), Res()]
    R_sb = [Res(), Res()]
    R_win = Res()
    R_wrest = Res()
    jobs = []
    for k in range(8):
        jobs.append((win_d[k * 128:(k + 1) * 128, :], DIN, rowv[:, k, 0:1], win_s[:, k, :]))
    for j, (src, n, sc, dst) in enumerate(jobs):
        b = j % 2
        P.dma("sync" if j % 2 == 0 else "scalar", lambda e, b=b, src=src, n=n: e.dma_start(out=stg_f[b][:, 0:n], in_=src), writes=[R_sf[b]])
        if j % 2 == 0:
            P.op(S, lambda e, b=b, n=n, sc=sc: e.activation(out=stg_b[b][:, 0:n], in_=stg_f[b][:, 0:n], func=AF.Copy, scale=sc),
                 reads=[R_sf[b], R_rv], writes=[R_sb[b]])
        else:
            P.op(V, lambda e, b=b, n=n, sc=sc: e.tensor_scalar(out=stg_b[b][:, 0:n], in0=stg_f[b][:, 0:n], scalar1=sc, scalar2=None, op0=ALU.mult),
                 reads=[R_sf[b], R_rv], writes=[R_sb[b]])
        P.dma("gpsimd", lambda e, b=b, n=n, dst=dst: e.dma_start(out=dst, in_=stg_b[b][:, 0:n]), reads=[R_sb[b]], writes=[R_win])
    P.barrier()

    RST_W = 512
    rst_off = main.n - (2 * RST_W + 2 * RST_W // 2)
    rsf = [main.t[:, rst_off + i * RST_W:rst_off + (i + 1) * RST_W] for i in range(2)]
    rsb = [main.t[:, rst_off + 2 * RST_W + i * (RST_W // 2):rst_off + 2 * RST_W + (i + 1) * (RST_W // 2)].bitcast(BF16) for i in range(2)]
    R_rsf = [Res(), Res()]
    R_rsb = [Res(), Res()]
    rest_jobs = []
    for k in range(8):
        for c in range(D // RST_W):
            rest_jobs.append((wout_d[k * 128:(k + 1) * 128, c * RST_W:(c + 1) * RST_W], RST_W, rowv[:, k, 2:3], wout_s[:, k, c * RST_W:(c + 1) * RST_W]))
    for k in range(8):
        for c in range(DFF // RST_W):
            rest_jobs.append((wup_d[k * 128:(k + 1) * 128, c * RST_W:(c + 1) * RST_W], RST_W, rowv[:, k, 1:2], wup_s[:, k, c * RST_W:(c + 1) * RST_W]))
    for k in range(32):
        for c in range(D // RST_W):
            rest_jobs.append((wdn_d[k * 128:(k + 1) * 128, c * RST_W:(c + 1) * RST_W], RST_W, None, wdn_s[:, k, c * RST_W:(c + 1) * RST_W]))

    def prep_rest():
        for j, (src, n, sc, dst) in enumerate(rest_jobs):
            bb_ = j % 2
            P.dma("sync", lambda e, bb_=bb_, src=src, n=n: e.dma_start(out=rsf[bb_][:, 0:n], in_=src), writes=[R_rsf[bb_]])
            yield
            if sc is None:
                eng = G if j % 2 == 0 else V
                P.op(eng, lambda e, bb_=bb_, n=n: e.tensor_copy(out=rsb[bb_][:, 0:n], in_=rsf[bb_][:, 0:n]), reads=[R_rsf[bb_]], writes=[R_rsb[bb_]])
            else:
                P.op(V, lambda e, bb_=bb_, n=n, sc=sc: e.tensor_scalar(out=rsb[bb_][:, 0:n], in0=rsf[bb_][:, 0:n], scalar1=sc, scalar2=None, op0=ALU.mult),
                     reads=[R_rsf[bb_], R_rv], writes=[R_rsb[bb_]])
            yield
            P.dma("sync", lambda e, bb_=bb_, n=n, dst=dst: e.dma_start(out=dst, in_=rsb[bb_][:, 0:n]), reads=[R_rsb[bb_]], writes=[R_wrest])
            yield

    if early('wprep'):
        return nc
    def rstd_from_ss(ss, tmp, out, scale):
        P.op(S, lambda e: e.activation(out=tmp, in_=ss, func=AF.Ln, scale=scale, bias=EPSC), reads=[R_st, R_const], writes=[R_st])
        P.op(S, lambda e: e.activation(out=out, in_=tmp, func=AF.Exp, scale=-0.5), reads=[R_st], writes=[R_st])

    bg = [None]

    def bg_step():
        if bg[0] is not None:
            try:
                next(bg[0])
            except StopIteration:
                bg[0] = None

    bg2 = [None]

    def bg2_step():
        if bg2[0] is not None:
            try:
                next(bg2[0])
            except StopIteration:
                bg2[0] = None

    def run_rr(gens, use_bg=False):
        gens = [g for g in gens if g is not None]
        while gens:
            for g in list(gens):
                try:
                    next(g)
                except StopIteration:
                    gens.remove(g)
            if use_bg:
                bg_step()


    for b in range(NB):
        if b == 0:
            bg[0] = prep_rest()
        main.reset()
        hnT = main.alloc([8, L], BF16)
        mixT = main.alloc([8, L], BF16)
        R_hnT = [Res() for _ in range(NT)]
        R_mixT = [[Res() for _ in range(NT)] for _ in range(8)]
        base_off0 = main.off
        siluz = main.alloc([NT, 512], BF16)
        R_sz = [Res() for _ in range(NT)]
        ba = main.alloc([NT, 8], F32)
        R_ba = Res()
        sm = main.alloc([12, NT, 4], F32)
        R_sm = Res()
        base_off = main.off
        dfq0 = main.alloc([4, L], BF16)
        dfq1 = main.alloc([4, L], BF16)
        dfkT = main.alloc([4, L], BF16)
        dfv = main.alloc([NT, 4, 130], BF16)
        R_dfq = [Res() for _ in range(NT)]
        R_dfk = [Res() for _ in range(NT)]
        R_dfv = [Res() for _ in range(NT)]
        wblk = [main.alloc([8, 512], BF16) for _ in range(3)]
        R_wblk = [Res(), Res(), Res()]
        qtok = [main.alloc([512], BF16) for _ in range(2)]
        R_qtok = [Res(), Res()]
        rt = [main.alloc([8, 8], F32) for _ in range(4)]
        R_rt = Res()
        off_A = main.off
        xt = [main.alloc([D], F32) for _ in range(2)]
        R_xt = [Res(), Res()]
        hnb = [main.alloc([D], BF16) for _ in range(2)]
        R_hnb = [Res(), Res()]
        st = main.alloc([NT, 4], F32)
        R_st = Res()
        P.op(G, lambda e: e.memset(dfv[:, :, :, 128:130], 1.0), writes=R_dfv)
        P.op(G, lambda e: e.memset(dfq0[64:128], 0.0), writes=R_dfq)
        P.op(G, lambda e: e.memset(dfq1[0:64], 0.0), writes=R_dfq)
        COL_Q, COL_K, COL_V = 2056, 2568, 3080
        for bi, col0 in enumerate((COL_Q, COL_K, COL_V)):
            P.dma("sync", lambda e, bi=bi, col0=col0: e.dma_start(out=wblk[bi], in_=win_s[:, :, col0:col0 + 512]), reads=[R_win], writes=[R_wblk[bi]])

        def genA(n):
            i2 = n % 2
            P.dma("sync", lambda e, b=b: e.dma_start(out=xt[i2], in_=x_d[b, n * 128:(n + 1) * 128, :]), writes=[R_xt[i2]])
            P.op(S, lambda e: e.activation(out=hnb[i2], in_=xt[i2], func=AF.Square, accum_out=st[:, n, 0:1]), reads=[R_xt[i2]], writes=[R_hnb[i2], R_st])
            yield
            P.op(S, lambda e: e.activation(out=st[:, n, 1:2], in_=st[:, n, 0:1], func=AF.Ln, scale=1.0 / D, bias=EPSC), reads=[R_st, R_const], writes=[R_st])
            yield
            P.op(S, lambda e: e.activation(out=st[:, n, 2:3], in_=st[:, n, 1:2], func=AF.Exp, scale=-0.5), reads=[R_st], writes=[R_st])
            yield
            P.op(V, lambda e: e.tensor_scalar(out=hnb[i2], in0=xt[i2], scalar1=st[:, n, 2:3], scalar2=None, op0=ALU.mult),
                 reads=[R_xt[i2], R_st], writes=[R_hnb[i2]])
            yield
            for k in range(8):
                P.op(T, lambda e, k=k: e.transpose(out=ptr[:, k * 128:(k + 1) * 128], in_=hnb[i2][:, k * 128:(k + 1) * 128], identity=ident_bf[:]),
                     reads=[R_hnb[i2], R_const], writes=[rptr])
            P.op(S, lambda e: e.activation(out=hnT[:, :, n * 128:(n + 1) * 128], in_=ptr[:].rearrange("p (k t) -> p k t", k=8), func=AF.Copy),
                 reads=[rptr], writes=[R_hnT[n]])
            yield

        pcount = [0]

        def genDF(n):
            for bi, kind in enumerate(("q", "k", "v")):
                pbi = pcount[0] % 2
                pcount[0] += 1
                ps = pb[pbi]
                rps = rpb[pbi]
                for k in range(8):
                    P.op(T, lambda e, k=k, ps=ps, bi=bi: e.matmul(ps[:, :], lhsT=hnT[:, k, n * 128:(n + 1) * 128], rhs=wblk[bi][:, k, :],
                                                                  start=(k == 0), stop=(k == 7)),
                         reads=[R_hnT[n], R_wblk[bi]], writes=[rps])
                yield
                if kind == "v":
                    P.op(S, lambda e, ps=ps: e.activation(out=dfv[:, n, :, 0:128], in_=ps[:, :].rearrange("p (h d) -> p h d", h=4), func=AF.Copy),
                         reads=[rps], writes=[R_dfv[n]])
                    yield
                    continue
                qi = pcount[0] % 2
                qt = qtok[qi]
                Rq = R_qtok[qi]
                P.op(S, lambda e, ps=ps, qt=qt: e.activation(out=qt, in_=ps[:, :], func=AF.Copy), reads=[rps], writes=[Rq])
                yield
                ps3 = ps[:, :].rearrange("p (g d) -> p g d", g=8)
                qt3 = qt.rearrange("p (g d) -> p g d", g=8)
                cb = cosT[:, n, :].unsqueeze(1).to_broadcast([128, 8, 8])
                sb = sinT[:, n, :].unsqueeze(1).to_broadcast([128, 8, 8])
                x1 = ps3[:, :, 0:8]
                x2 = ps3[:, :, 8:16]
                P.op(V, lambda e, x1=x1, cb=cb: e.tensor_tensor(out=rt[0], in0=x1, in1=cb, op=ALU.mult), reads=[rps, R_const], writes=[R_rt])
                P.op(V, lambda e, x2=x2, sb=sb: e.tensor_tensor(out=rt[1], in0=x2, in1=sb, op=ALU.mult), reads=[rps, R_const], writes=[R_rt])
                yield
                P.op(V, lambda e, x2=x2, cb=cb: e.tensor_tensor(out=rt[2], in0=x2, in1=cb, op=ALU.mult), reads=[rps, R_const], writes=[R_rt])
                P.op(V, lambda e, x1=x1, sb=sb: e.tensor_tensor(out=rt[3], in0=x1, in1=sb, op=ALU.mult), reads=[rps, R_const], writes=[R_rt])
                yield
                P.op(V, lambda e, qt3=qt3: e.tensor_tensor(out=qt3[:, :, 0:8], in0=rt[0], in1=rt[1], op=ALU.subtract), reads=[R_rt], writes=[Rq])
                P.op(V, lambda e, qt3=qt3: e.tensor_tensor(out=qt3[:, :, 8:16], in0=rt[2], in1=rt[3], op=ALU.add), reads=[R_rt], writes=[Rq])
                yield
                for h in range(4):
                    P.op(T, lambda e, h=h, qt=qt: e.transpose(out=ptr[:, h * 128:(h + 1) * 128], in_=qt[:, h * 128:(h + 1) * 128], identity=ident_bf[:]),
                         reads=[Rq, R_const], writes=[rptr])
                if kind == "q":
                    P.op(V, lambda e: e.tensor_copy(out=dfq0[0:64, :, n * 128:(n + 1) * 128], in_=ptr[0:64, 0:512].rearrange("p (h t) -> p h t", h=4)),
                         reads=[rptr], writes=[R_dfq[n]])
                    P.op(V, lambda e: e.tensor_copy(out=dfq1[64:128, :, n * 128:(n + 1) * 128], in_=ptr[64:128, 0:512].rearrange("p (h t) -> p h t", h=4)),
                         reads=[rptr], writes=[R_dfq[n]])
                else:
                    P.op(V, lambda e: e.tensor_copy(out=dfkT[:, :, n * 128:(n + 1) * 128], in_=ptr[:, 0:512].rearrange("p (h t) -> p h t", h=4)),
                         reads=[rptr], writes=[R_dfk[n]])
                yield

        run_rr([genA(0)], use_bg=True)
        for n in range(NT):
            run_rr([genA(n + 1) if n + 1 < NT else None, genDF(n)], use_bg=True)
        assert main.off <= rst_off, (main.off, rst_off)
        main.reset(off_A)
        P.barrier()

        if early('DFproj'):
            return nc
        PT = [main.alloc([2, 256], BF16) for _ in range(2)]
        R_PT = [Res(), Res()]
        dtmp = main.alloc([128], F32)
        R_dt = Res()
        mxb = main.alloc([128], BF16)
        R_mxb = Res()
        fst = main.alloc([16], F32)
        R_fst = Res()
        stb = [pb[0], pb[1]]
        rstb = [rpb[0], rpb[1]]
        accp = [[pb[2], pb[3]], [pb[4], pb[5]]]
        raccp = [[rpb[2], rpb[3]], [rpb[4], rpb[5]]]
        dfq = [dfq0, dfq1]
        NQG = NT // 2
        iters = [(h, qg, kb) for h in range(4) for qg in range(NQG) for kb in range(2 * qg + 2)]

        def df_ST(j):
            h, qg, kb = iters[j]
            bufi = j % 2
            i0 = max(0, kb - 2 * qg)
            c0 = i0 * 128
            for m in range(2):
                P.op(T, lambda e, m=m: e.matmul(
                    stb[bufi][:, m * 256 + c0:(m + 1) * 256], lhsT=dfkT[:, h, kb * 128:(kb + 1) * 128],
                    rhs=dfq[m][:, h, qg * 256 + c0:(qg + 1) * 256], start=True, stop=True),
                    reads=[R_dfk[kb]] + R_dfq[qg * 2 + i0:qg * 2 + 2], writes=[rstb[bufi]])

        def df_EXP(j):
            h, qg, kb = iters[j]
            bufi = j % 2
            c0 = max(0, kb - 2 * qg) * 128
            P.op(S, lambda e: e.activation(out=PT[bufi][:, :, c0:256], in_=stb[bufi][:, :].rearrange("p (m q) -> p m q", m=2)[:, :, c0:256],
                                           func=AF.Exp, scale=0.125), reads=[rstb[bufi]], writes=[R_PT[bufi]])
            if kb >= 2 * qg:
                for m in range(2):
                    P.op(G, lambda e, m=m: e.tensor_tensor(out=PT[bufi][:, m, c0:c0 + 128], in0=PT[bufi][:, m, c0:c0 + 128], in1=UI_bf[:], op=ALU.mult),
                         reads=[R_PT[bufi], R_const], writes=[R_PT[bufi]])

        NRING = 4
        stg = [main.alloc([2, 130], F32) for _ in range(NRING)]
        dring = [main.alloc([128], F32) for _ in range(NRING)]
        mring = [main.alloc([128], BF16) for _ in range(NRING)]
        fring = [main.alloc([8], F32) for _ in range(NRING)]
        R_ring = [Res() for _ in range(NRING)]
        fin_cnt = [0]
        pending = []

        def df_FIN(h, n, i, j):
            r = fin_cnt[0] % NRING
            fin_cnt[0] += 1
            sg, dd, mm_, ff, Rr = stg[r], dring[r], mring[r], fring[r], R_ring[r]
            for m in range(2):
                P.op(V, lambda e, m=m: e.tensor_copy(out=sg[:, m, :], in_=accp[m][i][:, 0:130]), reads=[raccp[m][i]], writes=[Rr])

            def part2():
                P.op(V, lambda e: e.reciprocal(out=ff[:, 0:2], in_=sg[:, :, 128:129].rearrange("p m o -> p (m o)")), reads=[Rr], writes=[Rr])
                P.op(V, lambda e: e.tensor_tensor(out=ff[:, 2:3], in0=ff[:, 1:2], in1=LAMN, op=ALU.mult), reads=[Rr, R_const], writes=[Rr])
                P.op(V, lambda e: e.tensor_scalar(out=dd, in0=sg[:, 0, 0:128], scalar1=ff[:, 0:1], scalar2=None, op0=ALU.mult), reads=[Rr], writes=[Rr])
                P.op(V, lambda e: e.scalar_tensor_tensor(out=dd, in0=sg[:, 1, 0:128], scalar=ff[:, 2:3], in1=dd, op0=ALU.mult, op1=ALU.add), reads=[Rr], writes=[Rr])

            def part3():
                P.op(S, lambda e: e.activation(out=mm_, in_=dd, func=AF.Square, accum_out=ff[:, 3:4]), reads=[Rr], writes=[Rr])
                P.op(S, lambda e: e.activation(out=ff[:, 4:5], in_=ff[:, 3:4], func=AF.Ln, scale=1.0 / 128, bias=EPSC), reads=[Rr, R_const], writes=[Rr])
                P.op(S, lambda e: e.activation(out=ff[:, 5:6], in_=ff[:, 4:5], func=AF.Exp, scale=-0.5), reads=[Rr], writes=[Rr])

            def part4():
                P.op(V, lambda e: e.tensor_scalar(out=mm_, in0=dd, scalar1=ff[:, 5:6], scalar2=None, op0=ALU.mult), reads=[Rr], writes=[Rr])
                P.op(T, lambda e: e.transpose(out=ptr[:, 0:128], in_=mm_, identity=ident_bf[:]), reads=[Rr, R_const], writes=[rptr])
                P.op(V, lambda e: e.tensor_copy(out=mixT[:, 4 + h, n * 128:(n + 1) * 128], in_=ptr[:, 0:128]), reads=[rptr], writes=[R_mixT[4 + h][n]])
            pending.append((j + 1, part2))
            pending.append((j + 2, part3))
            pending.append((j + 3, part4))

        def df_flush(j):
            keep = []
            for (due, fn) in pending:
                if due <= j:
                    fn()
                else:
                    keep.append((due, fn))
            pending[:] = keep

        def df_PV(j):
            h, qg, kb = iters[j]
            bufi = j % 2
            i0 = max(0, kb - 2 * qg)
            for m in range(2):
                for i in range(i0, 2):
                    P.op(T, lambda e, m=m, i=i: e.matmul(
                        accp[m][i][:, 0:129], lhsT=PT[bufi][:, m, i * 128:(i + 1) * 128], rhs=dfv[:, kb, h, 0:129],
                        start=(kb == 0), stop=(kb == 2 * qg + i)),
                        reads=[R_PT[bufi], R_dfv[kb]], writes=[raccp[m][i]])
            if kb >= 2 * qg:
                df_FIN(h, kb, kb - 2 * qg, j)

        wz = wblk[0]
        R_wz = R_wblk[0]
        wba = main.alloc([8, 8], BF16)
        R_wba = Res()
        etmp = main.alloc([512], F32)
        R_et = Res()

        def dn_prolog():
            P.dma("sync", lambda e: e.dma_start(out=wz, in_=win_s[:, :, 1536:2048]), reads=[R_win], writes=[R_wz])
            P.dma("sync", lambda e: e.dma_start(out=wba, in_=win_s[:, :, 2048:2056]), reads=[R_win], writes=[R_wba])
            yield
            for n in range(NT):
                for k in range(8):
                    P.op(T, lambda e, k=k, n=n: e.matmul(pb[6][:, :], lhsT=hnT[:, k, n * 128:(n + 1) * 128], rhs=wz[:, k, :], start=(k == 0), stop=(k == 7)),
                         reads=[R_hnT[n], R_wz], writes=[rpb[6]])
                yield
                P.op(S, lambda e: e.activation(out=etmp, in_=pb[6][:, :], func=AF.Exp, scale=-1.0), reads=[rpb[6]], writes=[R_et])
                yield
                P.op(V, lambda e: e.tensor_scalar(out=etmp, in0=etmp, scalar1=1.0, scalar2=None, op0=ALU.add), reads=[R_et], writes=[R_et])
                P.op(V, lambda e: e.reciprocal(out=etmp, in_=etmp), reads=[R_et], writes=[R_et])
                yield
                P.op(V, lambda e, n=n: e.tensor_tensor(out=siluz[:, n, :], in0=pb[6][:, :], in1=etmp, op=ALU.mult), reads=[rpb[6], R_et], writes=[R_sz[n]])
                yield
                for k in range(8):
                    P.op(T, lambda e, k=k, n=n: e.matmul(pb[6][:, 0:8], lhsT=hnT[:, k, n * 128:(n + 1) * 128], rhs=wba[:, k, :], start=(k == 0), stop=(k == 7)),
                         reads=[R_hnT[n], R_wba], writes=[rpb[6]])
                P.op(V, lambda e, n=n: e.tensor_copy(out=ba[:, n, :], in_=pb[6][:, 0:8]), reads=[rpb[6]], writes=[R_ba])
                yield
            bb = ba[:, :, 0:4]
            aa = ba[:, :, 4:8]
            dtb_b = DTB.unsqueeze(1).to_broadcast([128, NT, 4])
            nega_b = NEGA.unsqueeze(1).to_broadcast([128, NT, 4])

            def smop(eng, fn, extra=()):
                P.op(eng, fn, reads=[R_sm, R_ba, R_const] + list(extra), writes=[R_sm])
            smop(S, lambda e: e.activation(out=sm[:, 0], in_=bb, func=AF.Abs))
            yield
            smop(S, lambda e: e.activation(out=sm[:, 1], in_=sm[:, 0], func=AF.Exp, scale=-1.0))
            yield
            smop(S, lambda e: e.activation(out=sm[:, 2], in_=sm[:, 1], func=AF.Ln, bias=ONEC))
            yield
            smop(V, lambda e: e.scalar_tensor_tensor(out=sm[:, 3], in0=bb, scalar=0.0, in1=sm[:, 2], op0=ALU.min, op1=ALU.subtract))
            smop(V, lambda e: e.tensor_tensor(out=sm[:, 4], in0=aa, in1=dtb_b, op=ALU.add))
            yield
            smop(S, lambda e: e.activation(out=sm[:, 0], in_=sm[:, 4], func=AF.Abs))
            yield
            smop(S, lambda e: e.activation(out=sm[:, 1], in_=sm[:, 0], func=AF.Exp, scale=-1.0))
            yield
            smop(S, lambda e: e.activation(out=sm[:, 2], in_=sm[:, 1], func=AF.Ln, bias=ONEC))
            yield
            smop(V, lambda e: e.scalar_tensor_tensor(out=sm[:, 5], in0=sm[:, 4], scalar=0.0, in1=sm[:, 2], op0=ALU.max, op1=ALU.add))
            smop(V, lambda e: e.tensor_tensor(out=sm[:, 6], in0=sm[:, 5], in1=nega_b, op=ALU.mult))
            yield
            gflat = sm[:, 6].rearrange("p n h -> p (n h)")
            P.op(T, lambda e: e.matmul(pb[6][:, 0:NT * 4], lhsT=UI[:], rhs=gflat, start=True, stop=True), reads=[R_sm, R_const], writes=[rpb[6]])
            smop(V, lambda e: e.tensor_copy(out=sm[:, 7].rearrange("p n h -> p (n h)"), in_=pb[6][:, 0:NT * 4]), extra=[rpb[6]])
            yield
            P.op(T, lambda e: e.matmul(pb[6][:, 0:NT * 4], lhsT=ones_f[:], rhs=gflat, start=True, stop=True), reads=[R_sm, R_const], writes=[rpb[6]])
            smop(V, lambda e: e.tensor_copy(out=sm[:, 8].rearrange("p n h -> p (n h)"), in_=pb[6][:, 0:NT * 4]), extra=[rpb[6]])
            yield
            smop(V, lambda e: e.tensor_tensor(out=sm[:, 9], in0=sm[:, 3], in1=sm[:, 7], op=ALU.add))
            smop(V, lambda e: e.tensor_tensor(out=sm[:, 10], in0=sm[:, 8], in1=sm[:, 7], op=ALU.subtract))
            yield
            smop(S, lambda e: e.activation(out=sm[:, 11], in_=sm[:, 9], func=AF.Exp))
            yield
            smop(S, lambda e: e.activation(out=sm[:, 10], in_=sm[:, 10], func=AF.Exp))
            yield
            smop(S, lambda e: e.activation(out=sm[:, 8], in_=sm[:, 8], func=AF.Exp))
            yield
            smop(S, lambda e: e.activation(out=sm[:, 3], in_=sm[:, 3], func=AF.Exp))
            yield

        bg2[0] = dn_prolog()
        df_ST(0)
        for j in range(len(iters)):
            if j + 1 < len(iters):
                df_ST(j + 1)
            df_EXP(j)
            df_PV(j)
            df_flush(j)
            bg_step()
            bg2_step()
        df_flush(10 ** 9)
        while bg[0] is not None:
            bg_step()

        while bg2[0] is not None:
            bg2_step()
        P.barrier()
        main.reset(base_off)
        GC, HC, C1, C2, DEC, BETA = sm[:, 7], sm[:, 9], sm[:, 11], sm[:, 10], sm[:, 8], sm[:, 3]

        NBt = NT // 4
        HB = []
        for _i in range(2):
            HB.append(dict(qT=main.alloc([L], BF16), kT=main.alloc([L], BF16), vT=main.alloc([L], BF16),
                           kbg=main.alloc([NT, 128], BF16), ksc=main.alloc([NT, 128], BF16), vb=main.alloc([NT, 128], BF16),
                           wqkv=main.alloc([3, 8, 128], BF16),
                           R_q=Res(), R_k=Res(), R_v=Res(), R_tok=Res(), R_w=Res()))
        xc = main.alloc([L + 4], F32)
        R_xc = Res()
        yv = main.alloc([L], F32)
        R_yv = Res()
        sq = main.alloc([L], BF16)
        R_sq = Res()
        rn = main.alloc([512], F32)
        R_rn = Res()
        dA = main.alloc([4, 128], F32)
        dB = main.alloc([4, 128], F32)
        dC = main.alloc([4, 128], F32)
        dD = main.alloc([4, 128], F32)
        R_dA, R_dB, R_dC, R_dD = Res(), Res(), Res(), Res()
        Yb = [main.alloc([4, 128], F32) for _ in range(2)]
        Zb = [main.alloc([4, 128], F32) for _ in range(2)]
        R_Y = [Res(), Res()]
        R_Z = [Res(), Res()]
        Nm = main.alloc([4, 128], F32)
        R_N = Res()
        PB = []
        for _i in range(2):
            PB.append(dict(TTb=main.alloc([4, 128], BF16), nwT=main.alloc([4, 128], BF16), qsT=main.alloc([4, 128], BF16),
                           QKm=main.alloc([4, 128], BF16), R_TT=Res(), R_nw=Res(), R_qs=Res(), R_QK=Res()))
        vnb = main.alloc([128], BF16)
        R_vn = Res()
        Sf = main.alloc([128], F32)
        Sb = main.alloc([128], BF16)
        R_Sf, R_Sb = Res(), Res()
        mxd = main.alloc([128], BF16)
        R_mxd = Res()
        fs = main.alloc([8], F32)
        R_fs = Res()
        jk = main.alloc([128], BF16)
        R_jk = Res()
        P.op(V, lambda e: e.memset(xc[:, 0:4], 0.0), writes=[R_xc])
        idb = ident_f[:].unsqueeze(1).to_broadcast([128, 4, 128])
        negSLb = negSL[:].unsqueeze(1).to_broadcast([128, 4, 128])
        UIb = UI[:].unsqueeze(1).to_broadcast([128, 4, 128])
        BK_PREP = 0
        BK_G = 0
        BK_KQ = 0
        HBK = [(1, 2, 3), (4, 5, 6)]
        ptrf = ptr[:].bitcast(F32)

        def dn_prep(h):
            hb = HB[h % 2]
            for c in range(3):
                P.dma("sync", lambda e, c=c: e.dma_start(out=hb["wqkv"][:, c], in_=win_s[:, :, c * 512 + h * 128:c * 512 + (h + 1) * 128]),
                      reads=[R_win], writes=[hb["R_w"]])
            yield
            for c in range(3):
                cc = c * 4 + h
                for grp in range(NG):
                    for k in range(8):
                        P.op(T, lambda e, c=c, k=k, grp=grp: e.matmul(pb[BK_PREP][:, :], lhsT=hb["wqkv"][:, c, k, :], rhs=hnT[:, k, grp * 512:(grp + 1) * 512],
                                                                      start=(k == 0), stop=(k == 7)),
                             reads=[hb["R_w"]] + R_hnT[grp * 4:grp * 4 + 4], writes=[rpb[BK_PREP]])
                    P.op(S, lambda e, grp=grp: e.activation(out=xc[:, 3 + grp * 512:3 + (grp + 1) * 512], in_=pb[BK_PREP][:, :], func=AF.Copy),
                         reads=[rpb[BK_PREP]], writes=[R_xc])
                    yield
                P.op(V, lambda e, cc=cc: e.tensor_scalar(out=yv, in0=xc[:, 3:3 + L], scalar1=convw[:, cc, 3:4], scalar2=None, op0=ALU.mult),
                     reads=[R_xc, R_const], writes=[R_yv])
                yield
                for j in (2, 1, 0):
                    P.op(V, lambda e, cc=cc, j=j: e.scalar_tensor_tensor(out=yv, in0=xc[:, j:j + L], scalar=convw[:, cc, j:j + 1], in1=yv,
                                                                        op0=ALU.mult, op1=ALU.add), reads=[R_xc, R_const, R_yv], writes=[R_yv])
                    yield
                if c == 2:
                    P.op(S, lambda e: e.activation(out=hb["vT"], in_=yv, func=AF.Silu), reads=[R_yv], writes=[hb["R_v"]])
                    yield
                    continue
                P.op(S, lambda e: e.activation(out=yv, in_=yv, func=AF.Silu), reads=[R_yv], writes=[R_yv])
                yield
                P.op(G, lambda e: e.tensor_tensor(out=sq, in0=yv, in1=yv, op=ALU.mult), reads=[R_yv], writes=[R_sq])
                yield
                dstT, Rd, qscale = (hb["qT"], hb["R_q"], 128 ** -0.5) if c == 0 else (hb["kT"], hb["R_k"], 1.0)
                for grp in range(NG):
                    gs_ = slice(grp * 512, (grp + 1) * 512)
                    P.op(T, lambda e, gs_=gs_: e.matmul(pb[BK_PREP][:, :], lhsT=ones_bf[:], rhs=sq[:, gs_], start=True, stop=True),
                         reads=[R_sq, R_const], writes=[rpb[BK_PREP]])
                    P.op(S, lambda e: e.activation(out=rn, in_=pb[BK_PREP][:, :], func=AF.Ln, bias=EPSL2), reads=[rpb[BK_PREP], R_const], writes=[R_rn])
                    yield
                    P.op(S, lambda e: e.activation(out=rn, in_=rn, func=AF.Exp, scale=-0.5), reads=[R_rn], writes=[R_rn])
                    yield
                    P.op(V, lambda e, gs_=gs_, dstT=dstT, qscale=qscale: e.scalar_tensor_tensor(out=dstT[:, gs_], in0=yv[:, gs_], scalar=qscale, in1=rn,
                                                                                               op0=ALU.mult, op1=ALU.mult),
                         reads=[R_yv, R_rn], writes=[Rd])
                    yield
            for n in range(NT):
                ts_ = slice(n * 128, (n + 1) * 128)
                P.op(T, lambda e, ts_=ts_: e.transpose(out=ptr[:, 0:128], in_=hb["kT"][:, ts_], identity=ident_bf[:]), reads=[hb["R_k"], R_const], writes=[rptr])
                P.op(T, lambda e, ts_=ts_: e.transpose(out=ptr[:, 128:256], in_=hb["vT"][:, ts_], identity=ident_bf[:]), reads=[hb["R_v"], R_const], writes=[rptr])
                P.op(S, lambda e, n=n: e.activation(out=hb["kbg"][:, n, :], in_=ptr[:, 0:128], func=AF.Copy, scale=C1[:, n, h:h + 1]),
                     reads=[rptr, R_sm], writes=[hb["R_tok"]])
                P.op(V, lambda e, n=n: e.tensor_scalar(out=hb["ksc"][:, n, :], in0=ptr[:, 0:128], scalar1=C2[:, n, h:h + 1], scalar2=None, op0=ALU.mult),
                     reads=[rptr, R_sm], writes=[hb["R_tok"]])
                P.op(S, lambda e, n=n: e.activation(out=hb["vb"][:, n, :], in_=ptr[:, 128:256], func=AF.Copy, scale=BETA[:, n, h:h + 1]),
                     reads=[rptr, R_sm], writes=[hb["R_tok"]])
                yield

        R_dAh = [Res(), Res()]
        R_dBh = [Res(), Res()]
        R_dCh = [Res(), Res()]
        R_dDh = [Res(), Res()]
        R_Yh = [[Res(), Res()], [Res(), Res()]]
        R_Zh = [[Res(), Res()], [Res(), Res()]]
        R_Nh = [Res(), Res()]

        def dn_pre(h, jb, hf):
            hb = HB[h % 2]
            pbf = PB[(h * NBt + jb) % 2]
            BY, BZ, BN = HBK[hf]
            t0_ = hf * 2
            n0 = jb * 4 + t0_
            tv = slice(t0_, t0_ + 2)
            sl = slice(n0 * 128, (n0 + 2) * 128)
            gcb = GC[:, n0:n0 + 2, h:h + 1].to_broadcast([128, 2, 128])
            hcb = HC[:, n0:n0 + 2, h:h + 1].to_broadcast([128, 2, 128])
            idb2 = ident_f[:].unsqueeze(1).to_broadcast([128, 2, 128])
            nsl2 = negSL[:].unsqueeze(1).to_broadcast([128, 2, 128])
            ui2 = UI[:].unsqueeze(1).to_broadcast([128, 2, 128])
            kT, qT = hb["kT"], hb["qT"]
            Rpk = [pbf["R_TT"], pbf["R_nw"], pbf["R_qs"], pbf["R_QK"]]
            cols = slice(t0_ * 128, (t0_ + 2) * 128)

            def bv(bank):
                return pb[bank][:, cols].rearrange("p (i t) -> p i t", i=2)

            def mm2(bank, lhs_fn, rhs_fn, reads):
                for i in range(2):
                    P.op(T, lambda e, i=i: e.matmul(pb[bank][:, (t0_ + i) * 128:(t0_ + i + 1) * 128], lhsT=lhs_fn(i), rhs=rhs_fn(i), start=True, stop=True),
                         reads=reads, writes=[rpb[bank]])
            dAv, dBv, dCv, dDv, Nv = dA[:, tv, :], dB[:, tv, :], dC[:, tv, :], dD[:, tv, :], Nm[:, tv, :]
            Yv = [Yb[0][:, tv, :], Yb[1][:, tv, :]]
            Zv = [Zb[0][:, tv, :], Zb[1][:, tv, :]]
            RA, RB, RC, RD, RN = R_dAh[hf], R_dBh[hf], R_dCh[hf], R_dDh[hf], R_Nh[hf]
            RY, RZ = R_Yh[hf], R_Zh[hf]
            P.op(V, lambda e: e.tensor_tensor(out=dAv, in0=idb2, in1=gcb, op=ALU.mult), reads=[R_const, R_sm], writes=[RA])
            yield
            mm2(BN, lambda i: ones_f[:], lambda i: dAv[:, i, :], [RA, R_const])
            P.op(V, lambda e: e.tensor_tensor(out=dBv, in0=bv(BN), in1=hcb, op=ALU.subtract), reads=[rpb[BN], R_sm], writes=[RB])
            P.op(V, lambda e: e.tensor_tensor(out=dCv, in0=bv(BN), in1=gcb, op=ALU.subtract), reads=[rpb[BN], R_sm], writes=[RC])
            yield
            P.op(S, lambda e: e.activation(out=dBv, in_=dBv, func=AF.Exp, scale=-1.0), reads=[RB], writes=[RB])
            yield
            P.op(S, lambda e: e.activation(out=dCv, in_=dCv, func=AF.Exp), reads=[RC], writes=[RC])
            P.op(S, lambda e: e.activation(out=dDv, in_=bv(BN), func=AF.Exp), reads=[rpb[BN]], writes=[RD])
            yield
            P.op(V, lambda e: e.scalar_tensor_tensor(out=dBv, in0=dBv, scalar=1.0, in1=nsl2, op0=ALU.min, op1=ALU.mult), reads=[RB, R_const], writes=[RB])
            yield
            P.op(V, lambda e: e.scalar_tensor_tensor(out=dCv, in0=dCv, scalar=1.0, in1=ui2, op0=ALU.min, op1=ALU.mult), reads=[RC, R_const], writes=[RC])
            yield
            P.op(G, lambda e: e.tensor_tensor(out=pbf["qsT"][:, tv, :], in0=qT[:, sl].rearrange("p (i t) -> p i t", i=2), in1=dDv, op=ALU.mult),
                 reads=[hb["R_q"], RD], writes=[pbf["R_qs"]])
            yield
            mm2(BN, lambda i: kT[:, (n0 + i) * 128:(n0 + i + 1) * 128], lambda i: kT[:, (n0 + i) * 128:(n0 + i + 1) * 128], [hb["R_k"]])
            P.op(V, lambda e: e.tensor_tensor(out=Yv[0], in0=bv(BN), in1=dBv, op=ALU.mult), reads=[rpb[BN], RB], writes=[RY[0]])
            yield
            mm2(BN, lambda i: kT[:, (n0 + i) * 128:(n0 + i + 1) * 128], lambda i: qT[:, (n0 + i) * 128:(n0 + i + 1) * 128], [hb["R_k"], hb["R_q"]])
            P.op(V, lambda e: e.tensor_tensor(out=pbf["QKm"][:, tv, :], in0=bv(BN), in1=dCv, op=ALU.mult), reads=[rpb[BN], RC], writes=[pbf["R_QK"]])
            yield
            for i in range(2):
                P.op(T, lambda e, i=i: e.transpose(out=pb[BZ][:, (t0_ + i) * 128:(t0_ + i + 1) * 128], in_=Yv[0][:, i, :], identity=ident_f[:]),
                     reads=[RY[0], R_const], writes=[rpb[BZ]])
            yield
            P.op(S, lambda e: e.activation(out=Zv[0], in_=bv(BZ), func=AF.Copy), reads=[rpb[BZ]], writes=[RZ[0]])
            yield
            P.op(V, lambda e: e.tensor_tensor(out=Nv, in0=Zv[0], in1=idb2, op=ALU.add), reads=[RZ[0], R_const], writes=[RN])
            yield
            cur = 0
            for lv in range(1, 7):
                nx = 1 - cur
                mm2(BY, lambda i, cur=cur: Zv[cur][:, i, :], lambda i, cur=cur: Yv[cur][:, i, :], [RZ[cur], RY[cur]])
                yield
                if lv <= 5:
                    mm2(BZ, lambda i, cur=cur: Yv[cur][:, i, :], lambda i, cur=cur: Zv[cur][:, i, :], [RZ[cur], RY[cur]])
                    yield
                P.op(S, lambda e, nx=nx: e.activation(out=Yv[nx], in_=bv(BY), func=AF.Copy), reads=[rpb[BY]], writes=[RY[nx]])
                yield
                if lv <= 5:
                    P.op(V, lambda e, nx=nx: e.tensor_copy(out=Zv[nx], in_=bv(BZ)), reads=[rpb[BZ]], writes=[RZ[nx]])
                    yield
                mm2(BN, lambda i, nx=nx: Yv[nx][:, i, :], lambda i: Nv[:, i, :], [RY[nx], RN])
                yield
                P.op(V, lambda e: e.tensor_tensor(out=Nv, in0=Nv, in1=bv(BN), op=ALU.add), reads=[RN, rpb[BN]], writes=[RN])
                yield
                cur = nx
            P.op(S, lambda e: e.activation(out=pbf["TTb"][:, tv, :], in_=Nv, func=AF.Copy), reads=[RN], writes=[pbf["R_TT"]])
            yield
            mm2(BN, lambda i: hb["kbg"][:, n0 + i, :], lambda i: pbf["TTb"][:, t0_ + i, :], [hb["R_tok"], pbf["R_TT"]])
            yield
            P.op(S, lambda e: e.activation(out=pbf["nwT"][:, tv, :], in_=bv(BN), func=AF.Copy, scale=-1.0), reads=[rpb[BN]], writes=[pbf["R_nw"]])
            yield

        def dn_scan(h, jb):
            hb = HB[h % 2]
            pbf = PB[(h * NBt + jb) % 2]
            n0 = jb * 4
            VN = ptrf[:, 256:384]
            OO = ptrf[:, 384:512]
            DS = ptrf[:, 256:384]
            rS = rptr
            if jb == 0:
                P.op(V, lambda e: e.memset(Sf, 0.0), writes=[R_Sf])
                P.op(V, lambda e: e.memset(Sb, 0.0), writes=[R_Sb])
                yield
            for i in range(4):
                n = n0 + i
                P.op(T, lambda e, i=i, n=n: e.matmul(VN, lhsT=pbf["TTb"][:, i, :], rhs=hb["vb"][:, n, :], start=True, stop=False), reads=[pbf["R_TT"], hb["R_tok"]], writes=[rS])
                P.op(T, lambda e, i=i: e.matmul(VN, lhsT=pbf["nwT"][:, i, :], rhs=Sb, start=False, stop=True), reads=[pbf["R_nw"], R_Sb], writes=[rS])
                yield
                P.op(S, lambda e: e.activation(out=vnb, in_=VN, func=AF.Copy), reads=[rS], writes=[R_vn])
                yield
                P.op(T, lambda e, i=i: e.matmul(OO, lhsT=pbf["qsT"][:, i, :], rhs=Sb, start=True, stop=False), reads=[pbf["R_qs"], R_Sb], writes=[rS])
                P.op(T, lambda e, i=i: e.matmul(OO, lhsT=pbf["QKm"][:, i, :], rhs=vnb, start=False, stop=True), reads=[pbf["R_QK"], R_vn], writes=[rS])
                P.op(T, lambda e, n=n: e.matmul(DS, lhsT=hb["ksc"][:, n, :], rhs=vnb, start=True, stop=True), reads=[hb["R_tok"], R_vn], writes=[rS])
                yield
                P.op(V, lambda e, n=n: e.scalar_tensor_tensor(out=Sf, in0=Sf, scalar=DEC[:, n, h:h + 1], in1=DS, op0=ALU.mult, op1=ALU.add),
                     reads=[R_Sf, R_sm, rS], writes=[R_Sf])
                yield
                P.op(S, lambda e: e.activation(out=Sb, in_=Sf, func=AF.Copy), reads=[R_Sf], writes=[R_Sb])
                yield
                P.op(S, lambda e: e.activation(out=jk, in_=OO, func=AF.Square, accum_out=fs[:, 0:1]), reads=[rS], writes=[R_jk, R_fs])
                yield
                P.op(S, lambda e: e.activation(out=fs[:, 1:2], in_=fs[:, 0:1], func=AF.Ln, scale=1.0 / 128, bias=EPSC), reads=[R_fs, R_const], writes=[R_fs])
                yield
                P.op(S, lambda e: e.activation(out=fs[:, 2:3], in_=fs[:, 1:2], func=AF.Exp, scale=-0.5), reads=[R_fs], writes=[R_fs])
                yield
                P.op(V, lambda e, n=n: e.scalar_tensor_tensor(out=mxd, in0=OO, scalar=fs[:, 2:3], in1=siluz[:, n, h * 128:(h + 1) * 128],
                                                             op0=ALU.mult, op1=ALU.mult), reads=[rS, R_fs, R_sz[n]], writes=[R_mxd])
                yield
                P.op(T, lambda e: e.transpose(out=ptr[:, 256:384], in_=mxd, identity=ident_bf[:]), reads=[R_mxd, R_const], writes=[rptr])
                yield
                P.op(V, lambda e, n=n: e.tensor_copy(out=mixT[:, h, n * 128:(n + 1) * 128], in_=ptr[:, 256:384]), reads=[rptr], writes=[R_mixT[h][n]])
                yield

        run_rr([dn_prep(0)])
        run_rr([dn_pre(0, 0, 0), dn_pre(0, 0, 1)])
        for h in range(4):
            bg[0] = dn_prep(h + 1) if h + 1 < 4 else None
            for jb in range(NBt):
                if jb + 1 < NBt:
                    nxt = (dn_pre(h, jb + 1, 0), dn_pre(h, jb + 1, 1))
                elif h + 1 < 4:
                    while bg[0] is not None:
                        bg_step()
                    nxt = (dn_pre(h + 1, 0, 0), dn_pre(h + 1, 0, 1))
                else:
                    nxt = (None, None)
                run_rr([nxt[0], nxt[1], dn_scan(h, jb)], use_bg=True)
            while bg[0] is not None:
                bg_step()

        if dbg:
            P.barrier()
            main.reset(base_off)
            dbo = main.alloc([8 * L], F32)
            R_dbo = Res()
            P.op(V, lambda e: e.tensor_copy(out=dbo, in_=mixT.rearrange("p a b -> p (a b)")), writes=[R_dbo])
            P.dma("sync", lambda e: e.dma_start(out=dbg_d[:, :], in_=dbo), reads=[R_dbo], is_output=True)
            P.barrier()

        P.barrier()
        if L >= 2048:
            main.reset(0)
            wu = [main.alloc([8, 1024], BF16) for _ in range(2)]
            main.reset(base_off0)
        else:
            main.reset(base_off0)
            wu = [main.alloc([8, 1024], BF16) for _ in range(2)]
        wo = main.alloc([8, D], BF16)
        R_wo = Res()
        P.dma("sync", lambda e: e.dma_start(out=wo, in_=wout_s[:, :, :]), reads=[R_wrest], writes=[R_wo])
        x1b = [main.alloc([4, D], F32) for _ in range(2)]
        R_x1b = [[Res() for _ in range(4)] for _ in range(2)]
        hmTb = [main.alloc([8, 512], BF16) for _ in range(2)]
        R_hmb2 = [[Res() for _ in range(4)] for _ in range(2)]
        hT = main.alloc([8, 512], BF16)
        R_hT = [Res() for _ in range(8)]
        wd = [main.alloc([8, 1024], BF16) for _ in range(2)]
        R_wu = [Res(), Res()]
        R_wd = [Res(), Res()]
        xr = [main.alloc([D], F32) for _ in range(2)]
        R_xr = [Res(), Res()]
        xo = [main.alloc([D], F32) for _ in range(2)]
        R_xo = [Res(), Res()]
        hmb = main.alloc([D], BF16)
        R_hmb = Res()
        rl = [main.alloc([512], BF16) for _ in range(2)]
        R_rl = [Res(), Res()]
        mstb = [main.alloc([4, 8], F32) for _ in range(2)]
        R_mstb = [Res(), Res()]
        jkp = main.alloc([D], BF16)
        jke = main.alloc([D], BF16)
        R_jkp, R_jke = Res(), Res()
        NBLK = L // 512
        wq_cnt = [0]

        def mlp_pro(blk):
            pb_ = blk % 2
            x1, hmT, mst, R_x1, R_hm, R_mst = x1b[pb_], hmTb[pb_], mstb[pb_], R_x1b[pb_], R_hmb2[pb_], R_mstb[pb_]
            for i in range(4):
                n = blk * 4 + i
                ts_ = slice(n * 128, (n + 1) * 128)
                xi = n % 2
                P.dma("sync", lambda e, xi=xi, ts_=ts_, b=b: e.dma_start(out=xr[xi], in_=x_d[b, ts_, :]), writes=[R_xr[xi]])
                for c2 in range(2):
                    for k in range(8):
                        P.op(T, lambda e, k=k, c2=c2, ts_=ts_: e.matmul(pb[c2][:, :], lhsT=mixT[:, k, ts_], rhs=wo[:, k, c2 * 512:(c2 + 1) * 512],
                                                                       start=(k == 0), stop=(k == 7)),
                             reads=[R_mixT[k][n], R_wo], writes=[rpb[c2]])
                    yield
                    P.op(V, lambda e, c2=c2, i=i, xi=xi: e.tensor_tensor(out=x1[:, i, c2 * 512:(c2 + 1) * 512], in0=pb[c2][:, :],
                                                                          in1=xr[xi][:, c2 * 512:(c2 + 1) * 512], op=ALU.add),
                         reads=[rpb[c2], R_xr[xi]], writes=[R_x1[i]])
                    yield
                P.op(S, lambda e, i=i: e.activation(out=jkp, in_=x1[:, i, :], func=AF.Square, accum_out=mst[:, i, 0:1]), reads=[R_x1[i]], writes=[R_jkp, R_mst])
                yield
                P.op(S, lambda e, i=i: e.activation(out=mst[:, i, 1:2], in_=mst[:, i, 0:1], func=AF.Ln, scale=1.0 / D, bias=EPSC), reads=[R_mst, R_const], writes=[R_mst])
                yield
                P.op(S, lambda e, i=i: e.activation(out=mst[:, i, 2:3], in_=mst[:, i, 1:2], func=AF.Exp, scale=-0.5), reads=[R_mst], writes=[R_mst])
                yield
                P.op(V, lambda e, i=i: e.tensor_scalar(out=hmb, in0=x1[:, i, :], scalar1=mst[:, i, 2:3], scalar2=None, op0=ALU.mult),
                     reads=[R_x1[i], R_mst], writes=[R_hmb])
                yield
                for k in range(8):
                    P.op(T, lambda e, k=k: e.transpose(out=ptr[:, k * 128:(k + 1) * 128], in_=hmb[:, k * 128:(k + 1) * 128], identity=ident_bf[:]),
                         reads=[R_hmb, R_const], writes=[rptr])
                P.op(S, lambda e, i=i: e.activation(out=hmT[:, :, i * 128:(i + 1) * 128], in_=ptr[:].rearrange("p (k t) -> p k t", k=8), func=AF.Copy),
                     reads=[rptr], writes=[R_hm[i]])
                yield

        def mlp_main(blk):
            pb_ = blk % 2
            x1, hmT, R_x1, R_hm = x1b[pb_], hmTb[pb_], R_x1b[pb_], R_hmb2[pb_]
            for fq in range(4):
                wi = wq_cnt[0] % 2
                wq_cnt[0] += 1
                P.dma("sync", lambda e, wi=wi, fq=fq: e.dma_start(out=wu[wi], in_=wup_s[:, :, fq * 1024:(fq + 1) * 1024]), reads=[R_wrest], writes=[R_wu[wi]])
                P.dma("sync", lambda e, wi=wi, fq=fq: e.dma_start(out=wd[wi], in_=wdn_s[:, fq * 8:(fq + 1) * 8, :]), reads=[R_wrest], writes=[R_wd[wi]])
                for fc in range(8):
                    pz = 2 + fc % 2
                    for k in range(8):
                        P.op(T, lambda e, k=k, fc=fc, wi=wi, pz=pz: e.matmul(pb[pz][:, :], lhsT=wu[wi][:, k, fc * 128:(fc + 1) * 128], rhs=hmT[:, k, :],
                                                                          start=(k == 0), stop=(k == 7)),
                             reads=[R_wu[wi]] + R_hm, writes=[rpb[pz]])
                    ri = fc % 2
                    P.op(S, lambda e, pz=pz, ri=ri: e.activation(out=rl[ri], in_=pb[pz][:, :], func=AF.Relu), reads=[rpb[pz]], writes=[R_rl[ri]])
                    P.op(G, lambda e, fc=fc, ri=ri: e.tensor_tensor(out=hT[:, fc, :], in0=rl[ri], in1=rl[ri], op=ALU.mult), reads=[R_rl[ri]], writes=[R_hT[fc]])
                    yield
                for i in range(4):
                    for c2 in range(2):
                        pz = 4 + c2
                        for fc in range(8):
                            P.op(T, lambda e, fc=fc, i=i, c2=c2, wi=wi, pz=pz: e.matmul(pb[pz][:, :], lhsT=hT[:, fc, i * 128:(i + 1) * 128],
                                                                                      rhs=wd[wi][:, fc, c2 * 512:(c2 + 1) * 512],
                                                                                      start=(fc == 0), stop=(fc == 7)),
                                 reads=[R_hT[fc], R_wd[wi]], writes=[rpb[pz]])
                        P.op(V, lambda e, i=i, c2=c2, pz=pz: e.tensor_tensor(out=x1[:, i, c2 * 512:(c2 + 1) * 512], in0=x1[:, i, c2 * 512:(c2 + 1) * 512],
                                                                           in1=pb[pz][:, :], op=ALU.add),
                             reads=[R_x1[i], rpb[pz]], writes=[R_x1[i]])
                        yield

        def mlp_epi(blk):
            pb_ = blk % 2
            x1, mst, R_x1, R_mst = x1b[pb_], mstb[pb_], R_x1b[pb_], R_mstb[pb_]
            for i in range(4):
                n = blk * 4 + i
                xi = n % 2
                P.op(S, lambda e, i=i: e.activation(out=jke, in_=x1[:, i, :], func=AF.Square, accum_out=mst[:, i, 3:4]), reads=[R_x1[i]], writes=[R_jke, R_mst])
                yield
                P.op(S, lambda e, i=i: e.activation(out=mst[:, i, 4:5], in_=mst[:, i, 3:4], func=AF.Ln, scale=1.0 / D, bias=EPSC), reads=[R_mst, R_const], writes=[R_mst])
                yield
                P.op(S, lambda e, i=i: e.activation(out=mst[:, i, 5:6], in_=mst[:, i, 4:5], func=AF.Exp, scale=-0.5), reads=[R_mst], writes=[R_mst])
                yield
                P.op(V, lambda e, i=i, xi=xi: e.scalar_tensor_tensor(out=xo[xi], in0=x1[:, i, :], scalar=mst[:, i, 5:6], in1=fnw_bc[:], op0=ALU.mult, op1=ALU.mult),
                     reads=[R_x1[i], R_mst, R_const], writes=[R_xo[xi]])
                P.dma("sync", lambda e, xi=xi, n=n, b=b: e.dma_start(out=y_d[b, n * 128:(n + 1) * 128, :], in_=xo[xi]), reads=[R_xo[xi]], is_output=True)
                yield

        def seq_gens(*gens):
            for g in gens:
                if g is not None:
                    yield from g

        run_rr([mlp_pro(0)])
        for blk in range(NBLK):
            side = seq_gens(mlp_epi(blk - 1) if blk >= 1 else None, mlp_pro(blk + 1) if blk + 1 < NBLK else None)
            run_rr([side, mlp_main(blk)])
        run_rr([mlp_epi(NBLK - 1)])
        P.barrier()

    P.finish()
    P.emit()
    P.close()
    return nc


_NC_CACHE = {}


def kernel(x, positions, attn_norm_w, w_in, conv_w, a_log, dt_bias, dn_norm_w, lambda_q1, lambda_k1,
           lambda_q2, lambda_k2, diff_norm_w, group_scale, w_out, mlp_norm_w, w_up, w_down, final_norm_w):
    x = np.asarray(x, dtype=np.float32)
    B, L, _ = x.shape
    NB = B // NCORES
    key = (NB, L)
    if key not in _NC_CACHE:
        _NC_CACHE[key] = build(NB, L)
    nc = _NC_CACHE[key]

    def f(a):
        return np.ascontiguousarray(np.asarray(a, dtype=np.float32))
    shared = {
        "positions": np.ascontiguousarray(np.asarray(positions, dtype=np.int32)),
        "attn_norm_w": f(np.asarray(attn_norm_w)[0]), "w_in": f(np.asarray(w_in)[0]), "conv_w": f(np.asarray(conv_w)[0]),
        "a_log": f(np.asarray(a_log)[0]), "dt_bias": f(np.asarray(dt_bias)[0]), "dn_norm_w": f(np.asarray(dn_norm_w)[0]),
        "lambda_q1": f(np.asarray(lambda_q1)[0]), "lambda_k1": f(np.asarray(lambda_k1)[0]),
        "lambda_q2": f(np.asarray(lambda_q2)[0]), "lambda_k2": f(np.asarray(lambda_k2)[0]),
        "diff_norm_w": f(np.asarray(diff_norm_w)[0]), "group_scale": f(np.asarray(group_scale)[0]),
        "w_out": f(np.asarray(w_out)[0]), "mlp_norm_w": f(np.asarray(mlp_norm_w)[0]), "w_up": f(np.asarray(w_up)[0]),
        "w_down": f(np.asarray(w_down)[0]), "final_norm_w": f(final_norm_w),
    }
    in_maps = []
    for c in range(NCORES):
        m = dict(shared)
        m["x"] = np.ascontiguousarray(x[c * NB:(c + 1) * NB])
        in_maps.append(m)
    res = run_bass_kernel_spmd(nc, in_maps, core_ids=list(range(NCORES)))
    return np.concatenate([np.asarray(r["y"]) for r in res.results], axis=0).astype(np.float32)
```

```python
import math
import os
import numpy as np
from contextlib import ExitStack
import concourse.bass as bass
import concourse.mybir as mybir
from concourse.bass_utils import run_bass_kernel_spmd

F32 = mybir.dt.float32
BF16 = mybir.dt.bfloat16
I32 = mybir.dt.int32
AF = mybir.ActivationFunctionType
ALU = mybir.AluOpType

D = 1024
DIN = 3592
DFF = 4096
NCORES = 8
EPS = 1e-6
ROPE_THETA = 500000.0
LAMBDA_INIT = 0.8 - 0.6 * math.exp(-0.3 * 0)

EPOCH = 30000
ENGS = ("sync", "scalar", "vector", "gpsimd", "tensor")
COMPUTE = ("scalar", "vector", "gpsimd", "tensor")


class Res:
    __slots__ = ("w", "r", "excl")

    def __init__(self, excl=False):
        self.w = None
        self.r = []
        self.excl = excl


class Prog:
    def __init__(self, nc, n_dma_sems=(("sync", 24), ("scalar", 4), ("gpsimd", 8))):
        self.nc = nc
        self.ops = {e: [] for e in ENGS}
        self.cnt = {e: 0 for e in COMPUTE}
        self.waited = {e: {} for e in ENGS}
        self.dma_pool = {q: n for q, n in n_dma_sems}
        self.dma_uses = {q: [0] * n for q, n in n_dma_sems}
        self.dma_rr = {q: 0 for q, n in n_dma_sems}
        self.semkeys = set()
        self.stack = ExitStack()
        self.out_tokens = []
        self.last_tok = {}

    def sbuf(self, name, shape, dtype):
        return self.stack.enter_context(self.nc.sbuf_tensor(name, list(shape), dtype))

    def psum(self, name, shape, dtype=F32):
        return self.stack.enter_context(self.nc.psum_tensor(name, list(shape), dtype))

    def _need(self, eng, deps):
        best = {}
        for d in deps:
            if d is None:
                continue
            k, v = d
            if best.get(k, 0) < v:
                best[k] = v
        waits = []
        wd = self.waited[eng]
        for k, v in best.items():
            if eng == "tensor" and k[0] == "tensor":
                continue
            if wd.get(k, 0) >= v:
                continue
            wd[k] = v
            waits.append((k, v))
        return waits

    def _deps(self, reads, writes):
        deps = []
        for r in reads:
            deps.append(r.w)
            if r.excl:
                deps.extend(r.r)
        for w in writes:
            deps.append(w.w)
            deps.extend(w.r)
        return deps

    def _mark(self, tok, reads, writes):
        for r in reads:
            if r.excl:
                r.w = tok
                r.r = []
            else:
                r.r.append(tok)
        for w in writes:
            w.w = tok
            w.r = []
        self.last_tok[tok[0]] = tok

    def op(self, eng, fn, reads=(), writes=()):
        waits = self._need(eng, self._deps(reads, writes))
        self.cnt[eng] += 1
        c = self.cnt[eng]
        key = (eng, (c - 1) // EPOCH)
        tok = (key, (c - 1) % EPOCH + 1)
        self.semkeys.add(key)
        self._mark(tok, reads, writes)
        self.ops[eng].append((fn, waits, key, 1))
        return tok

    def dma(self, q, fn, reads=(), writes=(), is_output=False):
        deps = self._deps(reads, writes)
        i = self.dma_rr[q]
        self.dma_rr[q] = (i + 1) % self.dma_pool[q]
        key = ("dma_" + q, i)
        self.semkeys.add(key)
        prev = self.dma_uses[q][i]
        if prev > 0:
            deps.append((key, 16 * prev))
        self.dma_uses[q][i] = prev + 1
        tok = (key, 16 * (prev + 1))
        waits = self._need(q, deps)
        self._mark(tok, reads, writes)
        self.ops[q].append((fn, waits, key, 16))
        if is_output:
            self.out_tokens.append(tok)
        return tok

    def barrier(self):
        toks = list(self.last_tok.values())
        for e in ENGS:
            waits = self._need(e, toks)
            if waits:
                self.ops[e].append((None, waits, None, 0))

    def finish(self, q="sync"):
        waits = self._need(q, self.out_tokens)
        self.ops[q].append((None, waits, None, 0))

    def emit(self):
        nc = self.nc
        sems = {}
        for k in sorted(self.semkeys, key=str):
            sems[k] = self.stack.enter_context(nc.semaphore("s_%s_%d" % (k[0], k[1])))
        with nc.Block() as block:
            def make(engname):
                def body(e):
                    for fn, waits, key, inc in self.ops[engname]:
                        for (k, v) in waits:
                            e.wait_ge(sems[k], v)
                        if fn is not None:
                            fn(e).then_inc(sems[key], inc)
                return body
            block.sync(make("sync"))
            block.scalar(make("scalar"))
            block.vector(make("vector"))
            block.gpsimd(make("gpsimd"))
            block.tensor(make("tensor"))

    def close(self):
        self.stack.close()


class Arena:
    def __init__(self, P, name, kib):
        self.n = kib * 256
        self.t = P.sbuf(name, [128, self.n], F32)
        self.off = 0

    def reset(self, off=0):
        self.off = off

    def alloc(self, shape, dtype):
        n = 1
        for s in shape:
            n *= s
        nb = n * (2 if dtype == BF16 else 4)
        nw = (nb + 3) // 4
        nw = (nw + 7) // 8 * 8
        assert self.off + nw <= self.n, "arena overflow %d + %d > %d" % (self.off, nw, self.n)
        ap = self.t[:, self.off:self.off + nw]
        self.off += nw
        if dtype == BF16:
            ap = ap.bitcast(BF16)[:, 0:n]
        elif dtype == I32:
            ap = ap.bitcast(I32)[:, 0:n]
        else:
            ap = ap[:, 0:n]
        if len(shape) == 2:
            ap = ap.rearrange("p (a b) -> p a b", a=shape[0])
        elif len(shape) == 3:
            ap = ap.rearrange("p (a b c) -> p a b c", a=shape[0], b=shape[1])
        return ap


def build(NB, L, dbg=False, upto=None):
    NT = L // 128
    NG = L // 512
    assert L % 512 == 0
    nc = bass.Bass("TRN2", target_bir_lowering=False)
    x_d = nc.dram_tensor("x", [NB, L, D], F32, kind="ExternalInput").ap()
    pos_d = nc.dram_tensor("positions", [L], I32, kind="ExternalInput").ap()
    anw_d = nc.dram_tensor("attn_norm_w", [D], F32, kind="ExternalInput").ap()
    win_d = nc.dram_tensor("w_in", [D, DIN], F32, kind="ExternalInput").ap()
    convw_d = nc.dram_tensor("conv_w", [4, 1536], F32, kind="ExternalInput").ap()
    alog_d = nc.dram_tensor("a_log", [4], F32, kind="ExternalInput").ap()
    dtb_d = nc.dram_tensor("dt_bias", [4], F32, kind="ExternalInput").ap()
    dnw_d = nc.dram_tensor("dn_norm_w", [128], F32, kind="ExternalInput").ap()
    lq1_d = nc.dram_tensor("lambda_q1", [64], F32, kind="ExternalInput").ap()
    lk1_d = nc.dram_tensor("lambda_k1", [64], F32, kind="ExternalInput").ap()
    lq2_d = nc.dram_tensor("lambda_q2", [64], F32, kind="ExternalInput").ap()
    lk2_d = nc.dram_tensor("lambda_k2", [64], F32, kind="ExternalInput").ap()
    dfw_d = nc.dram_tensor("diff_norm_w", [128], F32, kind="ExternalInput").ap()
    gs_d = nc.dram_tensor("group_scale", [D], F32, kind="ExternalInput").ap()
    wout_d = nc.dram_tensor("w_out", [D, D], F32, kind="ExternalInput").ap()
    mnw_d = nc.dram_tensor("mlp_norm_w", [D], F32, kind="ExternalInput").ap()
    wup_d = nc.dram_tensor("w_up", [D, DFF], F32, kind="ExternalInput").ap()
    wdn_d = nc.dram_tensor("w_down", [DFF, D], F32, kind="ExternalInput").ap()
    fnw_d = nc.dram_tensor("final_norm_w", [D], F32, kind="ExternalInput").ap()
    y_d = nc.dram_tensor("y", [NB, L, D], F32, kind="ExternalOutput").ap()
    if dbg:
        dbg_d = nc.dram_tensor("dbg", [128, 8 * L], F32, kind="ExternalOutput").ap()
    win_s = nc.dram_tensor("win_s", [128, 8, DIN], BF16).ap()
    wout_s = nc.dram_tensor("wout_s", [128, 8, D], BF16).ap()
    wup_s = nc.dram_tensor("wup_s", [128, 8, DFF], BF16).ap()
    wdn_s = nc.dram_tensor("wdn_s", [128, 32, D], BF16).ap()

    P = Prog(nc)

    def early(tag):
        if upto != tag:
            return False
        P.barrier()
        zt = P.sbuf("zt_" + tag, [128, 64], F32)
        rz = Res()
        P.op("vector", lambda e: e.memset(zt[:], 1.0), writes=[rz])
        P.dma("sync", lambda e: e.dma_start(out=dbg_d[:, 0:64], in_=zt[:]), reads=[rz], is_output=True)
        P.finish(); P.emit(); P.close()
        return True
    V, S, G, T = "vector", "scalar", "gpsimd", "tensor"

    ident_bf = P.sbuf("ident_bf", [128, 128], BF16)
    ident_f = P.sbuf("ident_f", [128, 128], F32)
    ones_bf = P.sbuf("ones_bf", [128, 128], BF16)
    ones_f = P.sbuf("ones_f", [128, 128], F32)
    UI = P.sbuf("UI", [128, 128], F32)
    UI_bf = P.sbuf("UI_bf", [128, 128], BF16)
    negSL = P.sbuf("negSL", [128, 128], F32)
    cosT = P.sbuf("cosT", [128, NT, 8], F32)
    sinT = P.sbuf("sinT", [128, NT, 8], F32)
    cvec = P.sbuf("cvec", [128, 64], F32)
    convw = P.sbuf("convw", [128, 12, 4], F32)
    fnw_bc = P.sbuf("fnw_bc", [128, D], F32)
    R_const = Res()
    EPSC = cvec[:, 0:1]
    ONEC = cvec[:, 1:2]
    LAMN = cvec[:, 2:3]
    NEGA = cvec[:, 4:8]
    DTB = cvec[:, 8:12]
    EPSL2 = cvec[:, 12:13]

    main = Arena(P, "main", 199)
    hnT = P.sbuf("hnT", [128, 8, L], BF16) if False else None

    pb = [P.psum("pb%d" % i, [128, 512], F32) for i in range(7)]
    rpb = [Res(True) for _ in range(7)]
    ptr = P.psum("ptr", [128, 1024], BF16)
    rptr = Res(True)

    P.op(G, lambda e: e.memset(ident_f[:], 1.0), writes=[R_const])
    P.op(G, lambda e: e.affine_select(out=ident_f[:], in_=ident_f[:], pattern=[[1, 128]], compare_op=ALU.is_equal,
                                      fill=0.0, base=0, channel_multiplier=-1), reads=[R_const], writes=[R_const])
    P.op(V, lambda e: e.tensor_copy(out=ident_bf[:], in_=ident_f[:]), reads=[R_const], writes=[R_const])
    P.op(G, lambda e: e.memset(ones_f[:], 1.0), writes=[R_const])
    P.op(G, lambda e: e.memset(ones_bf[:], 1.0), writes=[R_const])
    P.op(G, lambda e: e.memset(UI[:], 1.0), writes=[R_const])
    P.op(G, lambda e: e.affine_select(out=UI[:], in_=UI[:], pattern=[[1, 128]], compare_op=ALU.is_ge,
                                      fill=0.0, base=0, channel_multiplier=-1), reads=[R_const], writes=[R_const])
    P.op(V, lambda e: e.tensor_copy(out=UI_bf[:], in_=UI[:]), reads=[R_const], writes=[R_const])
    P.op(G, lambda e: e.memset(negSL[:], -1.0), writes=[R_const])
    P.op(G, lambda e: e.affine_select(out=negSL[:], in_=negSL[:], pattern=[[-1, 128]], compare_op=ALU.is_gt,
                                      fill=0.0, base=0, channel_multiplier=1), reads=[R_const], writes=[R_const])
    P.op(V, lambda e: e.memset(cvec[:], 0.0), writes=[R_const])
    P.op(V, lambda e: e.memset(cvec[:, 0:1], EPS), reads=[R_const], writes=[R_const])
    P.op(V, lambda e: e.memset(cvec[:, 1:2], 1.0), reads=[R_const], writes=[R_const])
    P.op(V, lambda e: e.memset(cvec[:, 12:13], 1e-6), reads=[R_const], writes=[R_const])
    P.dma("sync", lambda e: e.dma_start(out=fnw_bc[:], in_=fnw_d.partition_broadcast(128)), writes=[R_const])
    P.dma("sync", lambda e: e.dma_start(out=cvec[:, 16:20], in_=alog_d.partition_broadcast(128)), reads=[R_const], writes=[R_const])
    P.dma("sync", lambda e: e.dma_start(out=cvec[:, 8:12], in_=dtb_d.partition_broadcast(128)), reads=[R_const], writes=[R_const])
    P.op(S, lambda e: e.activation(out=cvec[:, 20:24], in_=cvec[:, 16:20], func=AF.Exp), reads=[R_const], writes=[R_const])
    P.op(V, lambda e: e.tensor_scalar(out=cvec[:, 4:8], in0=cvec[:, 20:24], scalar1=-1.0, scalar2=None, op0=ALU.mult),
         reads=[R_const], writes=[R_const])

    if early('consts'):
        return nc
    main.reset()
    lt = main.alloc([4, 64], F32)
    R_s = Res()
    for i, d_ in enumerate((lq1_d, lk1_d, lq2_d, lk2_d)):
        P.dma("sync", lambda e, i=i, d_=d_: e.dma_start(out=lt[:, i, :], in_=d_.partition_broadcast(128)), reads=[R_s], writes=[R_s])
    lp = main.alloc([2, 64], F32)
    P.op(V, lambda e: e.tensor_tensor(out=lp[:, 0, :], in0=lt[:, 0, :], in1=lt[:, 1, :], op=ALU.mult), reads=[R_s], writes=[R_s])
    P.op(V, lambda e: e.tensor_tensor(out=lp[:, 1, :], in0=lt[:, 2, :], in1=lt[:, 3, :], op=ALU.mult), reads=[R_s], writes=[R_s])
    P.op(V, lambda e: e.reduce_sum(out=cvec[:, 24:25], in_=lp[:, 0, :], axis=mybir.AxisListType.X), reads=[R_s, R_const], writes=[R_const])
    P.op(V, lambda e: e.reduce_sum(out=cvec[:, 25:26], in_=lp[:, 1, :], axis=mybir.AxisListType.X), reads=[R_s, R_const], writes=[R_const])
    P.op(S, lambda e: e.activation(out=cvec[:, 26:28], in_=cvec[:, 24:26], func=AF.Exp), reads=[R_const], writes=[R_const])
    P.op(V, lambda e: e.tensor_tensor(out=cvec[:, 28:29], in0=cvec[:, 27:28], in1=cvec[:, 26:27], op=ALU.subtract), reads=[R_const], writes=[R_const])
    P.op(V, lambda e: e.tensor_scalar(out=cvec[:, 2:3], in0=cvec[:, 28:29], scalar1=-LAMBDA_INIT, scalar2=None, op0=ALU.add),
         reads=[R_const], writes=[R_const])
    cwr = main.alloc([1536], F32)
    P.dma("sync", lambda e: e.dma_start(out=cwr[0:4, :], in_=convw_d[:, :]), reads=[R_s], writes=[R_s])
    for c in range(12):
        P.op(T, lambda e, c=c: e.matmul(pb[0][:, c * 4:(c + 1) * 4], lhsT=cwr[0:4, c * 128:(c + 1) * 128], rhs=ident_f[0:4, 0:4],
                                         start=True, stop=True), reads=[R_s, R_const], writes=[rpb[0]])
    P.op(V, lambda e: e.tensor_copy(out=convw[:].rearrange("p a b -> p (a b)"), in_=pb[0][:, 0:48]), reads=[rpb[0]], writes=[R_const])
    posi = main.alloc([128], I32)
    posf = main.alloc([128], F32)
    P.dma("sync", lambda e: e.dma_start(out=posi[0:NT, :], in_=pos_d.rearrange("(n p) -> n p", p=128)), reads=[R_s], writes=[R_s])
    P.op(V, lambda e: e.tensor_copy(out=posf[0:NT, :], in_=posi[0:NT, :]), reads=[R_s], writes=[R_s])
    P.op(T, lambda e: e.matmul(pb[1][:, 0:NT], lhsT=posf[0:NT, :], rhs=ident_f[0:NT, 0:NT], start=True, stop=True),
         reads=[R_s, R_const], writes=[rpb[1]])
    post = main.alloc([NT], F32)
    P.op(V, lambda e: e.tensor_copy(out=post, in_=pb[1][:, 0:NT]), reads=[rpb[1]], writes=[R_s])
    invf = main.alloc([8], F32)
    for i in range(8):
        fr = float(np.float32(ROPE_THETA) ** np.float32(-(2.0 * i) / 16.0))
        P.op(V, lambda e, i=i, fr=fr: e.memset(invf[:, i:i + 1], fr), reads=[R_s], writes=[R_s])
    ang = main.alloc([NT, 8], F32)
    P.op(V, lambda e: e.tensor_tensor(out=ang, in0=post.unsqueeze(2).to_broadcast([128, NT, 8]),
                                      in1=invf.unsqueeze(1).to_broadcast([128, NT, 8]), op=ALU.mult), reads=[R_s], writes=[R_s])
    tA = main.alloc([NT, 8], F32)
    tB = main.alloc([NT, 8], F32)
    tI = main.alloc([NT, 8], I32)
    TWO_PI = 2.0 * math.pi
    for (shift, dst) in ((0.0, sinT), (math.pi / 2.0, cosT)):
        P.op(V, lambda e, shift=shift: e.tensor_scalar(out=tA, in0=ang, scalar1=shift, scalar2=None, op0=ALU.add), reads=[R_s], writes=[R_s])
        P.op(V, lambda e: e.tensor_scalar(out=tB, in0=tA, scalar1=1.0 / TWO_PI, scalar2=None, op0=ALU.mult), reads=[R_s], writes=[R_s])
        P.op(V, lambda e: e.tensor_copy(out=tI, in_=tB), reads=[R_s], writes=[R_s])
        P.op(V, lambda e: e.tensor_copy(out=tB, in_=tI), reads=[R_s], writes=[R_s])
        P.op(V, lambda e: e.scalar_tensor_tensor(out=tA, in0=tB, scalar=-TWO_PI, in1=tA, op0=ALU.mult, op1=ALU.add), reads=[R_s], writes=[R_s])
        P.op(V, lambda e: e.tensor_scalar(out=tB, in0=tA, scalar1=math.pi, scalar2=TWO_PI, op0=ALU.is_gt, op1=ALU.mult), reads=[R_s], writes=[R_s])
        P.op(V, lambda e: e.tensor_tensor(out=tA, in0=tA, in1=tB, op=ALU.subtract), reads=[R_s], writes=[R_s])
        P.op(V, lambda e: e.tensor_scalar(out=tB, in0=tA, scalar1=-math.pi, scalar2=TWO_PI, op0=ALU.is_lt, op1=ALU.mult), reads=[R_s], writes=[R_s])
        P.op(V, lambda e: e.tensor_tensor(out=tA, in0=tA, in1=tB, op=ALU.add), reads=[R_s], writes=[R_s])
        P.op(S, lambda e, dst=dst: e.activation(out=dst[:], in_=tA, func=AF.Sin), reads=[R_s, R_const], writes=[R_const])

    if early('setup'):
        return nc
    P.barrier()
    main.reset()
    rowv = P.sbuf("rowv", [128, 8, 4], F32)
    R_rv = Res()
    vraw = main.alloc([3, 128], F32)
    for wi, vd in enumerate((anw_d, mnw_d, gs_d)):
        P.dma("sync", lambda e, wi=wi, vd=vd: e.dma_start(out=vraw[0:8, wi, :], in_=vd.rearrange("(k p) -> k p", p=128)), reads=[R_rv], writes=[R_rv])
    for wi in range(3):
        P.op(T, lambda e, wi=wi: e.matmul(pb[0][:, wi * 8:(wi + 1) * 8], lhsT=vraw[0:8, wi, :], rhs=ident_f[0:8, 0:8], start=True, stop=True),
             reads=[R_rv, R_const], writes=[rpb[0]])
    P.op(V, lambda e: e.tensor_copy(out=rowv[:, :, 0:3], in_=pb[0][:, 0:24].rearrange("p (w k) -> p k w", w=3)), reads=[rpb[0]], writes=[R_rv])
    for k in range(8):
        src = dnw_d if k < 4 else dfw_d
        P.dma("sync", lambda e, k=k, src=src: e.dma_start(out=rowv[:, k, 3:4], in_=src.rearrange("(p o) -> p o", o=1)), reads=[R_rv], writes=[R_rv])
    P.op(V, lambda e: e.tensor_scalar(out=rowv[:, 4:8, 3:4], in0=rowv[:, 4:8, 3:4], scalar1=1.0 - LAMBDA_INIT, scalar2=None, op0=ALU.mult),
         reads=[R_rv], writes=[R_rv])
    P.op(V, lambda e: e.tensor_tensor(out=rowv[:, :, 2:3], in0=rowv[:, :, 2:3], in1=rowv[:, :, 3:4], op=ALU.mult), reads=[R_rv], writes=[R_rv])
    stg_f = [main.alloc([4096], F32) for _ in range(2)]
    stg_b = [main.alloc([4096], BF16) for _ in range(2)]
    R_sf = [Res(), Res()]
    R_sb = [Res(), Res()]
    R_win = Res()
    R_wrest = Res()
    jobs = []
    for k in range(8):
        jobs.append((win_d[k * 128:(k + 1) * 128, :], DIN, rowv[:, k, 0:1], win_s[:, k, :]))
    for j, (src, n, sc, dst) in enumerate(jobs):
        b = j % 2
        P.dma("sync" if j % 2 == 0 else "scalar", lambda e, b=b, src=src, n=n: e.dma_start(out=stg_f[b][:, 0:n], in_=src), writes=[R_sf[b]])
        if j % 2 == 0:
            P.op(S, lambda e, b=b, n=n, sc=sc: e.activation(out=stg_b[b][:, 0:n], in_=stg_f[b][:, 0:n], func=AF.Copy, scale=sc),
                 reads=[R_sf[b], R_rv], writes=[R_sb[b]])
        else:
            P.op(V, lambda e, b=b, n=n, sc=sc: e.tensor_scalar(out=stg_b[b][:, 0:n], in0=stg_f[b][:, 0:n], scalar1=sc, scalar2=None, op0=ALU.mult),
                 reads=[R_sf[b], R_rv], writes=[R_sb[b]])
        P.dma("gpsimd", lambda e, b=b, n=n, dst=dst: e.dma_start(out=dst, in_=stg_b[b][:, 0:n]), reads=[R_sb[b]], writes=[R_win])
    P.barrier()

    RST_W = 512
    rst_off = main.n - (2 * RST_W + 2 * RST_W // 2)
    rsf = [main.t[:, rst_off + i * RST_W:rst_off + (i + 1) * RST_W] for i in range(2)]
    rsb = [main.t[:, rst_off + 2 * RST_W + i * (RST_W // 2):rst_off + 2 * RST_W + (i + 1) * (RST_W // 2)].bitcast(BF16) for i in range(2)]
    R_rsf = [Res(), Res()]
    R_rsb = [Res(), Res()]
    rest_jobs = []
    for k in range(8):
        for c in range(D // RST_W):
            rest_jobs.append((wout_d[k * 128:(k + 1) * 128, c * RST_W:(c + 1) * RST_W], RST_W, rowv[:, k, 2:3], wout_s[:, k, c * RST_W:(c + 1) * RST_W]))
    for k in range(8):
        for c in range(DFF // RST_W):
            rest_jobs.append((wup_d[k * 128:(k + 1) * 128, c * RST_W:(c + 1) * RST_W], RST_W, rowv[:, k, 1:2], wup_s[:, k, c * RST_W:(c + 1) * RST_W]))
    for k in range(32):
        for c in range(D // RST_W):
            rest_jobs.append((wdn_d[k * 128:(k + 1) * 128, c * RST_W:(c + 1) * RST_W], RST_W, None, wdn_s[:, k, c * RST_W:(c + 1) * RST_W]))

    def prep_rest():
        for j, (src, n, sc, dst) in enumerate(rest_jobs):
            bb_ = j % 2
            P.dma("sync", lambda e, bb_=bb_, src=src, n=n: e.dma_start(out=rsf[bb_][:, 0:n], in_=src), writes=[R_rsf[bb_]])
            yield
            if sc is None:
                eng = G if j % 2 == 0 else V
                P.op(eng, lambda e, bb_=bb_, n=n: e.tensor_copy(out=rsb[bb_][:, 0:n], in_=rsf[bb_][:, 0:n]), reads=[R_rsf[bb_]], writes=[R_rsb[bb_]])
            else:
                P.op(V, lambda e, bb_=bb_, n=n, sc=sc: e.tensor_scalar(out=rsb[bb_][:, 0:n], in0=rsf[bb_][:, 0:n], scalar1=sc, scalar2=None, op0=ALU.mult),
                     reads=[R_rsf[bb_], R_rv], writes=[R_rsb[bb_]])
            yield
            P.dma("sync", lambda e, bb_=bb_, n=n, dst=dst: e.dma_start(out=dst, in_=rsb[bb_][:, 0:n]), reads=[R_rsb[bb_]], writes=[R_wrest])
            yield

    if early('wprep'):
        return nc
    def rstd_from_ss(ss, tmp, out, scale):
        P.op(S, lambda e: e.activation(out=tmp, in_=ss, func=AF.Ln, scale=scale, bias=EPSC), reads=[R_st, R_const], writes=[R_st])
        P.op(S, lambda e: e.activation(out=out, in_=tmp, func=AF.Exp, scale=-0.5), reads=[R_st], writes=[R_st])

    bg = [None]

    def bg_step():
        if bg[0] is not None:
            try:
                next(bg[0])
            except StopIteration:
                bg[0] = None

    bg2 = [None]

    def bg2_step():
        if bg2[0] is not None:
            try:
                next(bg2[0])
            except StopIteration:
                bg2[0] = None

    def run_rr(gens, use_bg=False):
        gens = [g for g in gens if g is not None]
        while gens:
            for g in list(gens):
                try:
                    next(g)
                except StopIteration:
                    gens.remove(g)
            if use_bg:
                bg_step()


    for b in range(NB):
        if b == 0:
            bg[0] = prep_rest()
        main.reset()
        hnT = main.alloc([8, L], BF16)
        mixT = main.alloc([8, L], BF16)
        R_hnT = [Res() for _ in range(NT)]
        R_mixT = [[Res() for _ in range(NT)] for _ in range(8)]
        base_off0 = main.off
        siluz = main.alloc([NT, 512], BF16)
        R_sz = [Res() for _ in range(NT)]
        ba = main.alloc([NT, 8], F32)
        R_ba = Res()
        sm = main.alloc([12, NT, 4], F32)
        R_sm = Res()
        base_off = main.off
        dfq0 = main.alloc([4, L], BF16)
        dfq1 = main.alloc([4, L], BF16)
        dfkT = main.alloc([4, L], BF16)
        dfv = main.alloc([NT, 4, 130], BF16)
        R_dfq = [Res() for _ in range(NT)]
        R_dfk = [Res() for _ in range(NT)]
        R_dfv = [Res() for _ in range(NT)]
        wblk = [main.alloc([8, 512], BF16) for _ in range(3)]
        R_wblk = [Res(), Res(), Res()]
        qtok = [main.alloc([512], BF16) for _ in range(2)]
        R_qtok = [Res(), Res()]
        rt = [main.alloc([8, 8], F32) for _ in range(4)]
        R_rt = Res()
        off_A = main.off
        xt = [main.alloc([D], F32) for _ in range(2)]
        R_xt = [Res(), Res()]
        hnb = [main.alloc([D], BF16) for _ in range(2)]
        R_hnb = [Res(), Res()]
        st = main.alloc([NT, 4], F32)
        R_st = Res()
        P.op(G, lambda e: e.memset(dfv[:, :, :, 128:130], 1.0), writes=R_dfv)
        P.op(G, lambda e: e.memset(dfq0[64:128], 0.0), writes=R_dfq)
        P.op(G, lambda e: e.memset(dfq1[0:64], 0.0), writes=R_dfq)
        COL_Q, COL_K, COL_V = 2056, 2568, 3080
        for bi, col0 in enumerate((COL_Q, COL_K, COL_V)):
            P.dma("sync", lambda e, bi=bi, col0=col0: e.dma_start(out=wblk[bi], in_=win_s[:, :, col0:col0 + 512]), reads=[R_win], writes=[R_wblk[bi]])

        def genA(n):
            i2 = n % 2
            P.dma("sync", lambda e, b=b: e.dma_start(out=xt[i2], in_=x_d[b, n * 128:(n + 1) * 128, :]), writes=[R_xt[i2]])
            P.op(S, lambda e: e.activation(out=hnb[i2], in_=xt[i2], func=AF.Square, accum_out=st[:, n, 0:1]), reads=[R_xt[i2]], writes=[R_hnb[i2], R_st])
            yield
            P.op(S, lambda e: e.activation(out=st[:, n, 1:2], in_=st[:, n, 0:1], func=AF.Ln, scale=1.0 / D, bias=EPSC), reads=[R_st, R_const], writes=[R_st])
            yield
            P.op(S, lambda e: e.activation(out=st[:, n, 2:3], in_=st[:, n, 1:2], func=AF.Exp, scale=-0.5), reads=[R_st], writes=[R_st])
            yield
            P.op(V, lambda e: e.tensor_scalar(out=hnb[i2], in0=xt[i2], scalar1=st[:, n, 2:3], scalar2=None, op0=ALU.mult),
                 reads=[R_xt[i2], R_st], writes=[R_hnb[i2]])
            yield
            for k in range(8):
                P.op(T, lambda e, k=k: e.transpose(out=ptr[:, k * 128:(k + 1) * 128], in_=hnb[i2][:, k * 128:(k + 1) * 128], identity=ident_bf[:]),
                     reads=[R_hnb[i2], R_const], writes=[rptr])
            P.op(S, lambda e: e.activation(out=hnT[:, :, n * 128:(n + 1) * 128], in_=ptr[:].rearrange("p (k t) -> p k t", k=8), func=AF.Copy),
                 reads=[rptr], writes=[R_hnT[n]])
            yield

        pcount = [0]

        def genDF(n):
            for bi, kind in enumerate(("q", "k", "v")):
                pbi = pcount[0] % 2
                pcount[0] += 1
                ps = pb[pbi]
                rps = rpb[pbi]
                for k in range(8):
                    P.op(T, lambda e, k=k, ps=ps, bi=bi: e.matmul(ps[:, :], lhsT=hnT[:, k, n * 128:(n + 1) * 128], rhs=wblk[bi][:, k, :],
                                                                  start=(k == 0), stop=(k == 7)),
                         reads=[R_hnT[n], R_wblk[bi]], writes=[rps])
                yield
                if kind == "v":
                    P.op(S, lambda e, ps=ps: e.activation(out=dfv[:, n, :, 0:128], in_=ps[:, :].rearrange("p (h d) -> p h d", h=4), func=AF.Copy),
                         reads=[rps], writes=[R_dfv[n]])
                    yield
                    continue
                qi = pcount[0] % 2
                qt = qtok[qi]
                Rq = R_qtok[qi]
                P.op(S, lambda e, ps=ps, qt=qt: e.activation(out=qt, in_=ps[:, :], func=AF.Copy), reads=[rps], writes=[Rq])
                yield
                ps3 = ps[:, :].rearrange("p (g d) -> p g d", g=8)
                qt3 = qt.rearrange("p (g d) -> p g d", g=8)
                cb = cosT[:, n, :].unsqueeze(1).to_broadcast([128, 8, 8])
                sb = sinT[:, n, :].unsqueeze(1).to_broadcast([128, 8, 8])
                x1 = ps3[:, :, 0:8]
                x2 = ps3[:, :, 8:16]
                P.op(V, lambda e, x1=x1, cb=cb: e.tensor_tensor(out=rt[0], in0=x1, in1=cb, op=ALU.mult), reads=[rps, R_const], writes=[R_rt])
                P.op(V, lambda e, x2=x2, sb=sb: e.tensor_tensor(out=rt[1], in0=x2, in1=sb, op=ALU.mult), reads=[rps, R_const], writes=[R_rt])
                yield
                P.op(V, lambda e, x2=x2, cb=cb: e.tensor_tensor(out=rt[2], in0=x2, in1=cb, op=ALU.mult), reads=[rps, R_const], writes=[R_rt])
                P.op(V, lambda e, x1=x1, sb=sb: e.tensor_tensor(out=rt[3], in0=x1, in1=sb, op=ALU.mult), reads=[rps, R_const], writes=[R_rt])
                yield
                P.op(V, lambda e, qt3=qt3: e.tensor_tensor(out=qt3[:, :, 0:8], in0=rt[0], in1=rt[1], op=ALU.subtract), reads=[R_rt], writes=[Rq])
                P.op(V, lambda e, qt3=qt3: e.tensor_tensor(out=qt3[:, :, 8:16], in0=rt[2], in1=rt[3], op=ALU.add), reads=[R_rt], writes=[Rq])
                yield
                for h in range(4):
                    P.op(T, lambda e, h=h, qt=qt: e.transpose(out=ptr[:, h * 128:(h + 1) * 128], in_=qt[:, h * 128:(h + 1) * 128], identity=ident_bf[:]),
                         reads=[Rq, R_const], writes=[rptr])
                if kind == "q":
                    P.op(V, lambda e: e.tensor_copy(out=dfq0[0:64, :, n * 128:(n + 1) * 128], in_=ptr[0:64, 0:512].rearrange("p (h t) -> p h t", h=4)),
                         reads=[rptr], writes=[R_dfq[n]])
                    P.op(V, lambda e: e.tensor_copy(out=dfq1[64:128, :, n * 128:(n + 1) * 128], in_=ptr[64:128, 0:512].rearrange("p (h t) -> p h t", h=4)),
                         reads=[rptr], writes=[R_dfq[n]])
                else:
                    P.op(V, lambda e: e.tensor_copy(out=dfkT[:, :, n * 128:(n + 1) * 128], in_=ptr[:, 0:512].rearrange("p (h t) -> p h t", h=4)),
                         reads=[rptr], writes=[R_dfk[n]])
                yield

        run_rr([genA(0)], use_bg=True)
        for n in range(NT):
            run_rr([genA(n + 1) if n + 1 < NT else None, genDF(n)], use_bg=True)
        assert main.off <= rst_off, (main.off, rst_off)
        main.reset(off_A)
        P.barrier()

        if early('DFproj'):
            return nc
        PT = [main.alloc([2, 256], BF16) for _ in range(2)]
        R_PT = [Res(), Res()]
        dtmp = main.alloc([128], F32)
        R_dt = Res()
        mxb = main.alloc([128], BF16)
        R_mxb = Res()
        fst = main.alloc([16], F32)
        R_fst = Res()
        stb = [pb[0], pb[1]]
        rstb = [rpb[0], rpb[1]]
        accp = [[pb[2], pb[3]], [pb[4], pb[5]]]
        raccp = [[rpb[2], rpb[3]], [rpb[4], rpb[5]]]
        dfq = [dfq0, dfq1]
        NQG = NT // 2
        iters = [(h, qg, kb) for h in range(4) for qg in range(NQG) for kb in range(2 * qg + 2)]

        def df_ST(j):
            h, qg, kb = iters[j]
            bufi = j % 2
            i0 = max(0, kb - 2 * qg)
            c0 = i0 * 128
            for m in range(2):
                P.op(T, lambda e, m=m: e.matmul(
                    stb[bufi][:, m * 256 + c0:(m + 1) * 256], lhsT=dfkT[:, h, kb * 128:(kb + 1) * 128],
                    rhs=dfq[m][:, h, qg * 256 + c0:(qg + 1) * 256], start=True, stop=True),
                    reads=[R_dfk[kb]] + R_dfq[qg * 2 + i0:qg * 2 + 2], writes=[rstb[bufi]])

        def df_EXP(j):
            h, qg, kb = iters[j]
            bufi = j % 2
            c0 = max(0, kb - 2 * qg) * 128
            P.op(S, lambda e: e.activation(out=PT[bufi][:, :, c0:256], in_=stb[bufi][:, :].rearrange("p (m q) -> p m q", m=2)[:, :, c0:256],
                                           func=AF.Exp, scale=0.125), reads=[rstb[bufi]], writes=[R_PT[bufi]])
            if kb >= 2 * qg:
                for m in range(2):
                    P.op(G, lambda e, m=m: e.tensor_tensor(out=PT[bufi][:, m, c0:c0 + 128], in0=PT[bufi][:, m, c0:c0 + 128], in1=UI_bf[:], op=ALU.mult),
                         reads=[R_PT[bufi], R_const], writes=[R_PT[bufi]])

        NRING = 4
        stg = [main.alloc([2, 130], F32) for _ in range(NRING)]
        dring = [main.alloc([128], F32) for _ in range(NRING)]
        mring = [main.alloc([128], BF16) for _ in range(NRING)]
        fring = [main.alloc([8], F32) for _ in range(NRING)]
        R_ring = [Res() for _ in range(NRING)]
        fin_cnt = [0]
        pending = []

        def df_FIN(h, n, i, j):
            r = fin_cnt[0] % NRING
            fin_cnt[0] += 1
            sg, dd, mm_, ff, Rr = stg[r], dring[r], mring[r], fring[r], R_ring[r]
            for m in range(2):
                P.op(V, lambda e, m=m: e.tensor_copy(out=sg[:, m, :], in_=accp[m][i][:, 0:130]), reads=[raccp[m][i]], writes=[Rr])

            def part2():
                P.op(V, lambda e: e.reciprocal(out=ff[:, 0:2], in_=sg[:, :, 128:129].rearrange("p m o -> p (m o)")), reads=[Rr], writes=[Rr])
                P.op(V, lambda e: e.tensor_tensor(out=ff[:, 2:3], in0=ff[:, 1:2], in1=LAMN, op=ALU.mult), reads=[Rr, R_const], writes=[Rr])
                P.op(V, lambda e: e.tensor_scalar(out=dd, in0=sg[:, 0, 0:128], scalar1=ff[:, 0:1], scalar2=None, op0=ALU.mult), reads=[Rr], writes=[Rr])
                P.op(V, lambda e: e.scalar_tensor_tensor(out=dd, in0=sg[:, 1, 0:128], scalar=ff[:, 2:3], in1=dd, op0=ALU.mult, op1=ALU.add), reads=[Rr], writes=[Rr])

            def part3():
                P.op(S, lambda e: e.activation(out=mm_, in_=dd, func=AF.Square, accum_out=ff[:, 3:4]), reads=[Rr], writes=[Rr])
                P.op(S, lambda e: e.activation(out=ff[:, 4:5], in_=ff[:, 3:4], func=AF.Ln, scale=1.0 / 128, bias=EPSC), reads=[Rr, R_const], writes=[Rr])
                P.op(S, lambda e: e.activation(out=ff[:, 5:6], in_=ff[:, 4:5], func=AF.Exp, scale=-0.5), reads=[Rr], writes=[Rr])

            def part4():
                P.op(V, lambda e: e.tensor_scalar(out=mm_, in0=dd, scalar1=ff[:, 5:6], scalar2=None, op0=ALU.mult), reads=[Rr], writes=[Rr])
                P.op(T, lambda e: e.transpose(out=ptr[:, 0:128], in_=mm_, identity=ident_bf[:]), reads=[Rr, R_const], writes=[rptr])
                P.op(V, lambda e: e.tensor_copy(out=mixT[:, 4 + h, n * 128:(n + 1) * 128], in_=ptr[:, 0:128]), reads=[rptr], writes=[R_mixT[4 + h][n]])
            pending.append((j + 1, part2))
            pending.append((j + 2, part3))
            pending.append((j + 3, part4))

        def df_flush(j):
            keep = []
            for (due, fn) in pending:
                if due <= j:
                    fn()
                else:
                    keep.append((due, fn))
            pending[:] = keep

        def df_PV(j):
            h, qg, kb = iters[j]
            bufi = j % 2
            i0 = max(0, kb - 2 * qg)
            for m in range(2):
                for i in range(i0, 2):
                    P.op(T, lambda e, m=m, i=i: e.matmul(
                        accp[m][i][:, 0:129], lhsT=PT[bufi][:, m, i * 128:(i + 1) * 128], rhs=dfv[:, kb, h, 0:129],
                        start=(kb == 0), stop=(kb == 2 * qg + i)),
                        reads=[R_PT[bufi], R_dfv[kb]], writes=[raccp[m][i]])
            if kb >= 2 * qg:
                df_FIN(h, kb, kb - 2 * qg, j)

        wz = wblk[0]
        R_wz = R_wblk[0]
        wba = main.alloc([8, 8], BF16)
        R_wba = Res()
        etmp = main.alloc([512], F32)
        R_et = Res()

        def dn_prolog():
            P.dma("sync", lambda e: e.dma_start(out=wz, in_=win_s[:, :, 1536:2048]), reads=[R_win], writes=[R_wz])
            P.dma("sync", lambda e: e.dma_start(out=wba, in_=win_s[:, :, 2048:2056]), reads=[R_win], writes=[R_wba])
            yield
            for n in range(NT):
                for k in range(8):
                    P.op(T, lambda e, k=k, n=n: e.matmul(pb[6][:, :], lhsT=hnT[:, k, n * 128:(n + 1) * 128], rhs=wz[:, k, :], start=(k == 0), stop=(k == 7)),
                         reads=[R_hnT[n], R_wz], writes=[rpb[6]])
                yield
                P.op(S, lambda e: e.activation(out=etmp, in_=pb[6][:, :], func=AF.Exp, scale=-1.0), reads=[rpb[6]], writes=[R_et])
                yield
                P.op(V, lambda e: e.tensor_scalar(out=etmp, in0=etmp, scalar1=1.0, scalar2=None, op0=ALU.add), reads=[R_et], writes=[R_et])
                P.op(V, lambda e: e.reciprocal(out=etmp, in_=etmp), reads=[R_et], writes=[R_et])
                yield
                P.op(V, lambda e, n=n: e.tensor_tensor(out=siluz[:, n, :], in0=pb[6][:, :], in1=etmp, op=ALU.mult), reads=[rpb[6], R_et], writes=[R_sz[n]])
                yield
                for k in range(8):
                    P.op(T, lambda e, k=k, n=n: e.matmul(pb[6][:, 0:8], lhsT=hnT[:, k, n * 128:(n + 1) * 128], rhs=wba[:, k, :], start=(k == 0), stop=(k == 7)),
                         reads=[R_hnT[n], R_wba], writes=[rpb[6]])
                P.op(V, lambda e, n=n: e.tensor_copy(out=ba[:, n, :], in_=pb[6][:, 0:8]), reads=[rpb[6]], writes=[R_ba])
                yield
            bb = ba[:, :, 0:4]
            aa = ba[:, :, 4:8]
            dtb_b = DTB.unsqueeze(1).to_broadcast([128, NT, 4])
            nega_b = NEGA.unsqueeze(1).to_broadcast([128, NT, 4])

            def smop(eng, fn, extra=()):
                P.op(eng, fn, reads=[R_sm, R_ba, R_const] + list(extra), writes=[R_sm])
            smop(S, lambda e: e.activation(out=sm[:, 0], in_=bb, func=AF.Abs))
            yield
            smop(S, lambda e: e.activation(out=sm[:, 1], in_=sm[:, 0], func=AF.Exp, scale=-1.0))
            yield
            smop(S, lambda e: e.activation(out=sm[:, 2], in_=sm[:, 1], func=AF.Ln, bias=ONEC))
            yield
            smop(V, lambda e: e.scalar_tensor_tensor(out=sm[:, 3], in0=bb, scalar=0.0, in1=sm[:, 2], op0=ALU.min, op1=ALU.subtract))
            smop(V, lambda e: e.tensor_tensor(out=sm[:, 4], in0=aa, in1=dtb_b, op=ALU.add))
            yield
            smop(S, lambda e: e.activation(out=sm[:, 0], in_=sm[:, 4], func=AF.Abs))
            yield
            smop(S, lambda e: e.activation(out=sm[:, 1], in_=sm[:, 0], func=AF.Exp, scale=-1.0))
            yield
            smop(S, lambda e: e.activation(out=sm[:, 2], in_=sm[:, 1], func=AF.Ln, bias=ONEC))
            yield
            smop(V, lambda e: e.scalar_tensor_tensor(out=sm[:, 5], in0=sm[:, 4], scalar=0.0, in1=sm[:, 2], op0=ALU.max, op1=ALU.add))
            smop(V, lambda e: e.tensor_tensor(out=sm[:, 6], in0=sm[:, 5], in1=nega_b, op=ALU.mult))
            yield
            gflat = sm[:, 6].rearrange("p n h -> p (n h)")
            P.op(T, lambda e: e.matmul(pb[6][:, 0:NT * 4], lhsT=UI[:], rhs=gflat, start=True, stop=True), reads=[R_sm, R_const], writes=[rpb[6]])
            smop(V, lambda e: e.tensor_copy(out=sm[:, 7].rearrange("p n h -> p (n h)"), in_=pb[6][:, 0:NT * 4]), extra=[rpb[6]])
            yield
            P.op(T, lambda e: e.matmul(pb[6][:, 0:NT * 4], lhsT=ones_f[:], rhs=gflat, start=True, stop=True), reads=[R_sm, R_const], writes=[rpb[6]])
            smop(V, lambda e: e.tensor_copy(out=sm[:, 8].rearrange("p n h -> p (n h)"), in_=pb[6][:, 0:NT * 4]), extra=[rpb[6]])
            yield
            smop(V, lambda e: e.tensor_tensor(out=sm[:, 9], in0=sm[:, 3], in1=sm[:, 7], op=ALU.add))
            smop(V, lambda e: e.tensor_tensor(out=sm[:, 10], in0=sm[:, 8], in1=sm[:, 7], op=ALU.subtract))
            yield
            smop(S, lambda e: e.activation(out=sm[:, 11], in_=sm[:, 9], func=AF.Exp))
            yield
            smop(S, lambda e: e.activation(out=sm[:, 10], in_=sm[:, 10], func=AF.Exp))
            yield
            smop(S, lambda e: e.activation(out=sm[:, 8], in_=sm[:, 8], func=AF.Exp))
            yield
            smop(S, lambda e: e.activation(out=sm[:, 3], in_=sm[:, 3], func=AF.Exp))
            yield

        bg2[0] = dn_prolog()
        df_ST(0)
        for j in range(len(iters)):
            if j + 1 < len(iters):
                df_ST(j + 1)
            df_EXP(j)
            df_PV(j)
            df_flush(j)
            bg_step()
            bg2_step()
        df_flush(10 ** 9)
        while bg[0] is not None:
            bg_step()

        while bg2[0] is not None:
            bg2_step()
        P.barrier()
        main.reset(base_off)
        GC, HC, C1, C2, DEC, BETA = sm[:, 7], sm[:, 9], sm[:, 11], sm[:, 10], sm[:, 8], sm[:, 3]

        NBt = NT // 4
        HB = []
        for _i in range(2):
            HB.append(dict(qT=main.alloc([L], BF16), kT=main.alloc([L], BF16), vT=main.alloc([L], BF16),
                           kbg=main.alloc([NT, 128], BF16), ksc=main.alloc([NT, 128], BF16), vb=main.alloc([NT, 128], BF16),
                           wqkv=main.alloc([3, 8, 128], BF16),
                           R_q=Res(), R_k=Res(), R_v=Res(), R_tok=Res(), R_w=Res()))
        xc = main.alloc([L + 4], F32)
        R_xc = Res()
        yv = main.alloc([L], F32)
        R_yv = Res()
        sq = main.alloc([L], BF16)
        R_sq = Res()
        rn = main.alloc([512], F32)
        R_rn = Res()
        dA = main.alloc([4, 128], F32)
        dB = main.alloc([4, 128], F32)
        dC = main.alloc([4, 128], F32)
        dD = main.alloc([4, 128], F32)
        R_dA, R_dB, R_dC, R_dD = Res(), Res(), Res(), Res()
        Yb = [main.alloc([4, 128], F32) for _ in range(2)]
        Zb = [main.alloc([4, 128], F32) for _ in range(2)]
        R_Y = [Res(), Res()]
        R_Z = [Res(), Res()]
        Nm = main.alloc([4, 128], F32)
        R_N = Res()
        PB = []
        for _i in range(2):
            PB.append(dict(TTb=main.alloc([4, 128], BF16), nwT=main.alloc([4, 128], BF16), qsT=main.alloc([4, 128], BF16),
                           QKm=main.alloc([4, 128], BF16), R_TT=Res(), R_nw=Res(), R_qs=Res(), R_QK=Res()))
        vnb = main.alloc([128], BF16)
        R_vn = Res()
        Sf = main.alloc([128], F32)
        Sb = main.alloc([128], BF16)
        R_Sf, R_Sb = Res(), Res()
        mxd = main.alloc([128], BF16)
        R_mxd = Res()
        fs = main.alloc([8], F32)
        R_fs = Res()
        jk = main.alloc([128], BF16)
        R_jk = Res()
        P.op(V, lambda e: e.memset(xc[:, 0:4], 0.0), writes=[R_xc])
        idb = ident_f[:].unsqueeze(1).to_broadcast([128, 4, 128])
        negSLb = negSL[:].unsqueeze(1).to_broadcast([128, 4, 128])
        UIb = UI[:].unsqueeze(1).to_broadcast([128, 4, 128])
        BK_PREP = 0
        BK_G = 0
        BK_KQ = 0
        HBK = [(1, 2, 3), (4, 5, 6)]
        ptrf = ptr[:].bitcast(F32)

        def dn_prep(h):
            hb = HB[h % 2]
            for c in range(3):
                P.dma("sync", lambda e, c=c: e.dma_start(out=hb["wqkv"][:, c], in_=win_s[:, :, c * 512 + h * 128:c * 512 + (h + 1) * 128]),
                      reads=[R_win], writes=[hb["R_w"]])
            yield
            for c in range(3):
                cc = c * 4 + h
                for grp in range(NG):
                    for k in range(8):
                        P.op(T, lambda e, c=c, k=k, grp=grp: e.matmul(pb[BK_PREP][:, :], lhsT=hb["wqkv"][:, c, k, :], rhs=hnT[:, k, grp * 512:(grp + 1) * 512],
                                                                      start=(k == 0), stop=(k == 7)),
                             reads=[hb["R_w"]] + R_hnT[grp * 4:grp * 4 + 4], writes=[rpb[BK_PREP]])
                    P.op(S, lambda e, grp=grp: e.activation(out=xc[:, 3 + grp * 512:3 + (grp + 1) * 512], in_=pb[BK_PREP][:, :], func=AF.Copy),
                         reads=[rpb[BK_PREP]], writes=[R_xc])
                    yield
                P.op(V, lambda e, cc=cc: e.tensor_scalar(out=yv, in0=xc[:, 3:3 + L], scalar1=convw[:, cc, 3:4], scalar2=None, op0=ALU.mult),
                     reads=[R_xc, R_const], writes=[R_yv])
                yield
                for j in (2, 1, 0):
                    P.op(V, lambda e, cc=cc, j=j: e.scalar_tensor_tensor(out=yv, in0=xc[:, j:j + L], scalar=convw[:, cc, j:j + 1], in1=yv,
                                                                        op0=ALU.mult, op1=ALU.add), reads=[R_xc, R_const, R_yv], writes=[R_yv])
                    yield
                if c == 2:
                    P.op(S, lambda e: e.activation(out=hb["vT"], in_=yv, func=AF.Silu), reads=[R_yv], writes=[hb["R_v"]])
                    yield
                    continue
                P.op(S, lambda e: e.activation(out=yv, in_=yv, func=AF.Silu), reads=[R_yv], writes=[R_yv])
                yield
                P.op(G, lambda e: e.tensor_tensor(out=sq, in0=yv, in1=yv, op=ALU.mult), reads=[R_yv], writes=[R_sq])
                yield
                dstT, Rd, qscale = (hb["qT"], hb["R_q"], 128 ** -0.5) if c == 0 else (hb["kT"], hb["R_k"], 1.0)
                for grp in range(NG):
                    gs_ = slice(grp * 512, (grp + 1) * 512)
                    P.op(T, lambda e, gs_=gs_: e.matmul(pb[BK_PREP][:, :], lhsT=ones_bf[:], rhs=sq[:, gs_], start=True, stop=True),
                         reads=[R_sq, R_const], writes=[rpb[BK_PREP]])
                    P.op(S, lambda e: e.activation(out=rn, in_=pb[BK_PREP][:, :], func=AF.Ln, bias=EPSL2), reads=[rpb[BK_PREP], R_const], writes=[R_rn])
                    yield
                    P.op(S, lambda e: e.activation(out=rn, in_=rn, func=AF.Exp, scale=-0.5), reads=[R_rn], writes=[R_rn])
                    yield
                    P.op(V, lambda e, gs_=gs_, dstT=dstT, qscale=qscale: e.scalar_tensor_tensor(out=dstT[:, gs_], in0=yv[:, gs_], scalar=qscale, in1=rn,
                                                                                               op0=ALU.mult, op1=ALU.mult),
                         reads=[R_yv, R_rn], writes=[Rd])
                    yield
            for n in range(NT):
                ts_ = slice(n * 128, (n + 1) * 128)
                P.op(T, lambda e, ts_=ts_: e.transpose(out=ptr[:, 0:128], in_=hb["kT"][:, ts_], identity=ident_bf[:]), reads=[hb["R_k"], R_const], writes=[rptr])
                P.op(T, lambda e, ts_=ts_: e.transpose(out=ptr[:, 128:256], in_=hb["vT"][:, ts_], identity=ident_bf[:]), reads=[hb["R_v"], R_const], writes=[rptr])
                P.op(S, lambda e, n=n: e.activation(out=hb["kbg"][:, n, :], in_=ptr[:, 0:128], func=AF.Copy, scale=C1[:, n, h:h + 1]),
                     reads=[rptr, R_sm], writes=[hb["R_tok"]])
                P.op(V, lambda e, n=n: e.tensor_scalar(out=hb["ksc"][:, n, :], in0=ptr[:, 0:128], scalar1=C2[:, n, h:h + 1], scalar2=None, op0=ALU.mult),
                     reads=[rptr, R_sm], writes=[hb["R_tok"]])
                P.op(S, lambda e, n=n: e.activation(out=hb["vb"][:, n, :], in_=ptr[:, 128:256], func=AF.Copy, scale=BETA[:, n, h:h + 1]),
                     reads=[rptr, R_sm], writes=[hb["R_tok"]])
                yield

        R_dAh = [Res(), Res()]
        R_dBh = [Res(), Res()]
        R_dCh = [Res(), Res()]
        R_dDh = [Res(), Res()]
        R_Yh = [[Res(), Res()], [Res(), Res()]]
        R_Zh = [[Res(), Res()], [Res(), Res()]]
        R_Nh = [Res(), Res()]

        def dn_pre(h, jb, hf):
            hb = HB[h % 2]
            pbf = PB[(h * NBt + jb) % 2]
            BY, BZ, BN = HBK[hf]
            t0_ = hf * 2
            n0 = jb * 4 + t0_
            tv = slice(t0_, t0_ + 2)
            sl = slice(n0 * 128, (n0 + 2) * 128)
            gcb = GC[:, n0:n0 + 2, h:h + 1].to_broadcast([128, 2, 128])
            hcb = HC[:, n0:n0 + 2, h:h + 1].to_broadcast([128, 2, 128])
            idb2 = ident_f[:].unsqueeze(1).to_broadcast([128, 2, 128])
            nsl2 = negSL[:].unsqueeze(1).to_broadcast([128, 2, 128])
            ui2 = UI[:].unsqueeze(1).to_broadcast([128, 2, 128])
            kT, qT = hb["kT"], hb["qT"]
            Rpk = [pbf["R_TT"], pbf["R_nw"], pbf["R_qs"], pbf["R_QK"]]
            cols = slice(t0_ * 128, (t0_ + 2) * 128)

            def bv(bank):
                return pb[bank][:, cols].rearrange("p (i t) -> p i t", i=2)

            def mm2(bank, lhs_fn, rhs_fn, reads):
                for i in range(2):
                    P.op(T, lambda e, i=i: e.matmul(pb[bank][:, (t0_ + i) * 128:(t0_ + i + 1) * 128], lhsT=lhs_fn(i), rhs=rhs_fn(i), start=True, stop=True),
                         reads=reads, writes=[rpb[bank]])
            dAv, dBv, dCv, dDv, Nv = dA[:, tv, :], dB[:, tv, :], dC[:, tv, :], dD[:, tv, :], Nm[:, tv, :]
            Yv = [Yb[0][:, tv, :], Yb[1][:, tv, :]]
            Zv = [Zb[0][:, tv, :], Zb[1][:, tv, :]]
            RA, RB, RC, RD, RN = R_dAh[hf], R_dBh[hf], R_dCh[hf], R_dDh[hf], R_Nh[hf]
            RY, RZ = R_Yh[hf], R_Zh[hf]
            P.op(V, lambda e: e.tensor_tensor(out=dAv, in0=idb2, in1=gcb, op=ALU.mult), reads=[R_const, R_sm], writes=[RA])
            yield
            mm2(BN, lambda i: ones_f[:], lambda i: dAv[:, i, :], [RA, R_const])
            P.op(V, lambda e: e.tensor_tensor(out=dBv, in0=bv(BN), in1=hcb, op=ALU.subtract), reads=[rpb[BN], R_sm], writes=[RB])
            P.op(V, lambda e: e.tensor_tensor(out=dCv, in0=bv(BN), in1=gcb, op=ALU.subtract), reads=[rpb[BN], R_sm], writes=[RC])
            yield
            P.op(S, lambda e: e.activation(out=dBv, in_=dBv, func=AF.Exp, scale=-1.0), reads=[RB], writes=[RB])
            yield
            P.op(S, lambda e: e.activation(out=dCv, in_=dCv, func=AF.Exp), reads=[RC], writes=[RC])
            P.op(S, lambda e: e.activation(out=dDv, in_=bv(BN), func=AF.Exp), reads=[rpb[BN]], writes=[RD])
            yield
            P.op(V, lambda e: e.scalar_tensor_tensor(out=dBv, in0=dBv, scalar=1.0, in1=nsl2, op0=ALU.min, op1=ALU.mult), reads=[RB, R_const], writes=[RB])
            yield
            P.op(V, lambda e: e.scalar_tensor_tensor(out=dCv, in0=dCv, scalar=1.0, in1=ui2, op0=ALU.min, op1=ALU.mult), reads=[RC, R_const], writes=[RC])
            yield
            P.op(G, lambda e: e.tensor_tensor(out=pbf["qsT"][:, tv, :], in0=qT[:, sl].rearrange("p (i t) -> p i t", i=2), in1=dDv, op=ALU.mult),
                 reads=[hb["R_q"], RD], writes=[pbf["R_qs"]])
            yield
            mm2(BN, lambda i: kT[:, (n0 + i) * 128:(n0 + i + 1) * 128], lambda i: kT[:, (n0 + i) * 128:(n0 + i + 1) * 128], [hb["R_k"]])
            P.op(V, lambda e: e.tensor_tensor(out=Yv[0], in0=bv(BN), in1=dBv, op=ALU.mult), reads=[rpb[BN], RB], writes=[RY[0]])
            yield
            mm2(BN, lambda i: kT[:, (n0 + i) * 128:(n0 + i + 1) * 128], lambda i: qT[:, (n0 + i) * 128:(n0 + i + 1) * 128], [hb["R_k"], hb["R_q"]])
            P.op(V, lambda e: e.tensor_tensor(out=pbf["QKm"][:, tv, :], in0=bv(BN), in1=dCv, op=ALU.mult), reads=[rpb[BN], RC], writes=[pbf["R_QK"]])
            yield
            for i in range(2):
                P.op(T, lambda e, i=i: e.transpose(out=pb[BZ][:, (t0_ + i) * 128:(t0_ + i + 1) * 128], in_=Yv[0][:, i, :], identity=ident_f[:]),
                     reads=[RY[0], R_const], writes=[rpb[BZ]])
            yield
            P.op(S, lambda e: e.activation(out=Zv[0], in_=bv(BZ), func=AF.Copy), reads=[rpb[BZ]], writes=[RZ[0]])
            yield
            P.op(V, lambda e: e.tensor_tensor(out=Nv, in0=Zv[0], in1=idb2, op=ALU.add), reads=[RZ[0], R_const], writes=[RN])
            yield
            cur = 0
            for lv in range(1, 7):
                nx = 1 - cur
                mm2(BY, lambda i, cur=cur: Zv[cur][:, i, :], lambda i, cur=cur: Yv[cur][:, i, :], [RZ[cur], RY[cur]])
                yield
                if lv <= 5:
                    mm2(BZ, lambda i, cur=cur: Yv[cur][:, i, :], lambda i, cur=cur: Zv[cur][:, i, :], [RZ[cur], RY[cur]])
                    yield
                P.op(S, lambda e, nx=nx: e.activation(out=Yv[nx], in_=bv(BY), func=AF.Copy), reads=[rpb[BY]], writes=[RY[nx]])
                yield
                if lv <= 5:
                    P.op(V, lambda e, nx=nx: e.tensor_copy(out=Zv[nx], in_=bv(BZ)), reads=[rpb[BZ]], writes=[RZ[nx]])
                    yield
                mm2(BN, lambda i, nx=nx: Yv[nx][:, i, :], lambda i: Nv[:, i, :], [RY[nx], RN])
                yield
                P.op(V, lambda e: e.tensor_tensor(out=Nv, in0=Nv, in1=bv(BN), op=ALU.add), reads=[RN, rpb[BN]], writes=[RN])
                yield
                cur = nx
            P.op(S, lambda e: e.activation(out=pbf["TTb"][:, tv, :], in_=Nv, func=AF.Copy), reads=[RN], writes=[pbf["R_TT"]])
            yield
            mm2(BN, lambda i: hb["kbg"][:, n0 + i, :], lambda i: pbf["TTb"][:, t0_ + i, :], [hb["R_tok"], pbf["R_TT"]])
            yield
            P.op(S, lambda e: e.activation(out=pbf["nwT"][:, tv, :], in_=bv(BN), func=AF.Copy, scale=-1.0), reads=[rpb[BN]], writes=[pbf["R_nw"]])
            yield

        def dn_scan(h, jb):
            hb = HB[h % 2]
            pbf = PB[(h * NBt + jb) % 2]
            n0 = jb * 4
            VN = ptrf[:, 256:384]
            OO = ptrf[:, 384:512]
            DS = ptrf[:, 256:384]
            rS = rptr
            if jb == 0:
                P.op(V, lambda e: e.memset(Sf, 0.0), writes=[R_Sf])
                P.op(V, lambda e: e.memset(Sb, 0.0), writes=[R_Sb])
                yield
            for i in range(4):
                n = n0 + i
                P.op(T, lambda e, i=i, n=n: e.matmul(VN, lhsT=pbf["TTb"][:, i, :], rhs=hb["vb"][:, n, :], start=True, stop=False), reads=[pbf["R_TT"], hb["R_tok"]], writes=[rS])
                P.op(T, lambda e, i=i: e.matmul(VN, lhsT=pbf["nwT"][:, i, :], rhs=Sb, start=False, stop=True), reads=[pbf["R_nw"], R_Sb], writes=[rS])
                yield
                P.op(S, lambda e: e.activation(out=vnb, in_=VN, func=AF.Copy), reads=[rS], writes=[R_vn])
                yield
                P.op(T, lambda e, i=i: e.matmul(OO, lhsT=pbf["qsT"][:, i, :], rhs=Sb, start=True, stop=False), reads=[pbf["R_qs"], R_Sb], writes=[rS])
                P.op(T, lambda e, i=i: e.matmul(OO, lhsT=pbf["QKm"][:, i, :], rhs=vnb, start=False, stop=True), reads=[pbf["R_QK"], R_vn], writes=[rS])
                P.op(T, lambda e, n=n: e.matmul(DS, lhsT=hb["ksc"][:, n, :], rhs=vnb, start=True, stop=True), reads=[hb["R_tok"], R_vn], writes=[rS])
                yield
                P.op(V, lambda e, n=n: e.scalar_tensor_tensor(out=Sf, in0=Sf, scalar=DEC[:, n, h:h + 1], in1=DS, op0=ALU.mult, op1=ALU.add),
                     reads=[R_Sf, R_sm, rS], writes=[R_Sf])
                yield
                P.op(S, lambda e: e.activation(out=Sb, in_=Sf, func=AF.Copy), reads=[R_Sf], writes=[R_Sb])
                yield
                P.op(S, lambda e: e.activation(out=jk, in_=OO, func=AF.Square, accum_out=fs[:, 0:1]), reads=[rS], writes=[R_jk, R_fs])
                yield
                P.op(S, lambda e: e.activation(out=fs[:, 1:2], in_=fs[:, 0:1], func=AF.Ln, scale=1.0 / 128, bias=EPSC), reads=[R_fs, R_const], writes=[R_fs])
                yield
                P.op(S, lambda e: e.activation(out=fs[:, 2:3], in_=fs[:, 1:2], func=AF.Exp, scale=-0.5), reads=[R_fs], writes=[R_fs])
                yield
                P.op(V, lambda e, n=n: e.scalar_tensor_tensor(out=mxd, in0=OO, scalar=fs[:, 2:3], in1=siluz[:, n, h * 128:(h + 1) * 128],
                                                             op0=ALU.mult, op1=ALU.mult), reads=[rS, R_fs, R_sz[n]], writes=[R_mxd])
                yield
                P.op(T, lambda e: e.transpose(out=ptr[:, 256:384], in_=mxd, identity=ident_bf[:]), reads=[R_mxd, R_const], writes=[rptr])
                yield
                P.op(V, lambda e, n=n: e.tensor_copy(out=mixT[:, h, n * 128:(n + 1) * 128], in_=ptr[:, 256:384]), reads=[rptr], writes=[R_mixT[h][n]])
                yield

        run_rr([dn_prep(0)])
        run_rr([dn_pre(0, 0, 0), dn_pre(0, 0, 1)])
        for h in range(4):
            bg[0] = dn_prep(h + 1) if h + 1 < 4 else None
            for jb in range(NBt):
                if jb + 1 < NBt:
                    nxt = (dn_pre(h, jb + 1, 0), dn_pre(h, jb + 1, 1))
                elif h + 1 < 4:
                    while bg[0] is not None:
                        bg_step()
                    nxt = (dn_pre(h + 1, 0, 0), dn_pre(h + 1, 0, 1))
                else:
                    nxt = (None, None)
                run_rr([nxt[0], dn_scan(h, jb), nxt[1]], use_bg=True)
            while bg[0] is not None:
                bg_step()

        if dbg:
            P.barrier()
            main.reset(base_off)
            dbo = main.alloc([8 * L], F32)
            R_dbo = Res()
            P.op(V, lambda e: e.tensor_copy(out=dbo, in_=mixT.rearrange("p a b -> p (a b)")), writes=[R_dbo])
            P.dma("sync", lambda e: e.dma_start(out=dbg_d[:, :], in_=dbo), reads=[R_dbo], is_output=True)
            P.barrier()

        P.barrier()
        if L >= 2048:
            main.reset(0)
            wu = [main.alloc([8, 1024], BF16) for _ in range(2)]
            main.reset(base_off0)
        else:
            main.reset(base_off0)
            wu = [main.alloc([8, 1024], BF16) for _ in range(2)]
        wo = main.alloc([8, D], BF16)
        R_wo = Res()
        P.dma("sync", lambda e: e.dma_start(out=wo, in_=wout_s[:, :, :]), reads=[R_wrest], writes=[R_wo])
        x1b = [main.alloc([4, D], F32) for _ in range(2)]
        R_x1b = [[Res() for _ in range(4)] for _ in range(2)]
        hmTb = [main.alloc([8, 512], BF16) for _ in range(2)]
        R_hmb2 = [[Res() for _ in range(4)] for _ in range(2)]
        hT = main.alloc([8, 512], BF16)
        R_hT = [Res() for _ in range(8)]
        wd = [main.alloc([8, 1024], BF16) for _ in range(2)]
        R_wu = [Res(), Res()]
        R_wd = [Res(), Res()]
        xr = [main.alloc([D], F32) for _ in range(2)]
        R_xr = [Res(), Res()]
        xo = [main.alloc([D], F32) for _ in range(2)]
        R_xo = [Res(), Res()]
        hmb = main.alloc([D], BF16)
        R_hmb = Res()
        rl = [main.alloc([512], BF16) for _ in range(2)]
        R_rl = [Res(), Res()]
        mstb = [main.alloc([4, 8], F32) for _ in range(2)]
        R_mstb = [Res(), Res()]
        jkp = main.alloc([D], BF16)
        jke = main.alloc([D], BF16)
        R_jkp, R_jke = Res(), Res()
        NBLK = L // 512
        wq_cnt = [0]

        def mlp_pro(blk):
            pb_ = blk % 2
            x1, hmT, mst, R_x1, R_hm, R_mst = x1b[pb_], hmTb[pb_], mstb[pb_], R_x1b[pb_], R_hmb2[pb_], R_mstb[pb_]
            for i in range(4):
                n = blk * 4 + i
                ts_ = slice(n * 128, (n + 1) * 128)
                xi = n % 2
                P.dma("scalar", lambda e, xi=xi, ts_=ts_, b=b: e.dma_start(out=xr[xi], in_=x_d[b, ts_, :]), writes=[R_xr[xi]])
                for c2 in range(2):
                    for k in range(8):
                        P.op(T, lambda e, k=k, c2=c2, ts_=ts_: e.matmul(pb[c2][:, :], lhsT=mixT[:, k, ts_], rhs=wo[:, k, c2 * 512:(c2 + 1) * 512],
                                                                       start=(k == 0), stop=(k == 7)),
                             reads=[R_mixT[k][n], R_wo], writes=[rpb[c2]])
                    yield
                    P.op(V, lambda e, c2=c2, i=i, xi=xi: e.tensor_tensor(out=x1[:, i, c2 * 512:(c2 + 1) * 512], in0=pb[c2][:, :],
                                                                          in1=xr[xi][:, c2 * 512:(c2 + 1) * 512], op=ALU.add),
                         reads=[rpb[c2], R_xr[xi]], writes=[R_x1[i]])
                    yield
                P.op(S, lambda e, i=i: e.activation(out=jkp, in_=x1[:, i, :], func=AF.Square, accum_out=mst[:, i, 0:1]), reads=[R_x1[i]], writes=[R_jkp, R_mst])
                yield
                P.op(S, lambda e, i=i: e.activation(out=mst[:, i, 1:2], in_=mst[:, i, 0:1], func=AF.Ln, scale=1.0 / D, bias=EPSC), reads=[R_mst, R_const], writes=[R_mst])
                yield
                P.op(S, lambda e, i=i: e.activation(out=mst[:, i, 2:3], in_=mst[:, i, 1:2], func=AF.Exp, scale=-0.5), reads=[R_mst], writes=[R_mst])
                yield
                P.op(V, lambda e, i=i: e.tensor_scalar(out=hmb, in0=x1[:, i, :], scalar1=mst[:, i, 2:3], scalar2=None, op0=ALU.mult),
                     reads=[R_x1[i], R_mst], writes=[R_hmb])
                yield
                for k in range(8):
                    P.op(T, lambda e, k=k: e.transpose(out=ptr[:, k * 128:(k + 1) * 128], in_=hmb[:, k * 128:(k + 1) * 128], identity=ident_bf[:]),
                         reads=[R_hmb, R_const], writes=[rptr])
                P.op(S, lambda e, i=i: e.activation(out=hmT[:, :, i * 128:(i + 1) * 128], in_=ptr[:].rearrange("p (k t) -> p k t", k=8), func=AF.Copy),
                     reads=[rptr], writes=[R_hm[i]])
                yield

        def mlp_main(blk):
            pb_ = blk % 2
            x1, hmT, R_x1, R_hm = x1b[pb_], hmTb[pb_], R_x1b[pb_], R_hmb2[pb_]
            for fq in range(4):
                wi = wq_cnt[0] % 2
                wq_cnt[0] += 1
                P.dma("sync", lambda e, wi=wi, fq=fq: e.dma_start(out=wu[wi], in_=wup_s[:, :, fq * 1024:(fq + 1) * 1024]), reads=[R_wrest], writes=[R_wu[wi]])
                P.dma("sync", lambda e, wi=wi, fq=fq: e.dma_start(out=wd[wi], in_=wdn_s[:, fq * 8:(fq + 1) * 8, :]), reads=[R_wrest], writes=[R_wd[wi]])
                for fc in range(8):
                    pz = 2 + fc % 2
                    for k in range(8):
                        P.op(T, lambda e, k=k, fc=fc, wi=wi, pz=pz: e.matmul(pb[pz][:, :], lhsT=wu[wi][:, k, fc * 128:(fc + 1) * 128], rhs=hmT[:, k, :],
                                                                          start=(k == 0), stop=(k == 7)),
                             reads=[R_wu[wi]] + R_hm, writes=[rpb[pz]])
                    ri = fc % 2
                    P.op(S, lambda e, pz=pz, ri=ri: e.activation(out=rl[ri], in_=pb[pz][:, :], func=AF.Relu), reads=[rpb[pz]], writes=[R_rl[ri]])
                    P.op(G, lambda e, fc=fc, ri=ri: e.tensor_tensor(out=hT[:, fc, :], in0=rl[ri], in1=rl[ri], op=ALU.mult), reads=[R_rl[ri]], writes=[R_hT[fc]])
                    yield
                for i in range(4):
                    for c2 in range(2):
                        pz = 4 + c2
                        for fc in range(8):
                            P.op(T, lambda e, fc=fc, i=i, c2=c2, wi=wi, pz=pz: e.matmul(pb[pz][:, :], lhsT=hT[:, fc, i * 128:(i + 1) * 128],
                                                                                      rhs=wd[wi][:, fc, c2 * 512:(c2 + 1) * 512],
                                                                                      start=(fc == 0), stop=(fc == 7)),
                                 reads=[R_hT[fc], R_wd[wi]], writes=[rpb[pz]])
                        P.op(V, lambda e, i=i, c2=c2, pz=pz: e.tensor_tensor(out=x1[:, i, c2 * 512:(c2 + 1) * 512], in0=x1[:, i, c2 * 512:(c2 + 1) * 512],
                                                                           in1=pb[pz][:, :], op=ALU.add),
                             reads=[R_x1[i], rpb[pz]], writes=[R_x1[i]])
                        yield

        def mlp_epi(blk):
            pb_ = blk % 2
            x1, mst, R_x1, R_mst = x1b[pb_], mstb[pb_], R_x1b[pb_], R_mstb[pb_]
            for i in range(4):
                n = blk * 4 + i
                xi = n % 2
                P.op(S, lambda e, i=i: e.activation(out=jke, in_=x1[:, i, :], func=AF.Square, accum_out=mst[:, i, 3:4]), reads=[R_x1[i]], writes=[R_jke, R_mst])
                yield
                P.op(S, lambda e, i=i: e.activation(out=mst[:, i, 4:5], in_=mst[:, i, 3:4], func=AF.Ln, scale=1.0 / D, bias=EPSC), reads=[R_mst, R_const], writes=[R_mst])
                yield
                P.op(S, lambda e, i=i: e.activation(out=mst[:, i, 5:6], in_=mst[:, i, 4:5], func=AF.Exp, scale=-0.5), reads=[R_mst], writes=[R_mst])
                yield
                P.op(V, lambda e, i=i, xi=xi: e.scalar_tensor_tensor(out=xo[xi], in0=x1[:, i, :], scalar=mst[:, i, 5:6], in1=fnw_bc[:], op0=ALU.mult, op1=ALU.mult),
                     reads=[R_x1[i], R_mst, R_const], writes=[R_xo[xi]])
                P.dma("scalar", lambda e, xi=xi, n=n, b=b: e.dma_start(out=y_d[b, n * 128:(n + 1) * 128, :], in_=xo[xi]), reads=[R_xo[xi]], is_output=True)
                yield

        def seq_gens(*gens):
            for g in gens:
                if g is not None:
                    yield from g

        run_rr([mlp_pro(0)])
        for blk in range(NBLK):
            side = seq_gens(mlp_epi(blk - 1) if blk >= 1 else None, mlp_pro(blk + 1) if blk + 1 < NBLK else None)
            run_rr([side, mlp_main(blk)])
        run_rr([mlp_epi(NBLK - 1)])
        P.barrier()

    P.finish()
    P.emit()
    P.close()
    return nc


_NC_CACHE = {}


def kernel(x, positions, attn_norm_w, w_in, conv_w, a_log, dt_bias, dn_norm_w, lambda_q1, lambda_k1,
           lambda_q2, lambda_k2, diff_norm_w, group_scale, w_out, mlp_norm_w, w_up, w_down, final_norm_w):
    x = np.asarray(x, dtype=np.float32)
    B, L, _ = x.shape
    NB = B // NCORES
    key = (NB, L)
    if key not in _NC_CACHE:
        _NC_CACHE[key] = build(NB, L)
    nc = _NC_CACHE[key]

    def f(a):
        return np.ascontiguousarray(np.asarray(a, dtype=np.float32))
    shared = {
        "positions": np.ascontiguousarray(np.asarray(positions, dtype=np.int32)),
        "attn_norm_w": f(np.asarray(attn_norm_w)[0]), "w_in": f(np.asarray(w_in)[0]), "conv_w": f(np.asarray(conv_w)[0]),
        "a_log": f(np.asarray(a_log)[0]), "dt_bias": f(np.asarray(dt_bias)[0]), "dn_norm_w": f(np.asarray(dn_norm_w)[0]),
        "lambda_q1": f(np.asarray(lambda_q1)[0]), "lambda_k1": f(np.asarray(lambda_k1)[0]),
        "lambda_q2": f(np.asarray(lambda_q2)[0]), "lambda_k2": f(np.asarray(lambda_k2)[0]),
        "diff_norm_w": f(np.asarray(diff_norm_w)[0]), "group_scale": f(np.asarray(group_scale)[0]),
        "w_out": f(np.asarray(w_out)[0]), "mlp_norm_w": f(np.asarray(mlp_norm_w)[0]), "w_up": f(np.asarray(w_up)[0]),
        "w_down": f(np.asarray(w_down)[0]), "final_norm_w": f(final_norm_w),
    }
    in_maps = []
    for c in range(NCORES):
        m = dict(shared)
        m["x"] = np.ascontiguousarray(x[c * NB:(c + 1) * NB])
        in_maps.append(m)
    res = run_bass_kernel_spmd(nc, in_maps, core_ids=list(range(NCORES)))
    return np.concatenate([np.asarray(r["y"]) for r in res.results], axis=0).astype(np.float32)
```

```python
import math
import os
import numpy as np
from contextlib import ExitStack
import concourse.bass as bass
import concourse.mybir as mybir
from concourse.bass_utils import run_bass_kernel_spmd

F32 = mybir.dt.float32
BF16 = mybir.dt.bfloat16
I32 = mybir.dt.int32
AF = mybir.ActivationFunctionType
ALU = mybir.AluOpType

D = 1024
DIN = 3592
DFF = 4096
NCORES = 8
EPS = 1e-6
ROPE_THETA = 500000.0
LAMBDA_INIT = 0.8 - 0.6 * math.exp(-0.3 * 0)

EPOCH = 30000
ENGS = ("sync", "scalar", "vector", "gpsimd", "tensor")
COMPUTE = ("scalar", "vector", "gpsimd", "tensor")


class Res:
    __slots__ = ("w", "r", "excl")

    def __init__(self, excl=False):
        self.w = None
        self.r = []
        self.excl = excl


class Prog:
    def __init__(self, nc, n_dma_sems=(("sync", 24), ("scalar", 4), ("gpsimd", 8))):
        self.nc = nc
        self.ops = {e: [] for e in ENGS}
        self.cnt = {e: 0 for e in COMPUTE}
        self.waited = {e: {} for e in ENGS}
        self.dma_pool = {q: n for q, n in n_dma_sems}
        self.dma_uses = {q: [0] * n for q, n in n_dma_sems}
        self.dma_rr = {q: 0 for q, n in n_dma_sems}
        self.semkeys = set()
        self.stack = ExitStack()
        self.out_tokens = []
        self.last_tok = {}

    def sbuf(self, name, shape, dtype):
        return self.stack.enter_context(self.nc.sbuf_tensor(name, list(shape), dtype))

    def psum(self, name, shape, dtype=F32):
        return self.stack.enter_context(self.nc.psum_tensor(name, list(shape), dtype))

    def _need(self, eng, deps):
        best = {}
        for d in deps:
            if d is None:
                continue
            k, v = d
            if best.get(k, 0) < v:
                best[k] = v
        waits = []
        wd = self.waited[eng]
        for k, v in best.items():
            if eng == "tensor" and k[0] == "tensor":
                continue
            if wd.get(k, 0) >= v:
                continue
            wd[k] = v
            waits.append((k, v))
        return waits

    def _deps(self, reads, writes):
        deps = []
        for r in reads:
            deps.append(r.w)
            if r.excl:
                deps.extend(r.r)
        for w in writes:
            deps.append(w.w)
            deps.extend(w.r)
        return deps

    def _mark(self, tok, reads, writes):
        for r in reads:
            if r.excl:
                r.w = tok
                r.r = []
            else:
                r.r.append(tok)
        for w in writes:
            w.w = tok
            w.r = []
        self.last_tok[tok[0]] = tok

    def op(self, eng, fn, reads=(), writes=()):
        waits = self._need(eng, self._deps(reads, writes))
        self.cnt[eng] += 1
        c = self.cnt[eng]
        key = (eng, (c - 1) // EPOCH)
        tok = (key, (c - 1) % EPOCH + 1)
        self.semkeys.add(key)
        self._mark(tok, reads, writes)
        self.ops[eng].append((fn, waits, key, 1))
        return tok

    def dma(self, q, fn, reads=(), writes=(), is_output=False):
        deps = self._deps(reads, writes)
        i = self.dma_rr[q]
        self.dma_rr[q] = (i + 1) % self.dma_pool[q]
        key = ("dma_" + q, i)
        self.semkeys.add(key)
        prev = self.dma_uses[q][i]
        if prev > 0:
            deps.append((key, 16 * prev))
        self.dma_uses[q][i] = prev + 1
        tok = (key, 16 * (prev + 1))
        waits = self._need(q, deps)
        self._mark(tok, reads, writes)
        self.ops[q].append((fn, waits, key, 16))
        if is_output:
            self.out_tokens.append(tok)
        return tok

    def barrier(self):
        toks = list(self.last_tok.values())
        for e in ENGS:
            waits = self._need(e, toks)
            if waits:
                self.ops[e].append((None, waits, None, 0))

    def finish(self, q="sync"):
        waits = self._need(q, self.out_tokens)
        self.ops[q].append((None, waits, None, 0))

    def emit(self):
        nc = self.nc
        sems = {}
        for k in sorted(self.semkeys, key=str):
            sems[k] = self.stack.enter_context(nc.semaphore("s_%s_%d" % (k[0], k[1])))
        with nc.Block() as block:
            def make(engname):
                def body(e):
                    for fn, waits, key, inc in self.ops[engname]:
                        for (k, v) in waits:
                            e.wait_ge(sems[k], v)
                        if fn is not None:
                            fn(e).then_inc(sems[key], inc)
                return body
            block.sync(make("sync"))
            block.scalar(make("scalar"))
            block.vector(make("vector"))
            block.gpsimd(make("gpsimd"))
            block.tensor(make("tensor"))

    def close(self):
        self.stack.close()


class Arena:
    def __init__(self, P, name, kib):
        self.n = kib * 256
        self.t = P.sbuf(name, [128, self.n], F32)
        self.off = 0

    def reset(self, off=0):
        self.off = off

    def alloc(self, shape, dtype):
        n = 1
        for s in shape:
            n *= s
        nb = n * (2 if dtype == BF16 else 4)
        nw = (nb + 3) // 4
        nw = (nw + 7) // 8 * 8
        assert self.off + nw <= self.n, "arena overflow %d + %d > %d" % (self.off, nw, self.n)
        ap = self.t[:, self.off:self.off + nw]
        self.off += nw
        if dtype == BF16:
            ap = ap.bitcast(BF16)[:, 0:n]
        elif dtype == I32:
            ap = ap.bitcast(I32)[:, 0:n]
        else:
            ap = ap[:, 0:n]
        if len(shape) == 2:
            ap = ap.rearrange("p (a b) -> p a b", a=shape[0])
        elif len(shape) == 3:
            ap = ap.rearrange("p (a b c) -> p a b c", a=shape[0], b=shape[1])
        return ap


def build(NB, L, dbg=False, upto=None):
    NT = L // 128
    NG = L // 512
    assert L % 512 == 0
    nc = bass.Bass("TRN2", target_bir_lowering=False)
    x_d = nc.dram_tensor("x", [NB, L, D], F32, kind="ExternalInput").ap()
    pos_d = nc.dram_tensor("positions", [L], I32, kind="ExternalInput").ap()
    anw_d = nc.dram_tensor("attn_norm_w", [D], F32, kind="ExternalInput").ap()
    win_d = nc.dram_tensor("w_in", [D, DIN], F32, kind="ExternalInput").ap()
    convw_d = nc.dram_tensor("conv_w", [4, 1536], F32, kind="ExternalInput").ap()
    alog_d = nc.dram_tensor("a_log", [4], F32, kind="ExternalInput").ap()
    dtb_d = nc.dram_tensor("dt_bias", [4], F32, kind="ExternalInput").ap()
    dnw_d = nc.dram_tensor("dn_norm_w", [128], F32, kind="ExternalInput").ap()
    lq1_d = nc.dram_tensor("lambda_q1", [64], F32, kind="ExternalInput").ap()
    lk1_d = nc.dram_tensor("lambda_k1", [64], F32, kind="ExternalInput").ap()
    lq2_d = nc.dram_tensor("lambda_q2", [64], F32, kind="ExternalInput").ap()
    lk2_d = nc.dram_tensor("lambda_k2", [64], F32, kind="ExternalInput").ap()
    dfw_d = nc.dram_tensor("diff_norm_w", [128], F32, kind="ExternalInput").ap()
    gs_d = nc.dram_tensor("group_scale", [D], F32, kind="ExternalInput").ap()
    wout_d = nc.dram_tensor("w_out", [D, D], F32, kind="ExternalInput").ap()
    mnw_d = nc.dram_tensor("mlp_norm_w", [D], F32, kind="ExternalInput").ap()
    wup_d = nc.dram_tensor("w_up", [D, DFF], F32, kind="ExternalInput").ap()
    wdn_d = nc.dram_tensor("w_down", [DFF, D], F32, kind="ExternalInput").ap()
    fnw_d = nc.dram_tensor("final_norm_w", [D], F32, kind="ExternalInput").ap()
    y_d = nc.dram_tensor("y", [NB, L, D], F32, kind="ExternalOutput").ap()
    if dbg:
        dbg_d = nc.dram_tensor("dbg", [128, 8 * L], F32, kind="ExternalOutput").ap()
    win_s = nc.dram_tensor("win_s", [128, 8, DIN], BF16).ap()
    wout_s = nc.dram_tensor("wout_s", [128, 8, D], BF16).ap()
    wup_s = nc.dram_tensor("wup_s", [128, 8, DFF], BF16).ap()
    wdn_s = nc.dram_tensor("wdn_s", [128, 32, D], BF16).ap()

    P = Prog(nc)

    def early(tag):
        if upto != tag:
            return False
        P.barrier()
        zt = P.sbuf("zt_" + tag, [128, 64], F32)
        rz = Res()
        P.op("vector", lambda e: e.memset(zt[:], 1.0), writes=[rz])
        P.dma("sync", lambda e: e.dma_start(out=dbg_d[:, 0:64], in_=zt[:]), reads=[rz], is_output=True)
        P.finish(); P.emit(); P.close()
        return True
    V, S, G, T = "vector", "scalar", "gpsimd", "tensor"

    ident_bf = P.sbuf("ident_bf", [128, 128], BF16)
    ident_f = P.sbuf("ident_f", [128, 128], F32)
    ones_bf = P.sbuf("ones_bf", [128, 128], BF16)
    ones_f = P.sbuf("ones_f", [128, 128], F32)
    UI = P.sbuf("UI", [128, 128], F32)
    UI_bf = P.sbuf("UI_bf", [128, 128], BF16)
    negSL = P.sbuf("negSL", [128, 128], F32)
    cosT = P.sbuf("cosT", [128, NT, 8], F32)
    sinT = P.sbuf("sinT", [128, NT, 8], F32)
    cvec = P.sbuf("cvec", [128, 64], F32)
    convw = P.sbuf("convw", [128, 12, 4], F32)
    fnw_bc = P.sbuf("fnw_bc", [128, D], F32)
    R_const = Res()
    EPSC = cvec[:, 0:1]
    ONEC = cvec[:, 1:2]
    LAMN = cvec[:, 2:3]
    NEGA = cvec[:, 4:8]
    DTB = cvec[:, 8:12]
    EPSL2 = cvec[:, 12:13]

    main = Arena(P, "main", 199)
    hnT = P.sbuf("hnT", [128, 8, L], BF16) if False else None

    pb = [P.psum("pb%d" % i, [128, 512], F32) for i in range(7)]
    rpb = [Res(True) for _ in range(7)]
    ptr = P.psum("ptr", [128, 1024], BF16)
    rptr = Res(True)

    P.op(G, lambda e: e.memset(ident_f[:], 1.0), writes=[R_const])
    P.op(G, lambda e: e.affine_select(out=ident_f[:], in_=ident_f[:], pattern=[[1, 128]], compare_op=ALU.is_equal,
                                      fill=0.0, base=0, channel_multiplier=-1), reads=[R_const], writes=[R_const])
    P.op(V, lambda e: e.tensor_copy(out=ident_bf[:], in_=ident_f[:]), reads=[R_const], writes=[R_const])
    P.op(G, lambda e: e.memset(ones_f[:], 1.0), writes=[R_const])
    P.op(G, lambda e: e.memset(ones_bf[:], 1.0), writes=[R_const])
    P.op(G, lambda e: e.memset(UI[:], 1.0), writes=[R_const])
    P.op(G, lambda e: e.affine_select(out=UI[:], in_=UI[:], pattern=[[1, 128]], compare_op=ALU.is_ge,
                                      fill=0.0, base=0, channel_multiplier=-1), reads=[R_const], writes=[R_const])
    P.op(V, lambda e: e.tensor_copy(out=UI_bf[:], in_=UI[:]), reads=[R_const], writes=[R_const])
    P.op(G, lambda e: e.memset(negSL[:], -1.0), writes=[R_const])
    P.op(G, lambda e: e.affine_select(out=negSL[:], in_=negSL[:], pattern=[[-1, 128]], compare_op=ALU.is_gt,
                                      fill=0.0, base=0, channel_multiplier=1), reads=[R_const], writes=[R_const])
    P.op(V, lambda e: e.memset(cvec[:], 0.0), writes=[R_const])
    P.op(V, lambda e: e.memset(cvec[:, 0:1], EPS), reads=[R_const], writes=[R_const])
    P.op(V, lambda e: e.memset(cvec[:, 1:2], 1.0), reads=[R_const], writes=[R_const])
    P.op(V, lambda e: e.memset(cvec[:, 12:13], 1e-6), reads=[R_const], writes=[R_const])
    P.dma("sync", lambda e: e.dma_start(out=fnw_bc[:], in_=fnw_d.partition_broadcast(128)), writes=[R_const])
    P.dma("sync", lambda e: e.dma_start(out=cvec[:, 16:20], in_=alog_d.partition_broadcast(128)), reads=[R_const], writes=[R_const])
    P.dma("sync", lambda e: e.dma_start(out=cvec[:, 8:12], in_=dtb_d.partition_broadcast(128)), reads=[R_const], writes=[R_const])
    P.op(S, lambda e: e.activation(out=cvec[:, 20:24], in_=cvec[:, 16:20], func=AF.Exp), reads=[R_const], writes=[R_const])
    P.op(V, lambda e: e.tensor_scalar(out=cvec[:, 4:8], in0=cvec[:, 20:24], scalar1=-1.0, scalar2=None, op0=ALU.mult),
         reads=[R_const], writes=[R_const])

    if early('consts'):
        return nc
    main.reset()
    lt = main.alloc([4, 64], F32)
    R_s = Res()
    for i, d_ in enumerate((lq1_d, lk1_d, lq2_d, lk2_d)):
        P.dma("sync", lambda e, i=i, d_=d_: e.dma_start(out=lt[:, i, :], in_=d_.partition_broadcast(128)), reads=[R_s], writes=[R_s])
    lp = main.alloc([2, 64], F32)
    P.op(V, lambda e: e.tensor_tensor(out=lp[:, 0, :], in0=lt[:, 0, :], in1=lt[:, 1, :], op=ALU.mult), reads=[R_s], writes=[R_s])
    P.op(V, lambda e: e.tensor_tensor(out=lp[:, 1, :], in0=lt[:, 2, :], in1=lt[:, 3, :], op=ALU.mult), reads=[R_s], writes=[R_s])
    P.op(V, lambda e: e.reduce_sum(out=cvec[:, 24:25], in_=lp[:, 0, :], axis=mybir.AxisListType.X), reads=[R_s, R_const], writes=[R_const])
    P.op(V, lambda e: e.reduce_sum(out=cvec[:, 25:26], in_=lp[:, 1, :], axis=mybir.AxisListType.X), reads=[R_s, R_const], writes=[R_const])
    P.op(S, lambda e: e.activation(out=cvec[:, 26:28], in_=cvec[:, 24:26], func=AF.Exp), reads=[R_const], writes=[R_const])
    P.op(V, lambda e: e.tensor_tensor(out=cvec[:, 28:29], in0=cvec[:, 27:28], in1=cvec[:, 26:27], op=ALU.subtract), reads=[R_const], writes=[R_const])
    P.op(V, lambda e: e.tensor_scalar(out=cvec[:, 2:3], in0=cvec[:, 28:29], scalar1=-LAMBDA_INIT, scalar2=None, op0=ALU.add),
         reads=[R_const], writes=[R_const])
    cwr = main.alloc([1536], F32)
    P.dma("sync", lambda e: e.dma_start(out=cwr[0:4, :], in_=convw_d[:, :]), reads=[R_s], writes=[R_s])
    for c in range(12):
        P.op(T, lambda e, c=c: e.matmul(pb[0][:, c * 4:(c + 1) * 4], lhsT=cwr[0:4, c * 128:(c + 1) * 128], rhs=ident_f[0:4, 0:4],
                                         start=True, stop=True), reads=[R_s, R_const], writes=[rpb[0]])
    P.op(V, lambda e: e.tensor_copy(out=convw[:].rearrange("p a b -> p (a b)"), in_=pb[0][:, 0:48]), reads=[rpb[0]], writes=[R_const])
    posi = main.alloc([128], I32)
    posf = main.alloc([128], F32)
    P.dma("sync", lambda e: e.dma_start(out=posi[0:NT, :], in_=pos_d.rearrange("(n p) -> n p", p=128)), reads=[R_s], writes=[R_s])
    P.op(V, lambda e: e.tensor_copy(out=posf[0:NT, :], in_=posi[0:NT, :]), reads=[R_s], writes=[R_s])
    P.op(T, lambda e: e.matmul(pb[1][:, 0:NT], lhsT=posf[0:NT, :], rhs=ident_f[0:NT, 0:NT], start=True, stop=True),
         reads=[R_s, R_const], writes=[rpb[1]])
    post = main.alloc([NT], F32)
    P.op(V, lambda e: e.tensor_copy(out=post, in_=pb[1][:, 0:NT]), reads=[rpb[1]], writes=[R_s])
    invf = main.alloc([8], F32)
    for i in range(8):
        fr = float(np.float32(ROPE_THETA) ** np.float32(-(2.0 * i) / 16.0))
        P.op(V, lambda e, i=i, fr=fr: e.memset(invf[:, i:i + 1], fr), reads=[R_s], writes=[R_s])
    ang = main.alloc([NT, 8], F32)
    P.op(V, lambda e: e.tensor_tensor(out=ang, in0=post.unsqueeze(2).to_broadcast([128, NT, 8]),
                                      in1=invf.unsqueeze(1).to_broadcast([128, NT, 8]), op=ALU.mult), reads=[R_s], writes=[R_s])
    tA = main.alloc([NT, 8], F32)
    tB = main.alloc([NT, 8], F32)
    tI = main.alloc([NT, 8], I32)
    TWO_PI = 2.0 * math.pi
    for (shift, dst) in ((0.0, sinT), (math.pi / 2.0, cosT)):
        P.op(V, lambda e, shift=shift: e.tensor_scalar(out=tA, in0=ang, scalar1=shift, scalar2=None, op0=ALU.add), reads=[R_s], writes=[R_s])
        P.op(V, lambda e: e.tensor_scalar(out=tB, in0=tA, scalar1=1.0 / TWO_PI, scalar2=None, op0=ALU.mult), reads=[R_s], writes=[R_s])
        P.op(V, lambda e: e.tensor_copy(out=tI, in_=tB), reads=[R_s], writes=[R_s])
        P.op(V, lambda e: e.tensor_copy(out=tB, in_=tI), reads=[R_s], writes=[R_s])
        P.op(V, lambda e: e.scalar_tensor_tensor(out=tA, in0=tB, scalar=-TWO_PI, in1=tA, op0=ALU.mult, op1=ALU.add), reads=[R_s], writes=[R_s])
        P.op(V, lambda e: e.tensor_scalar(out=tB, in0=tA, scalar1=math.pi, scalar2=TWO_PI, op0=ALU.is_gt, op1=ALU.mult), reads=[R_s], writes=[R_s])
        P.op(V, lambda e: e.tensor_tensor(out=tA, in0=tA, in1=tB, op=ALU.subtract), reads=[R_s], writes=[R_s])
        P.op(V, lambda e: e.tensor_scalar(out=tB, in0=tA, scalar1=-math.pi, scalar2=TWO_PI, op0=ALU.is_lt, op1=ALU.mult), reads=[R_s], writes=[R_s])
        P.op(V, lambda e: e.tensor_tensor(out=tA, in0=tA, in1=tB, op=ALU.add), reads=[R_s], writes=[R_s])
        P.op(S, lambda e, dst=dst: e.activation(out=dst[:], in_=tA, func=AF.Sin), reads=[R_s, R_const], writes=[R_const])

    if early('setup'):
        return nc
    P.barrier()
    main.reset()
    rowv = P.sbuf("rowv", [128, 8, 4], F32)
    R_rv = Res()
    vraw = main.alloc([3, 128], F32)
    for wi, vd in enumerate((anw_d, mnw_d, gs_d)):
        P.dma("sync", lambda e, wi=wi, vd=vd: e.dma_start(out=vraw[0:8, wi, :], in_=vd.rearrange("(k p) -> k p", p=128)), reads=[R_rv], writes=[R_rv])
    for wi in range(3):
        P.op(T, lambda e, wi=wi: e.matmul(pb[0][:, wi * 8:(wi + 1) * 8], lhsT=vraw[0:8, wi, :], rhs=ident_f[0:8, 0:8], start=True, stop=True),
             reads=[R_rv, R_const], writes=[rpb[0]])
    P.op(V, lambda e: e.tensor_copy(out=rowv[:, :, 0:3], in_=pb[0][:, 0:24].rearrange("p (w k) -> p k w", w=3)), reads=[rpb[0]], writes=[R_rv])
    for k in range(8):
        src = dnw_d if k < 4 else dfw_d
        P.dma("sync", lambda e, k=k, src=src: e.dma_start(out=rowv[:, k, 3:4], in_=src.rearrange("(p o) -> p o", o=1)), reads=[R_rv], writes=[R_rv])
    P.op(V, lambda e: e.tensor_scalar(out=rowv[:, 4:8, 3:4], in0=rowv[:, 4:8, 3:4], scalar1=1.0 - LAMBDA_INIT, scalar2=None, op0=ALU.mult),
         reads=[R_rv], writes=[R_rv])
    P.op(V, lambda e: e.tensor_tensor(out=rowv[:, :, 2:3], in0=rowv[:, :, 2:3], in1=rowv[:, :, 3:4], op=ALU.mult), reads=[R_rv], writes=[R_rv])
    stg_f = [main.alloc([4096], F32) for _ in range(2)]
    stg_b = [main.alloc([4096], BF16) for _ in range(2)]
    R_sf = [Res(), Res()]
    R_sb = [Res(), Res()]
    R_win = Res()
    R_wrest = Res()
    jobs = []
    for k in range(8):
        jobs.append((win_d[k * 128:(k + 1) * 128, :], DIN, rowv[:, k, 0:1], win_s[:, k, :]))
    for j, (src, n, sc, dst) in enumerate(jobs):
        b = j % 2
        P.dma("sync" if j % 2 == 0 else "scalar", lambda e, b=b, src=src, n=n: e.dma_start(out=stg_f[b][:, 0:n], in_=src), writes=[R_sf[b]])
        if j % 2 == 0:
            P.op(S, lambda e, b=b, n=n, sc=sc: e.activation(out=stg_b[b][:, 0:n], in_=stg_f[b][:, 0:n], func=AF.Copy, scale=sc),
                 reads=[R_sf[b], R_rv], writes=[R_sb[b]])
        else:
            P.op(V, lambda e, b=b, n=n, sc=sc: e.tensor_scalar(out=stg_b[b][:, 0:n], in0=stg_f[b][:, 0:n], scalar1=sc, scalar2=None, op0=ALU.mult),
                 reads=[R_sf[b], R_rv], writes=[R_sb[b]])
        P.dma("gpsimd", lambda e, b=b, n=n, dst=dst: e.dma_start(out=dst, in_=stg_b[b][:, 0:n]), reads=[R_sb[b]], writes=[R_win])
    P.barrier()

    RST_W = 512
    rst_off = main.n - (2 * RST_W + 2 * RST_W // 2)
    rsf = [main.t[:, rst_off + i * RST_W:rst_off + (i + 1) * RST_W] for i in range(2)]
    rsb = [main.t[:, rst_off + 2 * RST_W + i * (RST_W // 2):rst_off + 2 * RST_W + (i + 1) * (RST_W // 2)].bitcast(BF16) for i in range(2)]
    R_rsf = [Res(), Res()]
    R_rsb = [Res(), Res()]
    rest_jobs = []
    for k in range(8):
        for c in range(D // RST_W):
            rest_jobs.append((wout_d[k * 128:(k + 1) * 128, c * RST_W:(c + 1) * RST_W], RST_W, rowv[:, k, 2:3], wout_s[:, k, c * RST_W:(c + 1) * RST_W]))
    for k in range(8):
        for c in range(DFF // RST_W):
            rest_jobs.append((wup_d[k * 128:(k + 1) * 128, c * RST_W:(c + 1) * RST_W], RST_W, rowv[:, k, 1:2], wup_s[:, k, c * RST_W:(c + 1) * RST_W]))
    for k in range(32):
        for c in range(D // RST_W):
            rest_jobs.append((wdn_d[k * 128:(k + 1) * 128, c * RST_W:(c + 1) * RST_W], RST_W, None, wdn_s[:, k, c * RST_W:(c + 1) * RST_W]))

    def prep_rest():
        for j, (src, n, sc, dst) in enumerate(rest_jobs):
            bb_ = j % 2
            P.dma("sync", lambda e, bb_=bb_, src=src, n=n: e.dma_start(out=rsf[bb_][:, 0:n], in_=src), writes=[R_rsf[bb_]])
            yield
            if sc is None:
                eng = G if j % 2 == 0 else V
                P.op(eng, lambda e, bb_=bb_, n=n: e.tensor_copy(out=rsb[bb_][:, 0:n], in_=rsf[bb_][:, 0:n]), reads=[R_rsf[bb_]], writes=[R_rsb[bb_]])
            else:
                P.op(V, lambda e, bb_=bb_, n=n, sc=sc: e.tensor_scalar(out=rsb[bb_][:, 0:n], in0=rsf[bb_][:, 0:n], scalar1=sc, scalar2=None, op0=ALU.mult),
                     reads=[R_rsf[bb_], R_rv], writes=[R_rsb[bb_]])
            yield
            P.dma("sync", lambda e, bb_=bb_, n=n, dst=dst: e.dma_start(out=dst, in_=rsb[bb_][:, 0:n]), reads=[R_rsb[bb_]], writes=[R_wrest])
            yield

    if early('wprep'):
        return nc
    def rstd_from_ss(ss, tmp, out, scale):
        P.op(S, lambda e: e.activation(out=tmp, in_=ss, func=AF.Ln, scale=scale, bias=EPSC), reads=[R_st, R_const], writes=[R_st])
        P.op(S, lambda e: e.activation(out=out, in_=tmp, func=AF.Exp, scale=-0.5), reads=[R_st], writes=[R_st])

    bg = [None]

    def bg_step():
        if bg[0] is not None:
            try:
                next(bg[0])
            except StopIteration:
                bg[0] = None

    bg2 = [None]

    def bg2_step():
        if bg2[0] is not None:
            try:
                next(bg2[0])
            except StopIteration:
                bg2[0] = None

    def run_rr(gens, use_bg=False):
        gens = [g for g in gens if g is not None]
        while gens:
            for g in list(gens):
                try:
                    next(g)
                except StopIteration:
                    gens.remove(g)
            if use_bg:
                bg_step()


    for b in range(NB):
        if b == 0:
            bg[0] = prep_rest()
        main.reset()
        hnT = main.alloc([8, L], BF16)
        mixT = main.alloc([8, L], BF16)
        R_hnT = [Res() for _ in range(NT)]
        R_mixT = [[Res() for _ in range(NT)] for _ in range(8)]
        base_off0 = main.off
        siluz = main.alloc([NT, 512], BF16)
        R_sz = [Res() for _ in range(NT)]
        ba = main.alloc([NT, 8], F32)
        R_ba = Res()
        sm = main.alloc([12, NT, 4], F32)
        R_sm = Res()
        base_off = main.off
        dfq0 = main.alloc([4, L], BF16)
        dfq1 = main.alloc([4, L], BF16)
        dfkT = main.alloc([4, L], BF16)
        dfv = main.alloc([NT, 4, 130], BF16)
        R_dfq = [Res() for _ in range(NT)]
        R_dfk = [Res() for _ in range(NT)]
        R_dfv = [Res() for _ in range(NT)]
        wblk = [main.alloc([8, 512], BF16) for _ in range(3)]
        R_wblk = [Res(), Res(), Res()]
        qtok = [main.alloc([512], BF16) for _ in range(2)]
        R_qtok = [Res(), Res()]
        rt = [main.alloc([8, 8], F32) for _ in range(4)]
        R_rt = Res()
        off_A = main.off
        xt = [main.alloc([D], F32) for _ in range(2)]
        R_xt = [Res(), Res()]
        hnb = [main.alloc([D], BF16) for _ in range(2)]
        R_hnb = [Res(), Res()]
        st = main.alloc([NT, 4], F32)
        R_st = Res()
        P.op(G, lambda e: e.memset(dfv[:, :, :, 128:130], 1.0), writes=R_dfv)
        P.op(G, lambda e: e.memset(dfq0[64:128], 0.0), writes=R_dfq)
        P.op(G, lambda e: e.memset(dfq1[0:64], 0.0), writes=R_dfq)
        COL_Q, COL_K, COL_V = 2056, 2568, 3080
        for bi, col0 in enumerate((COL_Q, COL_K, COL_V)):
            P.dma("sync", lambda e, bi=bi, col0=col0: e.dma_start(out=wblk[bi], in_=win_s[:, :, col0:col0 + 512]), reads=[R_win], writes=[R_wblk[bi]])

        def genA(n):
            i2 = n % 2
            P.dma("sync", lambda e, b=b: e.dma_start(out=xt[i2], in_=x_d[b, n * 128:(n + 1) * 128, :]), writes=[R_xt[i2]])
            P.op(S, lambda e: e.activation(out=hnb[i2], in_=xt[i2], func=AF.Square, accum_out=st[:, n, 0:1]), reads=[R_xt[i2]], writes=[R_hnb[i2], R_st])
            yield
            P.op(S, lambda e: e.activation(out=st[:, n, 1:2], in_=st[:, n, 0:1], func=AF.Ln, scale=1.0 / D, bias=EPSC), reads=[R_st, R_const], writes=[R_st])
            yield
            P.op(S, lambda e: e.activation(out=st[:, n, 2:3], in_=st[:, n, 1:2], func=AF.Exp, scale=-0.5), reads=[R_st], writes=[R_st])
            yield
            P.op(V, lambda e: e.tensor_scalar(out=hnb[i2], in0=xt[i2], scalar1=st[:, n, 2:3], scalar2=None, op0=ALU.mult),
                 reads=[R_xt[i2], R_st], writes=[R_hnb[i2]])
            yield
            for k in range(8):
                P.op(T, lambda e, k=k: e.transpose(out=ptr[:, k * 128:(k + 1) * 128], in_=hnb[i2][:, k * 128:(k + 1) * 128], identity=ident_bf[:]),
                     reads=[R_hnb[i2], R_const], writes=[rptr])
            P.op(S, lambda e: e.activation(out=hnT[:, :, n * 128:(n + 1) * 128], in_=ptr[:].rearrange("p (k t) -> p k t", k=8), func=AF.Copy),
                 reads=[rptr], writes=[R_hnT[n]])
            yield

        pcount = [0]

        def genDF(n):
            for bi, kind in enumerate(("q", "k", "v")):
                pbi = pcount[0] % 2
                pcount[0] += 1
                ps = pb[pbi]
                rps = rpb[pbi]
                for k in range(8):
                    P.op(T, lambda e, k=k, ps=ps, bi=bi: e.matmul(ps[:, :], lhsT=hnT[:, k, n * 128:(n + 1) * 128], rhs=wblk[bi][:, k, :],
                                                                  start=(k == 0), stop=(k == 7)),
                         reads=[R_hnT[n], R_wblk[bi]], writes=[rps])
                yield
                if kind == "v":
                    P.op(S, lambda e, ps=ps: e.activation(out=dfv[:, n, :, 0:128], in_=ps[:, :].rearrange("p (h d) -> p h d", h=4), func=AF.Copy),
                         reads=[rps], writes=[R_dfv[n]])
                    yield
                    continue
                qi = pcount[0] % 2
                qt = qtok[qi]
                Rq = R_qtok[qi]
                P.op(S, lambda e, ps=ps, qt=qt: e.activation(out=qt, in_=ps[:, :], func=AF.Copy), reads=[rps], writes=[Rq])
                yield
                ps3 = ps[:, :].rearrange("p (g d) -> p g d", g=8)
                qt3 = qt.rearrange("p (g d) -> p g d", g=8)
                cb = cosT[:, n, :].unsqueeze(1).to_broadcast([128, 8, 8])
                sb = sinT[:, n, :].unsqueeze(1).to_broadcast([128, 8, 8])
                x1 = ps3[:, :, 0:8]
                x2 = ps3[:, :, 8:16]
                P.op(V, lambda e, x1=x1, cb=cb: e.tensor_tensor(out=rt[0], in0=x1, in1=cb, op=ALU.mult), reads=[rps, R_const], writes=[R_rt])
                P.op(V, lambda e, x2=x2, sb=sb: e.tensor_tensor(out=rt[1], in0=x2, in1=sb, op=ALU.mult), reads=[rps, R_const], writes=[R_rt])
                yield
                P.op(V, lambda e, x2=x2, cb=cb: e.tensor_tensor(out=rt[2], in0=x2, in1=cb, op=ALU.mult), reads=[rps, R_const], writes=[R_rt])
                P.op(V, lambda e, x1=x1, sb=sb: e.tensor_tensor(out=rt[3], in0=x1, in1=sb, op=ALU.mult), reads=[rps, R_const], writes=[R_rt])
                yield
                P.op(V, lambda e, qt3=qt3: e.tensor_tensor(out=qt3[:, :, 0:8], in0=rt[0], in1=rt[1], op=ALU.subtract), reads=[R_rt], writes=[Rq])
                P.op(V, lambda e, qt3=qt3: e.tensor_tensor(out=qt3[:, :, 8:16], in0=rt[2], in1=rt[3], op=ALU.add), reads=[R_rt], writes=[Rq])
                yield
                for h in range(4):
                    P.op(T, lambda e, h=h, qt=qt: e.transpose(out=ptr[:, h * 128:(h + 1) * 128], in_=qt[:, h * 128:(h + 1) * 128], identity=ident_bf[:]),
                         reads=[Rq, R_const], writes=[rptr])
                if kind == "q":
                    P.op(V, lambda e: e.tensor_copy(out=dfq0[0:64, :, n * 128:(n + 1) * 128], in_=ptr[0:64, 0:512].rearrange("p (h t) -> p h t", h=4)),
                         reads=[rptr], writes=[R_dfq[n]])
                    P.op(V, lambda e: e.tensor_copy(out=dfq1[64:128, :, n * 128:(n + 1) * 128], in_=ptr[64:128, 0:512].rearrange("p (h t) -> p h t", h=4)),
                         reads=[rptr], writes=[R_dfq[n]])
                else:
                    P.op(V, lambda e: e.tensor_copy(out=dfkT[:, :, n * 128:(n + 1) * 128], in_=ptr[:, 0:512].rearrange("p (h t) -> p h t", h=4)),
                         reads=[rptr], writes=[R_dfk[n]])
                yield

        run_rr([genA(0)], use_bg=True)
        for n in range(NT):
            run_rr([genA(n + 1) if n + 1 < NT else None, genDF(n)], use_bg=True)
        assert main.off <= rst_off, (main.off, rst_off)
        main.reset(off_A)
        P.barrier()

        if early('DFproj'):
            return nc
        PT = [main.alloc([2, 256], BF16) for _ in range(2)]
        R_PT = [Res(), Res()]
        dtmp = main.alloc([128], F32)
        R_dt = Res()
        mxb = main.alloc([128], BF16)
        R_mxb = Res()
        fst = main.alloc([16], F32)
        R_fst = Res()
        stb = [pb[0], pb[1]]
        rstb = [rpb[0], rpb[1]]
        accp = [[pb[2], pb[3]], [pb[4], pb[5]]]
        raccp = [[rpb[2], rpb[3]], [rpb[4], rpb[5]]]
        dfq = [dfq0, dfq1]
        NQG = NT // 2
        iters = [(h, qg, kb) for h in range(4) for qg in range(NQG) for kb in range(2 * qg + 2)]

        def df_ST(j):
            h, qg, kb = iters[j]
            bufi = j % 2
            i0 = max(0, kb - 2 * qg)
            c0 = i0 * 128
            for m in range(2):
                P.op(T, lambda e, m=m: e.matmul(
                    stb[bufi][:, m * 256 + c0:(m + 1) * 256], lhsT=dfkT[:, h, kb * 128:(kb + 1) * 128],
                    rhs=dfq[m][:, h, qg * 256 + c0:(qg + 1) * 256], start=True, stop=True),
                    reads=[R_dfk[kb]] + R_dfq[qg * 2 + i0:qg * 2 + 2], writes=[rstb[bufi]])

        def df_EXP(j):
            h, qg, kb = iters[j]
            bufi = j % 2
            c0 = max(0, kb - 2 * qg) * 128
            P.op(S, lambda e: e.activation(out=PT[bufi][:, :, c0:256], in_=stb[bufi][:, :].rearrange("p (m q) -> p m q", m=2)[:, :, c0:256],
                                           func=AF.Exp, scale=0.125), reads=[rstb[bufi]], writes=[R_PT[bufi]])
            if kb >= 2 * qg:
                for m in range(2):
                    P.op(G, lambda e, m=m: e.tensor_tensor(out=PT[bufi][:, m, c0:c0 + 128], in0=PT[bufi][:, m, c0:c0 + 128], in1=UI_bf[:], op=ALU.mult),
                         reads=[R_PT[bufi], R_const], writes=[R_PT[bufi]])

        NRING = 4
        stg = [main.alloc([2, 130], F32) for _ in range(NRING)]
        dring = [main.alloc([128], F32) for _ in range(NRING)]
        mring = [main.alloc([128], BF16) for _ in range(NRING)]
        fring = [main.alloc([8], F32) for _ in range(NRING)]
        R_ring = [Res() for _ in range(NRING)]
        fin_cnt = [0]
        pending = []

        def df_FIN(h, n, i, j):
            r = fin_cnt[0] % NRING
            fin_cnt[0] += 1
            sg, dd, mm_, ff, Rr = stg[r], dring[r], mring[r], fring[r], R_ring[r]
            for m in range(2):
                P.op(V, lambda e, m=m: e.tensor_copy(out=sg[:, m, :], in_=accp[m][i][:, 0:130]), reads=[raccp[m][i]], writes=[Rr])

            def part2():
                P.op(V, lambda e: e.reciprocal(out=ff[:, 0:2], in_=sg[:, :, 128:129].rearrange("p m o -> p (m o)")), reads=[Rr], writes=[Rr])
                P.op(V, lambda e: e.tensor_tensor(out=ff[:, 2:3], in0=ff[:, 1:2], in1=LAMN, op=ALU.mult), reads=[Rr, R_const], writes=[Rr])
                P.op(V, lambda e: e.tensor_scalar(out=dd, in0=sg[:, 0, 0:128], scalar1=ff[:, 0:1], scalar2=None, op0=ALU.mult), reads=[Rr], writes=[Rr])
                P.op(V, lambda e: e.scalar_tensor_tensor(out=dd, in0=sg[:, 1, 0:128], scalar=ff[:, 2:3], in1=dd, op0=ALU.mult, op1=ALU.add), reads=[Rr], writes=[Rr])

            def part3():
                P.op(S, lambda e: e.activation(out=mm_, in_=dd, func=AF.Square, accum_out=ff[:, 3:4]), reads=[Rr], writes=[Rr])
                P.op(S, lambda e: e.activation(out=ff[:, 4:5], in_=ff[:, 3:4], func=AF.Ln, scale=1.0 / 128, bias=EPSC), reads=[Rr, R_const], writes=[Rr])
                P.op(S, lambda e: e.activation(out=ff[:, 5:6], in_=ff[:, 4:5], func=AF.Exp, scale=-0.5), reads=[Rr], writes=[Rr])

            def part4():
                P.op(V, lambda e: e.tensor_scalar(out=mm_, in0=dd, scalar1=ff[:, 5:6], scalar2=None, op0=ALU.mult), reads=[Rr], writes=[Rr])
                P.op(T, lambda e: e.transpose(out=ptr[:, 0:128], in_=mm_, identity=ident_bf[:]), reads=[Rr, R_const], writes=[rptr])
                P.op(V, lambda e: e.tensor_copy(out=mixT[:, 4 + h, n * 128:(n + 1) * 128], in_=ptr[:, 0:128]), reads=[rptr], writes=[R_mixT[4 + h][n]])
            pending.append((j + 1, part2))
            pending.append((j + 2, part3))
            pending.append((j + 3, part4))

        def df_flush(j):
            keep = []
            for (due, fn) in pending:
                if due <= j:
                    fn()
                else:
                    keep.append((due, fn))
            pending[:] = keep

        def df_PV(j):
            h, qg, kb = iters[j]
            bufi = j % 2
            i0 = max(0, kb - 2 * qg)
            for m in range(2):
                for i in range(i0, 2):
                    P.op(T, lambda e, m=m, i=i: e.matmul(
                        accp[m][i][:, 0:129], lhsT=PT[bufi][:, m, i * 128:(i + 1) * 128], rhs=dfv[:, kb, h, 0:129],
                        start=(kb == 0), stop=(kb == 2 * qg + i)),
                        reads=[R_PT[bufi], R_dfv[kb]], writes=[raccp[m][i]])
            if kb >= 2 * qg:
                df_FIN(h, kb, kb - 2 * qg, j)

        wz = wblk[0]
        R_wz = R_wblk[0]
        wba = main.alloc([8, 8], BF16)
        R_wba = Res()
        etmp = main.alloc([512], F32)
        R_et = Res()

        def dn_prolog():
            P.dma("sync", lambda e: e.dma_start(out=wz, in_=win_s[:, :, 1536:2048]), reads=[R_win], writes=[R_wz])
            P.dma("sync", lambda e: e.dma_start(out=wba, in_=win_s[:, :, 2048:2056]), reads=[R_win], writes=[R_wba])
            yield
            for n in range(NT):
                for k in range(8):
                    P.op(T, lambda e, k=k, n=n: e.matmul(pb[6][:, :], lhsT=hnT[:, k, n * 128:(n + 1) * 128], rhs=wz[:, k, :], start=(k == 0), stop=(k == 7)),
                         reads=[R_hnT[n], R_wz], writes=[rpb[6]])
                yield
                P.op(S, lambda e: e.activation(out=etmp, in_=pb[6][:, :], func=AF.Exp, scale=-1.0), reads=[rpb[6]], writes=[R_et])
                yield
                P.op(V, lambda e: e.tensor_scalar(out=etmp, in0=etmp, scalar1=1.0, scalar2=None, op0=ALU.add), reads=[R_et], writes=[R_et])
                P.op(V, lambda e: e.reciprocal(out=etmp, in_=etmp), reads=[R_et], writes=[R_et])
                yield
                P.op(V, lambda e, n=n: e.tensor_tensor(out=siluz[:, n, :], in0=pb[6][:, :], in1=etmp, op=ALU.mult), reads=[rpb[6], R_et], writes=[R_sz[n]])
                yield
                for k in range(8):
                    P.op(T, lambda e, k=k, n=n: e.matmul(pb[6][:, 0:8], lhsT=hnT[:, k, n * 128:(n + 1) * 128], rhs=wba[:, k, :], start=(k == 0), stop=(k == 7)),
                         reads=[R_hnT[n], R_wba], writes=[rpb[6]])
                P.op(V, lambda e, n=n: e.tensor_copy(out=ba[:, n, :], in_=pb[6][:, 0:8]), reads=[rpb[6]], writes=[R_ba])
                yield
            bb = ba[:, :, 0:4]
            aa = ba[:, :, 4:8]
            dtb_b = DTB.unsqueeze(1).to_broadcast([128, NT, 4])
            nega_b = NEGA.unsqueeze(1).to_broadcast([128, NT, 4])

            def smop(eng, fn, extra=()):
                P.op(eng, fn, reads=[R_sm, R_ba, R_const] + list(extra), writes=[R_sm])
            smop(S, lambda e: e.activation(out=sm[:, 0], in_=bb, func=AF.Abs))
            yield
            smop(S, lambda e: e.activation(out=sm[:, 1], in_=sm[:, 0], func=AF.Exp, scale=-1.0))
            yield
            smop(S, lambda e: e.activation(out=sm[:, 2], in_=sm[:, 1], func=AF.Ln, bias=ONEC))
            yield
            smop(V, lambda e: e.scalar_tensor_tensor(out=sm[:, 3], in0=bb, scalar=0.0, in1=sm[:, 2], op0=ALU.min, op1=ALU.subtract))
            smop(V, lambda e: e.tensor_tensor(out=sm[:, 4], in0=aa, in1=dtb_b, op=ALU.add))
            yield
            smop(S, lambda e: e.activation(out=sm[:, 0], in_=sm[:, 4], func=AF.Abs))
            yield
            smop(S, lambda e: e.activation(out=sm[:, 1], in_=sm[:, 0], func=AF.Exp, scale=-1.0))
            yield
            smop(S, lambda e: e.activation(out=sm[:, 2], in_=sm[:, 1], func=AF.Ln, bias=ONEC))
            yield
            smop(V, lambda e: e.scalar_tensor_tensor(out=sm[:, 5], in0=sm[:, 4], scalar=0.0, in1=sm[:, 2], op0=ALU.max, op1=ALU.add))
            smop(V, lambda e: e.tensor_tensor(out=sm[:, 6], in0=sm[:, 5], in1=nega_b, op=ALU.mult))
            yield
            gflat = sm[:, 6].rearrange("p n h -> p (n h)")
            P.op(T, lambda e: e.matmul(pb[6][:, 0:NT * 4], lhsT=UI[:], rhs=gflat, start=True, stop=True), reads=[R_sm, R_const], writes=[rpb[6]])
            smop(V, lambda e: e.tensor_copy(out=sm[:, 7].rearrange("p n h -> p (n h)"), in_=pb[6][:, 0:NT * 4]), extra=[rpb[6]])
            yield
            P.op(T, lambda e: e.matmul(pb[6][:, 0:NT * 4], lhsT=ones_f[:], rhs=gflat, start=True, stop=True), reads=[R_sm, R_const], writes=[rpb[6]])
            smop(V, lambda e: e.tensor_copy(out=sm[:, 8].rearrange("p n h -> p (n h)"), in_=pb[6][:, 0:NT * 4]), extra=[rpb[6]])
            yield
            smop(V, lambda e: e.tensor_tensor(out=sm[:, 9], in0=sm[:, 3], in1=sm[:, 7], op=ALU.add))
            smop(V, lambda e: e.tensor_tensor(out=sm[:, 10], in0=sm[:, 8], in1=sm[:, 7], op=ALU.subtract))
            yield
            smop(S, lambda e: e.activation(out=sm[:, 11], in_=sm[:, 9], func=AF.Exp))
            yield
            smop(S, lambda e: e.activation(out=sm[:, 10], in_=sm[:, 10], func=AF.Exp))
            yield
            smop(S, lambda e: e.activation(out=sm[:, 8], in_=sm[:, 8], func=AF.Exp))
            yield
            smop(S, lambda e: e.activation(out=sm[:, 3], in_=sm[:, 3], func=AF.Exp))
            yield

        bg2[0] = dn_prolog()
        df_ST(0)
        for j in range(len(iters)):
            if j + 1 < len(iters):
                df_ST(j + 1)
            df_EXP(j)
            df_PV(j)
            df_flush(j)
            bg_step()
            bg2_step()
        df_flush(10 ** 9)
        while bg[0] is not None:
            bg_step()

        while bg2[0] is not None:
            bg2_step()
        P.barrier()
        main.reset(base_off)
        GC, HC, C1, C2, DEC, BETA = sm[:, 7], sm[:, 9], sm[:, 11], sm[:, 10], sm[:, 8], sm[:, 3]

        NBt = NT // 4
        HB = []
        for _i in range(2):
            HB.append(dict(qT=main.alloc([L], BF16), kT=main.alloc([L], BF16), vT=main.alloc([L], BF16),
                           kbg=main.alloc([NT, 128], BF16), ksc=main.alloc([NT, 128], BF16), vb=main.alloc([NT, 128], BF16),
                           wqkv=main.alloc([3, 8, 128], BF16),
                           R_q=Res(), R_k=Res(), R_v=Res(), R_tok=Res(), R_w=Res()))
        xc = main.alloc([L + 4], F32)
        R_xc = Res()
        yv = main.alloc([L], F32)
        R_yv = Res()
        sq = main.alloc([L], BF16)
        R_sq = Res()
        rn = main.alloc([512], F32)
        R_rn = Res()
        dA = main.alloc([4, 128], F32)
        dB = main.alloc([4, 128], F32)
        dC = main.alloc([4, 128], F32)
        dD = main.alloc([4, 128], F32)
        R_dA, R_dB, R_dC, R_dD = Res(), Res(), Res(), Res()
        Yb = [main.alloc([4, 128], F32) for _ in range(2)]
        Zb = [main.alloc([4, 128], F32) for _ in range(2)]
        R_Y = [Res(), Res()]
        R_Z = [Res(), Res()]
        Nm = main.alloc([4, 128], F32)
        R_N = Res()
        PB = []
        for _i in range(2):
            PB.append(dict(TTb=main.alloc([4, 128], BF16), nwT=main.alloc([4, 128], BF16), qsT=main.alloc([4, 128], BF16),
                           QKm=main.alloc([4, 128], BF16), R_TT=Res(), R_nw=Res(), R_qs=Res(), R_QK=Res()))
        vnb = main.alloc([128], BF16)
        R_vn = Res()
        Sf = main.alloc([128], F32)
        Sb = main.alloc([128], BF16)
        R_Sf, R_Sb = Res(), Res()
        mxd = main.alloc([128], BF16)
        R_mxd = Res()
        fs = main.alloc([8], F32)
        R_fs = Res()
        jk = main.alloc([128], BF16)
        R_jk = Res()
        P.op(V, lambda e: e.memset(xc[:, 0:4], 0.0), writes=[R_xc])
        idb = ident_f[:].unsqueeze(1).to_broadcast([128, 4, 128])
        negSLb = negSL[:].unsqueeze(1).to_broadcast([128, 4, 128])
        UIb = UI[:].unsqueeze(1).to_broadcast([128, 4, 128])
        BK_PREP = 0
        BK_G = 0
        BK_KQ = 0
        HBK = [(1, 2, 3), (4, 5, 6)]
        ptrf = ptr[:].bitcast(F32)

        def dn_prep(h):
            hb = HB[h % 2]
            for c in range(3):
                P.dma("sync", lambda e, c=c: e.dma_start(out=hb["wqkv"][:, c], in_=win_s[:, :, c * 512 + h * 128:c * 512 + (h + 1) * 128]),
                      reads=[R_win], writes=[hb["R_w"]])
            yield
            for c in range(3):
                cc = c * 4 + h
                for grp in range(NG):
                    for k in range(8):
                        P.op(T, lambda e, c=c, k=k, grp=grp: e.matmul(pb[BK_PREP][:, :], lhsT=hb["wqkv"][:, c, k, :], rhs=hnT[:, k, grp * 512:(grp + 1) * 512],
                                                                      start=(k == 0), stop=(k == 7)),
                             reads=[hb["R_w"]] + R_hnT[grp * 4:grp * 4 + 4], writes=[rpb[BK_PREP]])
                        if k == 3:
                            yield
                    P.op(S, lambda e, grp=grp: e.activation(out=xc[:, 3 + grp * 512:3 + (grp + 1) * 512], in_=pb[BK_PREP][:, :], func=AF.Copy),
                         reads=[rpb[BK_PREP]], writes=[R_xc])
                    yield
                P.op(V, lambda e, cc=cc: e.tensor_scalar(out=yv, in0=xc[:, 3:3 + L], scalar1=convw[:, cc, 3:4], scalar2=None, op0=ALU.mult),
                     reads=[R_xc, R_const], writes=[R_yv])
                yield
                for j in (2, 1, 0):
                    P.op(V, lambda e, cc=cc, j=j: e.scalar_tensor_tensor(out=yv, in0=xc[:, j:j + L], scalar=convw[:, cc, j:j + 1], in1=yv,
                                                                        op0=ALU.mult, op1=ALU.add), reads=[R_xc, R_const, R_yv], writes=[R_yv])
                    yield
                if c == 2:
                    P.op(S, lambda e: e.activation(out=hb["vT"], in_=yv, func=AF.Silu), reads=[R_yv], writes=[hb["R_v"]])
                    yield
                    continue
                P.op(S, lambda e: e.activation(out=yv, in_=yv, func=AF.Silu), reads=[R_yv], writes=[R_yv])
                yield
                P.op(G, lambda e: e.tensor_tensor(out=sq, in0=yv, in1=yv, op=ALU.mult), reads=[R_yv], writes=[R_sq])
                yield
                dstT, Rd, qscale = (hb["qT"], hb["R_q"], 128 ** -0.5) if c == 0 else (hb["kT"], hb["R_k"], 1.0)
                for grp in range(NG):
                    gs_ = slice(grp * 512, (grp + 1) * 512)
                    P.op(T, lambda e, gs_=gs_: e.matmul(pb[BK_PREP][:, :], lhsT=ones_bf[:], rhs=sq[:, gs_], start=True, stop=True),
                         reads=[R_sq, R_const], writes=[rpb[BK_PREP]])
                    P.op(S, lambda e: e.activation(out=rn, in_=pb[BK_PREP][:, :], func=AF.Ln, bias=EPSL2), reads=[rpb[BK_PREP], R_const], writes=[R_rn])
                    yield
                    P.op(S, lambda e: e.activation(out=rn, in_=rn, func=AF.Exp, scale=-0.5), reads=[R_rn], writes=[R_rn])
                    yield
                    P.op(V, lambda e, gs_=gs_, dstT=dstT, qscale=qscale: e.scalar_tensor_tensor(out=dstT[:, gs_], in0=yv[:, gs_], scalar=qscale, in1=rn,
                                                                                               op0=ALU.mult, op1=ALU.mult),
                         reads=[R_yv, R_rn], writes=[Rd])
                    yield
            for n in range(NT):
                ts_ = slice(n * 128, (n + 1) * 128)
                P.op(T, lambda e, ts_=ts_: e.transpose(out=ptr[:, 0:128], in_=hb["kT"][:, ts_], identity=ident_bf[:]), reads=[hb["R_k"], R_const], writes=[rptr])
                P.op(T, lambda e, ts_=ts_: e.transpose(out=ptr[:, 128:256], in_=hb["vT"][:, ts_], identity=ident_bf[:]), reads=[hb["R_v"], R_const], writes=[rptr])
                P.op(S, lambda e, n=n: e.activation(out=hb["kbg"][:, n, :], in_=ptr[:, 0:128], func=AF.Copy, scale=C1[:, n, h:h + 1]),
                     reads=[rptr, R_sm], writes=[hb["R_tok"]])
                P.op(V, lambda e, n=n: e.tensor_scalar(out=hb["ksc"][:, n, :], in0=ptr[:, 0:128], scalar1=C2[:, n, h:h + 1], scalar2=None, op0=ALU.mult),
                     reads=[rptr, R_sm], writes=[hb["R_tok"]])
                P.op(S, lambda e, n=n: e.activation(out=hb["vb"][:, n, :], in_=ptr[:, 128:256], func=AF.Copy, scale=BETA[:, n, h:h + 1]),
                     reads=[rptr, R_sm], writes=[hb["R_tok"]])
                yield

        R_dAh = [Res(), Res()]
        R_dBh = [Res(), Res()]
        R_dCh = [Res(), Res()]
        R_dDh = [Res(), Res()]
        R_Yh = [[Res(), Res()], [Res(), Res()]]
        R_Zh = [[Res(), Res()], [Res(), Res()]]
        R_Nh = [Res(), Res()]

        def dn_pre(h, jb, hf):
            hb = HB[h % 2]
            pbf = PB[(h * NBt + jb) % 2]
            BY, BZ, BN = HBK[hf]
            t0_ = hf * 2
            n0 = jb * 4 + t0_
            tv = slice(t0_, t0_ + 2)
            sl = slice(n0 * 128, (n0 + 2) * 128)
            gcb = GC[:, n0:n0 + 2, h:h + 1].to_broadcast([128, 2, 128])
            hcb = HC[:, n0:n0 + 2, h:h + 1].to_broadcast([128, 2, 128])
            idb2 = ident_f[:].unsqueeze(1).to_broadcast([128, 2, 128])
            nsl2 = negSL[:].unsqueeze(1).to_broadcast([128, 2, 128])
            ui2 = UI[:].unsqueeze(1).to_broadcast([128, 2, 128])
            kT, qT = hb["kT"], hb["qT"]
            Rpk = [pbf["R_TT"], pbf["R_nw"], pbf["R_qs"], pbf["R_QK"]]
            cols = slice(t0_ * 128, (t0_ + 2) * 128)

            def bv(bank):
                return pb[bank][:, cols].rearrange("p (i t) -> p i t", i=2)

            def mm2(bank, lhs_fn, rhs_fn, reads):
                for i in range(2):
                    P.op(T, lambda e, i=i: e.matmul(pb[bank][:, (t0_ + i) * 128:(t0_ + i + 1) * 128], lhsT=lhs_fn(i), rhs=rhs_fn(i), start=True, stop=True),
                         reads=reads, writes=[rpb[bank]])
            dAv, dBv, dCv, dDv, Nv = dA[:, tv, :], dB[:, tv, :], dC[:, tv, :], dD[:, tv, :], Nm[:, tv, :]
            Yv = [Yb[0][:, tv, :], Yb[1][:, tv, :]]
            Zv = [Zb[0][:, tv, :], Zb[1][:, tv, :]]
            RA, RB, RC, RD, RN = R_dAh[hf], R_dBh[hf], R_dCh[hf], R_dDh[hf], R_Nh[hf]
            RY, RZ = R_Yh[hf], R_Zh[hf]
            P.op(V, lambda e: e.tensor_tensor(out=dAv, in0=idb2, in1=gcb, op=ALU.mult), reads=[R_const, R_sm], writes=[RA])
            yield
            mm2(BN, lambda i: ones_f[:], lambda i: dAv[:, i, :], [RA, R_const])
            P.op(V, lambda e: e.tensor_tensor(out=dBv, in0=bv(BN), in1=hcb, op=ALU.subtract), reads=[rpb[BN], R_sm], writes=[RB])
            P.op(V, lambda e: e.tensor_tensor(out=dCv, in0=bv(BN), in1=gcb, op=ALU.subtract), reads=[rpb[BN], R_sm], writes=[RC])
            yield
            P.op(S, lambda e: e.activation(out=dBv, in_=dBv, func=AF.Exp, scale=-1.0), reads=[RB], writes=[RB])
            yield
            P.op(S, lambda e: e.activation(out=dCv, in_=dCv, func=AF.Exp), reads=[RC], writes=[RC])
            P.op(S, lambda e: e.activation(out=dDv, in_=bv(BN), func=AF.Exp), reads=[rpb[BN]], writes=[RD])
            yield
            P.op(V, lambda e: e.scalar_tensor_tensor(out=dBv, in0=dBv, scalar=1.0, in1=nsl2, op0=ALU.min, op1=ALU.mult), reads=[RB, R_const], writes=[RB])
            yield
            P.op(V, lambda e: e.scalar_tensor_tensor(out=dCv, in0=dCv, scalar=1.0, in1=ui2, op0=ALU.min, op1=ALU.mult), reads=[RC, R_const], writes=[RC])
            yield
            P.op(G, lambda e: e.tensor_tensor(out=pbf["qsT"][:, tv, :], in0=qT[:, sl].rearrange("p (i t) -> p i t", i=2), in1=dDv, op=ALU.mult),
                 reads=[hb["R_q"], RD], writes=[pbf["R_qs"]])
            yield
            mm2(BN, lambda i: kT[:, (n0 + i) * 128:(n0 + i + 1) * 128], lambda i: kT[:, (n0 + i) * 128:(n0 + i + 1) * 128], [hb["R_k"]])
            P.op(V, lambda e: e.tensor_tensor(out=Yv[0], in0=bv(BN), in1=dBv, op=ALU.mult), reads=[rpb[BN], RB], writes=[RY[0]])
            yield
            mm2(BN, lambda i: kT[:, (n0 + i) * 128:(n0 + i + 1) * 128], lambda i: qT[:, (n0 + i) * 128:(n0 + i + 1) * 128], [hb["R_k"], hb["R_q"]])
            P.op(V, lambda e: e.tensor_tensor(out=pbf["QKm"][:, tv, :], in0=bv(BN), in1=dCv, op=ALU.mult), reads=[rpb[BN], RC], writes=[pbf["R_QK"]])
            yield
            for i in range(2):
                P.op(T, lambda e, i=i: e.transpose(out=pb[BZ][:, (t0_ + i) * 128:(t0_ + i + 1) * 128], in_=Yv[0][:, i, :], identity=ident_f[:]),
                     reads=[RY[0], R_const], writes=[rpb[BZ]])
            yield
            P.op(S, lambda e: e.activation(out=Zv[0], in_=bv(BZ), func=AF.Copy), reads=[rpb[BZ]], writes=[RZ[0]])
            yield
            P.op(V, lambda e: e.tensor_tensor(out=Nv, in0=Zv[0], in1=idb2, op=ALU.add), reads=[RZ[0], R_const], writes=[RN])
            yield
            cur = 0
            for lv in range(1, 7):
                nx = 1 - cur
                mm2(BY, lambda i, cur=cur: Zv[cur][:, i, :], lambda i, cur=cur: Yv[cur][:, i, :], [RZ[cur], RY[cur]])
                yield
                if lv <= 5:
                    mm2(BZ, lambda i, cur=cur: Yv[cur][:, i, :], lambda i, cur=cur: Zv[cur][:, i, :], [RZ[cur], RY[cur]])
                    yield
                P.op(S, lambda e, nx=nx: e.activation(out=Yv[nx], in_=bv(BY), func=AF.Copy), reads=[rpb[BY]], writes=[RY[nx]])
                yield
                if lv <= 5:
                    P.op(V, lambda e, nx=nx: e.tensor_copy(out=Zv[nx], in_=bv(BZ)), reads=[rpb[BZ]], writes=[RZ[nx]])
                    yield
                mm2(BN, lambda i, nx=nx: Yv[nx][:, i, :], lambda i: Nv[:, i, :], [RY[nx], RN])
                yield
                P.op(V, lambda e: e.tensor_tensor(out=Nv, in0=Nv, in1=bv(BN), op=ALU.add), reads=[RN, rpb[BN]], writes=[RN])
                yield
                cur = nx
            P.op(S, lambda e: e.activation(out=pbf["TTb"][:, tv, :], in_=Nv, func=AF.Copy), reads=[RN], writes=[pbf["R_TT"]])
            yield
            mm2(BN, lambda i: hb["kbg"][:, n0 + i, :], lambda i: pbf["TTb"][:, t0_ + i, :], [hb["R_tok"], pbf["R_TT"]])
            yield
            P.op(S, lambda e: e.activation(out=pbf["nwT"][:, tv, :], in_=bv(BN), func=AF.Copy, scale=-1.0), reads=[rpb[BN]], writes=[pbf["R_nw"]])
            yield

        def dn_scan(h, jb):
            hb = HB[h % 2]
            pbf = PB[(h * NBt + jb) % 2]
            n0 = jb * 4
            VN = ptrf[:, 256:384]
            OO = ptrf[:, 384:512]
            DS = ptrf[:, 256:384]
            rS = rptr
            if jb == 0:
                P.op(V, lambda e: e.memset(Sf, 0.0), writes=[R_Sf])
                P.op(V, lambda e: e.memset(Sb, 0.0), writes=[R_Sb])
                yield
            for i in range(4):
                n = n0 + i
                P.op(T, lambda e, i=i, n=n: e.matmul(VN, lhsT=pbf["TTb"][:, i, :], rhs=hb["vb"][:, n, :], start=True, stop=False), reads=[pbf["R_TT"], hb["R_tok"]], writes=[rS])
                P.op(T, lambda e, i=i: e.matmul(VN, lhsT=pbf["nwT"][:, i, :], rhs=Sb, start=False, stop=True), reads=[pbf["R_nw"], R_Sb], writes=[rS])
                yield
                P.op(S, lambda e: e.activation(out=vnb, in_=VN, func=AF.Copy), reads=[rS], writes=[R_vn])
                yield
                P.op(T, lambda e, i=i: e.matmul(OO, lhsT=pbf["qsT"][:, i, :], rhs=Sb, start=True, stop=False), reads=[pbf["R_qs"], R_Sb], writes=[rS])
                P.op(T, lambda e, i=i: e.matmul(OO, lhsT=pbf["QKm"][:, i, :], rhs=vnb, start=False, stop=True), reads=[pbf["R_QK"], R_vn], writes=[rS])
                P.op(T, lambda e, n=n: e.matmul(DS, lhsT=hb["ksc"][:, n, :], rhs=vnb, start=True, stop=True), reads=[hb["R_tok"], R_vn], writes=[rS])
                yield
                P.op(V, lambda e, n=n: e.scalar_tensor_tensor(out=Sf, in0=Sf, scalar=DEC[:, n, h:h + 1], in1=DS, op0=ALU.mult, op1=ALU.add),
                     reads=[R_Sf, R_sm, rS], writes=[R_Sf])
                yield
                P.op(S, lambda e: e.activation(out=Sb, in_=Sf, func=AF.Copy), reads=[R_Sf], writes=[R_Sb])
                yield
                P.op(S, lambda e: e.activation(out=jk, in_=OO, func=AF.Square, accum_out=fs[:, 0:1]), reads=[rS], writes=[R_jk, R_fs])
                yield
                P.op(S, lambda e: e.activation(out=fs[:, 1:2], in_=fs[:, 0:1], func=AF.Ln, scale=1.0 / 128, bias=EPSC), reads=[R_fs, R_const], writes=[R_fs])
                yield
                P.op(S, lambda e: e.activation(out=fs[:, 2:3], in_=fs[:, 1:2], func=AF.Exp, scale=-0.5), reads=[R_fs], writes=[R_fs])
                yield
                P.op(V, lambda e, n=n: e.scalar_tensor_tensor(out=mxd, in0=OO, scalar=fs[:, 2:3], in1=siluz[:, n, h * 128:(h + 1) * 128],
                                                             op0=ALU.mult, op1=ALU.mult), reads=[rS, R_fs, R_sz[n]], writes=[R_mxd])
                yield
                P.op(T, lambda e: e.transpose(out=ptr[:, 256:384], in_=mxd, identity=ident_bf[:]), reads=[R_mxd, R_const], writes=[rptr])
                yield
                P.op(V, lambda e, n=n: e.tensor_copy(out=mixT[:, h, n * 128:(n + 1) * 128], in_=ptr[:, 256:384]), reads=[rptr], writes=[R_mixT[h][n]])
                yield

        run_rr([dn_prep(0)])
        run_rr([dn_pre(0, 0, 0), dn_pre(0, 0, 1)])
        for h in range(4):
            bg[0] = dn_prep(h + 1) if h + 1 < 4 else None
            for jb in range(NBt):
                if jb + 1 < NBt:
                    nxt = (dn_pre(h, jb + 1, 0), dn_pre(h, jb + 1, 1))
                elif h + 1 < 4:
                    while bg[0] is not None:
                        bg_step()
                    nxt = (dn_pre(h + 1, 0, 0), dn_pre(h + 1, 0, 1))
                else:
                    nxt = (None, None)
                run_rr([nxt[0], nxt[1], dn_scan(h, jb)], use_bg=True)
            while bg[0] is not None:
                bg_step()

        if dbg:
            P.barrier()
            main.reset(base_off)
            dbo = main.alloc([8 * L], F32)
            R_dbo = Res()
            P.op(V, lambda e: e.tensor_copy(out=dbo, in_=mixT.rearrange("p a b -> p (a b)")), writes=[R_dbo])
            P.dma("sync", lambda e: e.dma_start(out=dbg_d[:, :], in_=dbo), reads=[R_dbo], is_output=True)
            P.barrier()

        P.barrier()
        if L >= 2048:
            main.reset(0)
            wu = [main.alloc([8, 1024], BF16) for _ in range(2)]
            main.reset(base_off0)
        else:
            main.reset(base_off0)
            wu = [main.alloc([8, 1024], BF16) for _ in range(2)]
        wo = main.alloc([8, D], BF16)
        R_wo = Res()
        P.dma("sync", lambda e: e.dma_start(out=wo, in_=wout_s[:, :, :]), reads=[R_wrest], writes=[R_wo])
        x1b = [main.alloc([4, D], F32) for _ in range(2)]
        R_x1b = [[Res() for _ in range(4)] for _ in range(2)]
        hmTb = [main.alloc([8, 512], BF16) for _ in range(2)]
        R_hmb2 = [[Res() for _ in range(4)] for _ in range(2)]
        hT = main.alloc([8, 512], BF16)
        R_hT = [Res() for _ in range(8)]
        wd = [main.alloc([8, 1024], BF16) for _ in range(2)]
        R_wu = [Res(), Res()]
        R_wd = [Res(), Res()]
        xr = [main.alloc([D], F32) for _ in range(2)]
        R_xr = [Res(), Res()]
        xo = [main.alloc([D], F32) for _ in range(2)]
        R_xo = [Res(), Res()]
        hmb = main.alloc([D], BF16)
        R_hmb = Res()
        rl = [main.alloc([512], BF16) for _ in range(2)]
        R_rl = [Res(), Res()]
        mstb = [main.alloc([4, 8], F32) for _ in range(2)]
        R_mstb = [Res(), Res()]
        jkp = main.alloc([D], BF16)
        jke = main.alloc([D], BF16)
        R_jkp, R_jke = Res(), Res()
        NBLK = L // 512
        wq_cnt = [0]

        def mlp_pro(blk):
            pb_ = blk % 2
            x1, hmT, mst, R_x1, R_hm, R_mst = x1b[pb_], hmTb[pb_], mstb[pb_], R_x1b[pb_], R_hmb2[pb_], R_mstb[pb_]
            for i in range(4):
                n = blk * 4 + i
                ts_ = slice(n * 128, (n + 1) * 128)
                xi = n % 2
                P.dma("scalar", lambda e, xi=xi, ts_=ts_, b=b: e.dma_start(out=xr[xi], in_=x_d[b, ts_, :]), writes=[R_xr[xi]])
                for c2 in range(2):
                    for k in range(8):
                        P.op(T, lambda e, k=k, c2=c2, ts_=ts_: e.matmul(pb[c2][:, :], lhsT=mixT[:, k, ts_], rhs=wo[:, k, c2 * 512:(c2 + 1) * 512],
                                                                       start=(k == 0), stop=(k == 7)),
                             reads=[R_mixT[k][n], R_wo], writes=[rpb[c2]])
                    yield
                    P.op(V, lambda e, c2=c2, i=i, xi=xi: e.tensor_tensor(out=x1[:, i, c2 * 512:(c2 + 1) * 512], in0=pb[c2][:, :],
                                                                          in1=xr[xi][:, c2 * 512:(c2 + 1) * 512], op=ALU.add),
                         reads=[rpb[c2], R_xr[xi]], writes=[R_x1[i]])
                    yield
                P.op(S, lambda e, i=i: e.activation(out=jkp, in_=x1[:, i, :], func=AF.Square, accum_out=mst[:, i, 0:1]), reads=[R_x1[i]], writes=[R_jkp, R_mst])
                yield
                P.op(S, lambda e, i=i: e.activation(out=mst[:, i, 1:2], in_=mst[:, i, 0:1], func=AF.Ln, scale=1.0 / D, bias=EPSC), reads=[R_mst, R_const], writes=[R_mst])
                yield
                P.op(S, lambda e, i=i: e.activation(out=mst[:, i, 2:3], in_=mst[:, i, 1:2], func=AF.Exp, scale=-0.5), reads=[R_mst], writes=[R_mst])
                yield
                P.op(V, lambda e, i=i: e.tensor_scalar(out=hmb, in0=x1[:, i, :], scalar1=mst[:, i, 2:3], scalar2=None, op0=ALU.mult),
                     reads=[R_x1[i], R_mst], writes=[R_hmb])
                yield
                for k in range(8):
                    P.op(T, lambda e, k=k: e.transpose(out=ptr[:, k * 128:(k + 1) * 128], in_=hmb[:, k * 128:(k + 1) * 128], identity=ident_bf[:]),
                         reads=[R_hmb, R_const], writes=[rptr])
                P.op(S, lambda e, i=i: e.activation(out=hmT[:, :, i * 128:(i + 1) * 128], in_=ptr[:].rearrange("p (k t) -> p k t", k=8), func=AF.Copy),
                     reads=[rptr], writes=[R_hm[i]])
                yield

        def mlp_main(blk):
            pb_ = blk % 2
            x1, hmT, R_x1, R_hm = x1b[pb_], hmTb[pb_], R_x1b[pb_], R_hmb2[pb_]
            for fq in range(4):
                wi = wq_cnt[0] % 2
                wq_cnt[0] += 1
                P.dma("sync", lambda e, wi=wi, fq=fq: e.dma_start(out=wu[wi], in_=wup_s[:, :, fq * 1024:(fq + 1) * 1024]), reads=[R_wrest], writes=[R_wu[wi]])
                P.dma("sync", lambda e, wi=wi, fq=fq: e.dma_start(out=wd[wi], in_=wdn_s[:, fq * 8:(fq + 1) * 8, :]), reads=[R_wrest], writes=[R_wd[wi]])
                for fc in range(8):
                    pz = 2 + fc % 2
                    for k in range(8):
                        P.op(T, lambda e, k=k, fc=fc, wi=wi, pz=pz: e.matmul(pb[pz][:, :], lhsT=wu[wi][:, k, fc * 128:(fc + 1) * 128], rhs=hmT[:, k, :],
                                                                          start=(k == 0), stop=(k == 7)),
                             reads=[R_wu[wi]] + R_hm, writes=[rpb[pz]])
                    ri = fc % 2
                    P.op(S, lambda e, pz=pz, ri=ri: e.activation(out=rl[ri], in_=pb[pz][:, :], func=AF.Relu), reads=[rpb[pz]], writes=[R_rl[ri]])
                    P.op(G, lambda e, fc=fc, ri=ri: e.tensor_tensor(out=hT[:, fc, :], in0=rl[ri], in1=rl[ri], op=ALU.mult), reads=[R_rl[ri]], writes=[R_hT[fc]])
                    yield
                for i in range(4):
                    for c2 in range(2):
                        pz = 4 + c2
                        for fc in range(8):
                            P.op(T, lambda e, fc=fc, i=i, c2=c2, wi=wi, pz=pz: e.matmul(pb[pz][:, :], lhsT=hT[:, fc, i * 128:(i + 1) * 128],
                                                                                      rhs=wd[wi][:, fc, c2 * 512:(c2 + 1) * 512],
                                                                                      start=(fc == 0), stop=(fc == 7)),
                                 reads=[R_hT[fc], R_wd[wi]], writes=[rpb[pz]])
                        P.op(V, lambda e, i=i, c2=c2, pz=pz: e.tensor_tensor(out=x1[:, i, c2 * 512:(c2 + 1) * 512], in0=x1[:, i, c2 * 512:(c2 + 1) * 512],
                                                                           in1=pb[pz][:, :], op=ALU.add),
                             reads=[R_x1[i], rpb[pz]], writes=[R_x1[i]])
                        yield

        def mlp_epi(blk):
            pb_ = blk % 2
            x1, mst, R_x1, R_mst = x1b[pb_], mstb[pb_], R_x1b[pb_], R_mstb[pb_]
            for i in range(4):
                n = blk * 4 + i
                xi = n % 2
                P.op(S, lambda e, i=i: e.activation(out=jke, in_=x1[:, i, :], func=AF.Square, accum_out=mst[:, i, 3:4]), reads=[R_x1[i]], writes=[R_jke, R_mst])
                yield
                P.op(S, lambda e, i=i: e.activation(out=mst[:, i, 4:5], in_=mst[:, i, 3:4], func=AF.Ln, scale=1.0 / D, bias=EPSC), reads=[R_mst, R_const], writes=[R_mst])
                yield
                P.op(S, lambda e, i=i: e.activation(out=mst[:, i, 5:6], in_=mst[:, i, 4:5], func=AF.Exp, scale=-0.5), reads=[R_mst], writes=[R_mst])
                yield
                P.op(V, lambda e, i=i, xi=xi: e.scalar_tensor_tensor(out=xo[xi], in0=x1[:, i, :], scalar=mst[:, i, 5:6], in1=fnw_bc[:], op0=ALU.mult, op1=ALU.mult),
                     reads=[R_x1[i], R_mst, R_const], writes=[R_xo[xi]])
                P.dma("scalar", lambda e, xi=xi, n=n, b=b: e.dma_start(out=y_d[b, n * 128:(n + 1) * 128, :], in_=xo[xi]), reads=[R_xo[xi]], is_output=True)
                yield

        def seq_gens(*gens):
            for g in gens:
                if g is not None:
                    yield from g

        run_rr([mlp_pro(0)])
        for blk in range(NBLK):
            side = seq_gens(mlp_epi(blk - 1) if blk >= 1 else None, mlp_pro(blk + 1) if blk + 1 < NBLK else None)
            run_rr([side, mlp_main(blk)])
        run_rr([mlp_epi(NBLK - 1)])
        P.barrier()

    P.finish()
    P.emit()
    P.close()
    return nc


_NC_CACHE = {}


def kernel(x, positions, attn_norm_w, w_in, conv_w, a_log, dt_bias, dn_norm_w, lambda_q1, lambda_k1,
           lambda_q2, lambda_k2, diff_norm_w, group_scale, w_out, mlp_norm_w, w_up, w_down, final_norm_w):
    x = np.asarray(x, dtype=np.float32)
    B, L, _ = x.shape
    NB = B // NCORES
    key = (NB, L)
    if key not in _NC_CACHE:
        _NC_CACHE[key] = build(NB, L)
    nc = _NC_CACHE[key]

    def f(a):
        return np.ascontiguousarray(np.asarray(a, dtype=np.float32))
    shared = {
        "positions": np.ascontiguousarray(np.asarray(positions, dtype=np.int32)),
        "attn_norm_w": f(np.asarray(attn_norm_w)[0]), "w_in": f(np.asarray(w_in)[0]), "conv_w": f(np.asarray(conv_w)[0]),
        "a_log": f(np.asarray(a_log)[0]), "dt_bias": f(np.asarray(dt_bias)[0]), "dn_norm_w": f(np.asarray(dn_norm_w)[0]),
        "lambda_q1": f(np.asarray(lambda_q1)[0]), "lambda_k1": f(np.asarray(lambda_k1)[0]),
        "lambda_q2": f(np.asarray(lambda_q2)[0]), "lambda_k2": f(np.asarray(lambda_k2)[0]),
        "diff_norm_w": f(np.asarray(diff_norm_w)[0]), "group_scale": f(np.asarray(group_scale)[0]),
        "w_out": f(np.asarray(w_out)[0]), "mlp_norm_w": f(np.asarray(mlp_norm_w)[0]), "w_up": f(np.asarray(w_up)[0]),
        "w_down": f(np.asarray(w_down)[0]), "final_norm_w": f(final_norm_w),
    }
    in_maps = []
    for c in range(NCORES):
        m = dict(shared)
        m["x"] = np.ascontiguousarray(x[c * NB:(c + 1) * NB])
        in_maps.append(m)
    res = run_bass_kernel_spmd(nc, in_maps, core_ids=list(range(NCORES)))
    return np.concatenate([np.asarray(r["y"]) for r in res.results], axis=0).astype(np.float32)
```

```python
import math
import os
import numpy as np
from contextlib import ExitStack
import concourse.bass as bass
import concourse.mybir as mybir
from concourse.bass_utils import run_bass_kernel_spmd

F32 = mybir.dt.float32
BF16 = mybir.dt.bfloat16
I32 = mybir.dt.int32
AF = mybir.ActivationFunctionType
ALU = mybir.AluOpType

D = 1024
DIN = 3592
DFF = 4096
NCORES = 8
EPS = 1e-6
ROPE_THETA = 500000.0
LAMBDA_INIT = 0.8 - 0.6 * math.exp(-0.3 * 0)

EPOCH = 30000
ENGS = ("sync", "scalar", "vector", "gpsimd", "tensor")
COMPUTE = ("scalar", "vector", "gpsimd", "tensor")


class Res:
    __slots__ = ("w", "r", "excl")

    def __init__(self, excl=False):
        self.w = None
        self.r = []
        self.excl = excl


class Prog:
    def __init__(self, nc, n_dma_sems=(("sync", 24), ("scalar", 4), ("gpsimd", 8))):
        self.nc = nc
        self.ops = {e: [] for e in ENGS}
        self.cnt = {e: 0 for e in COMPUTE}
        self.waited = {e: {} for e in ENGS}
        self.dma_pool = {q: n for q, n in n_dma_sems}
        self.dma_uses = {q: [0] * n for q, n in n_dma_sems}
        self.dma_rr = {q: 0 for q, n in n_dma_sems}
        self.semkeys = set()
        self.stack = ExitStack()
        self.out_tokens = []
        self.last_tok = {}

    def sbuf(self, name, shape, dtype):
        return self.stack.enter_context(self.nc.sbuf_tensor(name, list(shape), dtype))

    def psum(self, name, shape, dtype=F32):
        return self.stack.enter_context(self.nc.psum_tensor(name, list(shape), dtype))

    def _need(self, eng, deps):
        best = {}
        for d in deps:
            if d is None:
                continue
            k, v = d
            if best.get(k, 0) < v:
                best[k] = v
        waits = []
        wd = self.waited[eng]
        for k, v in best.items():
            if eng == "tensor" and k[0] == "tensor":
                continue
            if wd.get(k, 0) >= v:
                continue
            wd[k] = v
            waits.append((k, v))
        return waits

    def _deps(self, reads, writes):
        deps = []
        for r in reads:
            deps.append(r.w)
            if r.excl:
                deps.extend(r.r)
        for w in writes:
            deps.append(w.w)
            deps.extend(w.r)
        return deps

    def _mark(self, tok, reads, writes):
        for r in reads:
            if r.excl:
                r.w = tok
                r.r = []
            else:
                r.r.append(tok)
        for w in writes:
            w.w = tok
            w.r = []
        self.last_tok[tok[0]] = tok

    def op(self, eng, fn, reads=(), writes=()):
        waits = self._need(eng, self._deps(reads, writes))
        self.cnt[eng] += 1
        c = self.cnt[eng]
        key = (eng, (c - 1) // EPOCH)
        tok = (key, (c - 1) % EPOCH + 1)
        self.semkeys.add(key)
        self._mark(tok, reads, writes)
        self.ops[eng].append((fn, waits, key, 1))
        return tok

    def dma(self, q, fn, reads=(), writes=(), is_output=False):
        deps = self._deps(reads, writes)
        i = self.dma_rr[q]
        self.dma_rr[q] = (i + 1) % self.dma_pool[q]
        key = ("dma_" + q, i)
        self.semkeys.add(key)
        prev = self.dma_uses[q][i]
        if prev > 0:
            deps.append((key, 16 * prev))
        self.dma_uses[q][i] = prev + 1
        tok = (key, 16 * (prev + 1))
        waits = self._need(q, deps)
        self._mark(tok, reads, writes)
        self.ops[q].append((fn, waits, key, 16))
        if is_output:
            self.out_tokens.append(tok)
        return tok

    def barrier(self):
        toks = list(self.last_tok.values())
        for e in ENGS:
            waits = self._need(e, toks)
            if waits:
                self.ops[e].append((None, waits, None, 0))

    def finish(self, q="sync"):
        waits = self._need(q, self.out_tokens)
        self.ops[q].append((None, waits, None, 0))

    def emit(self):
        nc = self.nc
        sems = {}
        for k in sorted(self.semkeys, key=str):
            sems[k] = self.stack.enter_context(nc.semaphore("s_%s_%d" % (k[0], k[1])))
        with nc.Block() as block:
            def make(engname):
                def body(e):
                    for fn, waits, key, inc in self.ops[engname]:
                        for (k, v) in waits:
                            e.wait_ge(sems[k], v)
                        if fn is not None:
                            fn(e).then_inc(sems[key], inc)
                return body
            block.sync(make("sync"))
            block.scalar(make("scalar"))
            block.vector(make("vector"))
            block.gpsimd(make("gpsimd"))
            block.tensor(make("tensor"))

    def close(self):
        self.stack.close()


class Arena:
    def __init__(self, P, name, kib):
        self.n = kib * 256
        self.t = P.sbuf(name, [128, self.n], F32)
        self.off = 0

    def reset(self, off=0):
        self.off = off

    def alloc(self, shape, dtype):
        n = 1
        for s in shape:
            n *= s
        nb = n * (2 if dtype == BF16 else 4)
        nw = (nb + 3) // 4
        nw = (nw + 7) // 8 * 8
        assert self.off + nw <= self.n, "arena overflow %d + %d > %d" % (self.off, nw, self.n)
        ap = self.t[:, self.off:self.off + nw]
        self.off += nw
        if dtype == BF16:
            ap = ap.bitcast(BF16)[:, 0:n]
        elif dtype == I32:
            ap = ap.bitcast(I32)[:, 0:n]
        else:
            ap = ap[:, 0:n]
        if len(shape) == 2:
            ap = ap.rearrange("p (a b) -> p a b", a=shape[0])
        elif len(shape) == 3:
            ap = ap.rearrange("p (a b c) -> p a b c", a=shape[0], b=shape[1])
        return ap


def build(NB, L, dbg=False, upto=None):
    NT = L // 128
    NG = L // 512
    assert L % 512 == 0
    nc = bass.Bass("TRN2", target_bir_lowering=False)
    x_d = nc.dram_tensor("x", [NB, L, D], F32, kind="ExternalInput").ap()
    pos_d = nc.dram_tensor("positions", [L], I32, kind="ExternalInput").ap()
    anw_d = nc.dram_tensor("attn_norm_w", [D], F32, kind="ExternalInput").ap()
    win_d = nc.dram_tensor("w_in", [D, DIN], F32, kind="ExternalInput").ap()
    convw_d = nc.dram_tensor("conv_w", [4, 1536], F32, kind="ExternalInput").ap()
    alog_d = nc.dram_tensor("a_log", [4], F32, kind="ExternalInput").ap()
    dtb_d = nc.dram_tensor("dt_bias", [4], F32, kind="ExternalInput").ap()
    dnw_d = nc.dram_tensor("dn_norm_w", [128], F32, kind="ExternalInput").ap()
    lq1_d = nc.dram_tensor("lambda_q1", [64], F32, kind="ExternalInput").ap()
    lk1_d = nc.dram_tensor("lambda_k1", [64], F32, kind="ExternalInput").ap()
    lq2_d = nc.dram_tensor("lambda_q2", [64], F32, kind="ExternalInput").ap()
    lk2_d = nc.dram_tensor("lambda_k2", [64], F32, kind="ExternalInput").ap()
    dfw_d = nc.dram_tensor("diff_norm_w", [128], F32, kind="ExternalInput").ap()
    gs_d = nc.dram_tensor("group_scale", [D], F32, kind="ExternalInput").ap()
    wout_d = nc.dram_tensor("w_out", [D, D], F32, kind="ExternalInput").ap()
    mnw_d = nc.dram_tensor("mlp_norm_w", [D], F32, kind="ExternalInput").ap()
    wup_d = nc.dram_tensor("w_up", [D, DFF], F32, kind="ExternalInput").ap()
    wdn_d = nc.dram_tensor("w_down", [DFF, D], F32, kind="ExternalInput").ap()
    fnw_d = nc.dram_tensor("final_norm_w", [D], F32, kind="ExternalInput").ap()
    y_d = nc.dram_tensor("y", [NB, L, D], F32, kind="ExternalOutput").ap()
    if dbg:
        dbg_d = nc.dram_tensor("dbg", [128, 8 * L], F32, kind="ExternalOutput").ap()
    win_s = nc.dram_tensor("win_s", [128, 8, DIN], BF16).ap()
    wout_s = nc.dram_tensor("wout_s", [128, 8, D], BF16).ap()
    wup_s = nc.dram_tensor("wup_s", [128, 8, DFF], BF16).ap()
    wdn_s = nc.dram_tensor("wdn_s", [128, 32, D], BF16).ap()

    P = Prog(nc)

    def early(tag):
        if upto != tag:
            return False
        P.barrier()
        zt = P.sbuf("zt_" + tag, [128, 64], F32)
        rz = Res()
        P.op("vector", lambda e: e.memset(zt[:], 1.0), writes=[rz])
        P.dma("sync", lambda e: e.dma_start(out=dbg_d[:, 0:64], in_=zt[:]), reads=[rz], is_output=True)
        P.finish(); P.emit(); P.close()
        return True
    V, S, G, T = "vector", "scalar", "gpsimd", "tensor"

    ident_bf = P.sbuf("ident_bf", [128, 128], BF16)
    ident_f = P.sbuf("ident_f", [128, 128], F32)
    ones_bf = P.sbuf("ones_bf", [128, 128], BF16)
    ones_f = P.sbuf("ones_f", [128, 128], F32)
    UI = P.sbuf("UI", [128, 128], F32)
    UI_bf = P.sbuf("UI_bf", [128, 128], BF16)
    negSL = P.sbuf("negSL", [128, 128], F32)
    cosT = P.sbuf("cosT", [128, NT, 8], F32)
    sinT = P.sbuf("sinT", [128, NT, 8], F32)
    cvec = P.sbuf("cvec", [128, 64], F32)
    convw = P.sbuf("convw", [128, 12, 4], F32)
    fnw_bc = P.sbuf("fnw_bc", [128, D], F32)
    R_const = Res()
    EPSC = cvec[:, 0:1]
    ONEC = cvec[:, 1:2]
    LAMN = cvec[:, 2:3]
    NEGA = cvec[:, 4:8]
    DTB = cvec[:, 8:12]
    EPSL2 = cvec[:, 12:13]

    main = Arena(P, "main", 199)
    hnT = P.sbuf("hnT", [128, 8, L], BF16) if False else None

    pb = [P.psum("pb%d" % i, [128, 512], F32) for i in range(7)]
    rpb = [Res(True) for _ in range(7)]
    ptr = P.psum("ptr", [128, 1024], BF16)
    rptr = Res(True)

    P.op(G, lambda e: e.memset(ident_f[:], 1.0), writes=[R_const])
    P.op(G, lambda e: e.affine_select(out=ident_f[:], in_=ident_f[:], pattern=[[1, 128]], compare_op=ALU.is_equal,
                                      fill=0.0, base=0, channel_multiplier=-1), reads=[R_const], writes=[R_const])
    P.op(V, lambda e: e.tensor_copy(out=ident_bf[:], in_=ident_f[:]), reads=[R_const], writes=[R_const])
    P.op(G, lambda e: e.memset(ones_f[:], 1.0), writes=[R_const])
    P.op(G, lambda e: e.memset(ones_bf[:], 1.0), writes=[R_const])
    P.op(G, lambda e: e.memset(UI[:], 1.0), writes=[R_const])
    P.op(G, lambda e: e.affine_select(out=UI[:], in_=UI[:], pattern=[[1, 128]], compare_op=ALU.is_ge,
                                      fill=0.0, base=0, channel_multiplier=-1), reads=[R_const], writes=[R_const])
    P.op(V, lambda e: e.tensor_copy(out=UI_bf[:], in_=UI[:]), reads=[R_const], writes=[R_const])
    P.op(G, lambda e: e.memset(negSL[:], -1.0), writes=[R_const])
    P.op(G, lambda e: e.affine_select(out=negSL[:], in_=negSL[:], pattern=[[-1, 128]], compare_op=ALU.is_gt,
                                      fill=0.0, base=0, channel_multiplier=1), reads=[R_const], writes=[R_const])
    P.op(V, lambda e: e.memset(cvec[:], 0.0), writes=[R_const])
    P.op(V, lambda e: e.memset(cvec[:, 0:1], EPS), reads=[R_const], writes=[R_const])
    P.op(V, lambda e: e.memset(cvec[:, 1:2], 1.0), reads=[R_const], writes=[R_const])
    P.op(V, lambda e: e.memset(cvec[:, 12:13], 1e-6), reads=[R_const], writes=[R_const])
    P.dma("sync", lambda e: e.dma_start(out=fnw_bc[:], in_=fnw_d.partition_broadcast(128)), writes=[R_const])
    P.dma("sync", lambda e: e.dma_start(out=cvec[:, 16:20], in_=alog_d.partition_broadcast(128)), reads=[R_const], writes=[R_const])
    P.dma("sync", lambda e: e.dma_start(out=cvec[:, 8:12], in_=dtb_d.partition_broadcast(128)), reads=[R_const], writes=[R_const])
    P.op(S, lambda e: e.activation(out=cvec[:, 20:24], in_=cvec[:, 16:20], func=AF.Exp), reads=[R_const], writes=[R_const])
    P.op(V, lambda e: e.tensor_scalar(out=cvec[:, 4:8], in0=cvec[:, 20:24], scalar1=-1.0, scalar2=None, op0=ALU.mult),
         reads=[R_const], writes=[R_const])

    if early('consts'):
        return nc
    main.reset()
    lt = main.alloc([4, 64], F32)
    R_s = Res()
    for i, d_ in enumerate((lq1_d, lk1_d, lq2_d, lk2_d)):
        P.dma("sync", lambda e, i=i, d_=d_: e.dma_start(out=lt[:, i, :], in_=d_.partition_broadcast(128)), reads=[R_s], writes=[R_s])
    lp = main.alloc([2, 64], F32)
    P.op(V, lambda e: e.tensor_tensor(out=lp[:, 0, :], in0=lt[:, 0, :], in1=lt[:, 1, :], op=ALU.mult), reads=[R_s], writes=[R_s])
    P.op(V, lambda e: e.tensor_tensor(out=lp[:, 1, :], in0=lt[:, 2, :], in1=lt[:, 3, :], op=ALU.mult), reads=[R_s], writes=[R_s])
    P.op(V, lambda e: e.reduce_sum(out=cvec[:, 24:25], in_=lp[:, 0, :], axis=mybir.AxisListType.X), reads=[R_s, R_const], writes=[R_const])
    P.op(V, lambda e: e.reduce_sum(out=cvec[:, 25:26], in_=lp[:, 1, :], axis=mybir.AxisListType.X), reads=[R_s, R_const], writes=[R_const])
    P.op(S, lambda e: e.activation(out=cvec[:, 26:28], in_=cvec[:, 24:26], func=AF.Exp), reads=[R_const], writes=[R_const])
    P.op(V, lambda e: e.tensor_tensor(out=cvec[:, 28:29], in0=cvec[:, 27:28], in1=cvec[:, 26:27], op=ALU.subtract), reads=[R_const], writes=[R_const])
    P.op(V, lambda e: e.tensor_scalar(out=cvec[:, 2:3], in0=cvec[:, 28:29], scalar1=-LAMBDA_INIT, scalar2=None, op0=ALU.add),
         reads=[R_const], writes=[R_const])
    cwr = main.alloc([1536], F32)
    P.dma("sync", lambda e: e.dma_start(out=cwr[0:4, :], in_=convw_d[:, :]), reads=[R_s], writes=[R_s])
    for c in range(12):
        P.op(T, lambda e, c=c: e.matmul(pb[0][:, c * 4:(c + 1) * 4], lhsT=cwr[0:4, c * 128:(c + 1) * 128], rhs=ident_f[0:4, 0:4],
                                         start=True, stop=True), reads=[R_s, R_const], writes=[rpb[0]])
    P.op(V, lambda e: e.tensor_copy(out=convw[:].rearrange("p a b -> p (a b)"), in_=pb[0][:, 0:48]), reads=[rpb[0]], writes=[R_const])
    posi = main.alloc([128], I32)
    posf = main.alloc([128], F32)
    P.dma("sync", lambda e: e.dma_start(out=posi[0:NT, :], in_=pos_d.rearrange("(n p) -> n p", p=128)), reads=[R_s], writes=[R_s])
    P.op(V, lambda e: e.tensor_copy(out=posf[0:NT, :], in_=posi[0:NT, :]), reads=[R_s], writes=[R_s])
    P.op(T, lambda e: e.matmul(pb[1][:, 0:NT], lhsT=posf[0:NT, :], rhs=ident_f[0:NT, 0:NT], start=True, stop=True),
         reads=[R_s, R_const], writes=[rpb[1]])
    post = main.alloc([NT], F32)
    P.op(V, lambda e: e.tensor_copy(out=post, in_=pb[1][:, 0:NT]), reads=[rpb[1]], writes=[R_s])
    invf = main.alloc([8], F32)
    for i in range(8):
        fr = float(np.float32(ROPE_THETA) ** np.float32(-(2.0 * i) / 16.0))
        P.op(V, lambda e, i=i, fr=fr: e.memset(invf[:, i:i + 1], fr), reads=[R_s], writes=[R_s])
    ang = main.alloc([NT, 8], F32)
    P.op(V, lambda e: e.tensor_tensor(out=ang, in0=post.unsqueeze(2).to_broadcast([128, NT, 8]),
                                      in1=invf.unsqueeze(1).to_broadcast([128, NT, 8]), op=ALU.mult), reads=[R_s], writes=[R_s])
    tA = main.alloc([NT, 8], F32)
    tB = main.alloc([NT, 8], F32)
    tI = main.alloc([NT, 8], I32)
    TWO_PI = 2.0 * math.pi
    for (shift, dst) in ((0.0, sinT), (math.pi / 2.0, cosT)):
        P.op(V, lambda e, shift=shift: e.tensor_scalar(out=tA, in0=ang, scalar1=shift, scalar2=None, op0=ALU.add), reads=[R_s], writes=[R_s])
        P.op(V, lambda e: e.tensor_scalar(out=tB, in0=tA, scalar1=1.0 / TWO_PI, scalar2=None, op0=ALU.mult), reads=[R_s], writes=[R_s])
        P.op(V, lambda e: e.tensor_copy(out=tI, in_=tB), reads=[R_s], writes=[R_s])
        P.op(V, lambda e: e.tensor_copy(out=tB, in_=tI), reads=[R_s], writes=[R_s])
        P.op(V, lambda e: e.scalar_tensor_tensor(out=tA, in0=tB, scalar=-TWO_PI, in1=tA, op0=ALU.mult, op1=ALU.add), reads=[R_s], writes=[R_s])
        P.op(V, lambda e: e.tensor_scalar(out=tB, in0=tA, scalar1=math.pi, scalar2=TWO_PI, op0=ALU.is_gt, op1=ALU.mult), reads=[R_s], writes=[R_s])
        P.op(V, lambda e: e.tensor_tensor(out=tA, in0=tA, in1=tB, op=ALU.subtract), reads=[R_s], writes=[R_s])
        P.op(V, lambda e: e.tensor_scalar(out=tB, in0=tA, scalar1=-math.pi, scalar2=TWO_PI, op0=ALU.is_lt, op1=ALU.mult), reads=[R_s], writes=[R_s])
        P.op(V, lambda e: e.tensor_tensor(out=tA, in0=tA, in1=tB, op=ALU.add), reads=[R_s], writes=[R_s])
        P.op(S, lambda e, dst=dst: e.activation(out=dst[:], in_=tA, func=AF.Sin), reads=[R_s, R_const], writes=[R_const])

    if early('setup'):
        return nc
    P.barrier()
    main.reset()
    rowv = P.sbuf("rowv", [128, 8, 4], F32)
    R_rv = Res()
    vraw = main.alloc([3, 128], F32)
    for wi, vd in enumerate((anw_d, mnw_d, gs_d)):
        P.dma("sync", lambda e, wi=wi, vd=vd: e.dma_start(out=vraw[0:8, wi, :], in_=vd.rearrange("(k p) -> k p", p=128)), reads=[R_rv], writes=[R_rv])
    for wi in range(3):
        P.op(T, lambda e, wi=wi: e.matmul(pb[0][:, wi * 8:(wi + 1) * 8], lhsT=vraw[0:8, wi, :], rhs=ident_f[0:8, 0:8], start=True, stop=True),
             reads=[R_rv, R_const], writes=[rpb[0]])
    P.op(V, lambda e: e.tensor_copy(out=rowv[:, :, 0:3], in_=pb[0][:, 0:24].rearrange("p (w k) -> p k w", w=3)), reads=[rpb[0]], writes=[R_rv])
    for k in range(8):
        src = dnw_d if k < 4 else dfw_d
        P.dma("sync", lambda e, k=k, src=src: e.dma_start(out=rowv[:, k, 3:4], in_=src.rearrange("(p o) -> p o", o=1)), reads=[R_rv], writes=[R_rv])
    P.op(V, lambda e: e.tensor_scalar(out=rowv[:, 4:8, 3:4], in0=rowv[:, 4:8, 3:4], scalar1=1.0 - LAMBDA_INIT, scalar2=None, op0=ALU.mult),
         reads=[R_rv], writes=[R_rv])
    P.op(V, lambda e: e.tensor_tensor(out=rowv[:, :, 2:3], in0=rowv[:, :, 2:3], in1=rowv[:, :, 3:4], op=ALU.mult), reads=[R_rv], writes=[R_rv])
    stg_f = [main.alloc([4096], F32) for _ in range(2)]
    stg_b = [main.alloc([4096], BF16) for _ in range(2)]
    R_sf = [Res(), Res()]
    R_sb = [Res(), Res()]
    R_win = Res()
    R_wrest = Res()
    jobs = []
    for k in range(8):
        jobs.append((win_d[k * 128:(k + 1) * 128, :], DIN, rowv[:, k, 0:1], win_s[:, k, :]))
    for j, (src, n, sc, dst) in enumerate(jobs):
        b = j % 2
        P.dma("sync" if j % 2 == 0 else "scalar", lambda e, b=b, src=src, n=n: e.dma_start(out=stg_f[b][:, 0:n], in_=src), writes=[R_sf[b]])
        if j % 2 == 0:
            P.op(S, lambda e, b=b, n=n, sc=sc: e.activation(out=stg_b[b][:, 0:n], in_=stg_f[b][:, 0:n], func=AF.Copy, scale=sc),
                 reads=[R_sf[b], R_rv], writes=[R_sb[b]])
        else:
            P.op(V, lambda e, b=b, n=n, sc=sc: e.tensor_scalar(out=stg_b[b][:, 0:n], in0=stg_f[b][:, 0:n], scalar1=sc, scalar2=None, op0=ALU.mult),
                 reads=[R_sf[b], R_rv], writes=[R_sb[b]])
        P.dma("gpsimd", lambda e, b=b, n=n, dst=dst: e.dma_start(out=dst, in_=stg_b[b][:, 0:n]), reads=[R_sb[b]], writes=[R_win])
    P.barrier()

    RST_W = 512
    rst_off = main.n - (2 * RST_W + 2 * RST_W // 2)
    rsf = [main.t[:, rst_off + i * RST_W:rst_off + (i + 1) * RST_W] for i in range(2)]
    rsb = [main.t[:, rst_off + 2 * RST_W + i * (RST_W // 2):rst_off + 2 * RST_W + (i + 1) * (RST_W // 2)].bitcast(BF16) for i in range(2)]
    R_rsf = [Res(), Res()]
    R_rsb = [Res(), Res()]
    rest_jobs = []
    for k in range(8):
        for c in range(D // RST_W):
            rest_jobs.append((wout_d[k * 128:(k + 1) * 128, c * RST_W:(c + 1) * RST_W], RST_W, rowv[:, k, 2:3], wout_s[:, k, c * RST_W:(c + 1) * RST_W]))
    for k in range(8):
        for c in range(DFF // RST_W):
            rest_jobs.append((wup_d[k * 128:(k + 1) * 128, c * RST_W:(c + 1) * RST_W], RST_W, rowv[:, k, 1:2], wup_s[:, k, c * RST_W:(c + 1) * RST_W]))
    for k in range(32):
        for c in range(D // RST_W):
            rest_jobs.append((wdn_d[k * 128:(k + 1) * 128, c * RST_W:(c + 1) * RST_W], RST_W, None, wdn_s[:, k, c * RST_W:(c + 1) * RST_W]))

    def prep_rest():
        for j, (src, n, sc, dst) in enumerate(rest_jobs):
            bb_ = j % 2
            P.dma("sync", lambda e, bb_=bb_, src=src, n=n: e.dma_start(out=rsf[bb_][:, 0:n], in_=src), writes=[R_rsf[bb_]])
            yield
            if sc is None:
                eng = G if j % 2 == 0 else V
                P.op(eng, lambda e, bb_=bb_, n=n: e.tensor_copy(out=rsb[bb_][:, 0:n], in_=rsf[bb_][:, 0:n]), reads=[R_rsf[bb_]], writes=[R_rsb[bb_]])
            else:
                P.op(V, lambda e, bb_=bb_, n=n, sc=sc: e.tensor_scalar(out=rsb[bb_][:, 0:n], in0=rsf[bb_][:, 0:n], scalar1=sc, scalar2=None, op0=ALU.mult),
                     reads=[R_rsf[bb_], R_rv], writes=[R_rsb[bb_]])
            yield
            P.dma("sync", lambda e, bb_=bb_, n=n, dst=dst: e.dma_start(out=dst, in_=rsb[bb_][:, 0:n]), reads=[R_rsb[bb_]], writes=[R_wrest])
            yield

    if early('wprep'):
        return nc
    def rstd_from_ss(ss, tmp, out, scale):
        P.op(S, lambda e: e.activation(out=tmp, in_=ss, func=AF.Ln, scale=scale, bias=EPSC), reads=[R_st, R_const], writes=[R_st])
        P.op(S, lambda e: e.activation(out=out, in_=tmp, func=AF.Exp, scale=-0.5), reads=[R_st], writes=[R_st])

    bg = [None]

    def bg_step():
        if bg[0] is not None:
            try:
                next(bg[0])
            except StopIteration:
                bg[0] = None

    bg2 = [None]

    def bg2_step():
        if bg2[0] is not None:
            try:
                next(bg2[0])
            except StopIteration:
                bg2[0] = None

    def run_rr(gens, use_bg=False):
        gens = [g for g in gens if g is not None]
        while gens:
            for g in list(gens):
                try:
                    next(g)
                except StopIteration:
                    gens.remove(g)
            if use_bg:
                bg_step()


    for b in range(NB):
        if b == 0:
            bg[0] = prep_rest()
        main.reset()
        hnT = main.alloc([8, L], BF16)
        mixT = main.alloc([8, L], BF16)
        R_hnT = [Res() for _ in range(NT)]
        R_mixT = [[Res() for _ in range(NT)] for _ in range(8)]
        base_off0 = main.off
        siluz = main.alloc([NT, 512], BF16)
        R_sz = [Res() for _ in range(NT)]
        ba = main.alloc([NT, 8], F32)
        R_ba = Res()
        sm = main.alloc([12, NT, 4], F32)
        R_sm = Res()
        base_off = main.off
        dfq0 = main.alloc([4, L], BF16)
        dfq1 = main.alloc([4, L], BF16)
        dfkT = main.alloc([4, L], BF16)
        dfv = main.alloc([NT, 4, 130], BF16)
        R_dfq = [Res() for _ in range(NT)]
        R_dfk = [Res() for _ in range(NT)]
        R_dfv = [Res() for _ in range(NT)]
        wblk = [main.alloc([8, 512], BF16) for _ in range(3)]
        R_wblk = [Res(), Res(), Res()]
        qtok = [main.alloc([512], BF16) for _ in range(2)]
        R_qtok = [Res(), Res()]
        rt = [main.alloc([8, 8], F32) for _ in range(4)]
        R_rt = Res()
        off_A = main.off
        xt = [main.alloc([D], F32) for _ in range(2)]
        R_xt = [Res(), Res()]
        hnb = [main.alloc([D], BF16) for _ in range(2)]
        R_hnb = [Res(), Res()]
        st = main.alloc([NT, 4], F32)
        R_st = Res()
        P.op(G, lambda e: e.memset(dfv[:, :, :, 128:130], 1.0), writes=R_dfv)
        P.op(G, lambda e: e.memset(dfq0[64:128], 0.0), writes=R_dfq)
        P.op(G, lambda e: e.memset(dfq1[0:64], 0.0), writes=R_dfq)
        COL_Q, COL_K, COL_V = 2056, 2568, 3080
        for bi, col0 in enumerate((COL_Q, COL_K, COL_V)):
            P.dma("sync", lambda e, bi=bi, col0=col0: e.dma_start(out=wblk[bi], in_=win_s[:, :, col0:col0 + 512]), reads=[R_win], writes=[R_wblk[bi]])

        def genA(n):
            i2 = n % 2
            P.dma("sync", lambda e, b=b: e.dma_start(out=xt[i2], in_=x_d[b, n * 128:(n + 1) * 128, :]), writes=[R_xt[i2]])
            P.op(S, lambda e: e.activation(out=hnb[i2], in_=xt[i2], func=AF.Square, accum_out=st[:, n, 0:1]), reads=[R_xt[i2]], writes=[R_hnb[i2], R_st])
            yield
            P.op(S, lambda e: e.activation(out=st[:, n, 1:2], in_=st[:, n, 0:1], func=AF.Ln, scale=1.0 / D, bias=EPSC), reads=[R_st, R_const], writes=[R_st])
            yield
            P.op(S, lambda e: e.activation(out=st[:, n, 2:3], in_=st[:, n, 1:2], func=AF.Exp, scale=-0.5), reads=[R_st], writes=[R_st])
            yield
            P.op(V, lambda e: e.tensor_scalar(out=hnb[i2], in0=xt[i2], scalar1=st[:, n, 2:3], scalar2=None, op0=ALU.mult),
                 reads=[R_xt[i2], R_st], writes=[R_hnb[i2]])
            yield
            for k in range(8):
                P.op(T, lambda e, k=k: e.transpose(out=ptr[:, k * 128:(k + 1) * 128], in_=hnb[i2][:, k * 128:(k + 1) * 128], identity=ident_bf[:]),
                     reads=[R_hnb[i2], R_const], writes=[rptr])
            P.op(S, lambda e: e.activation(out=hnT[:, :, n * 128:(n + 1) * 128], in_=ptr[:].rearrange("p (k t) -> p k t", k=8), func=AF.Copy),
                 reads=[rptr], writes=[R_hnT[n]])
            yield

        pcount = [0]

        def genDF(n):
            for bi, kind in enumerate(("q", "k", "v")):
                pbi = pcount[0] % 2
                pcount[0] += 1
                ps = pb[pbi]
                rps = rpb[pbi]
                for k in range(8):
                    P.op(T, lambda e, k=k, ps=ps, bi=bi: e.matmul(ps[:, :], lhsT=hnT[:, k, n * 128:(n + 1) * 128], rhs=wblk[bi][:, k, :],
                                                                  start=(k == 0), stop=(k == 7)),
                         reads=[R_hnT[n], R_wblk[bi]], writes=[rps])
                yield
                if kind == "v":
                    P.op(S, lambda e, ps=ps: e.activation(out=dfv[:, n, :, 0:128], in_=ps[:, :].rearrange("p (h d) -> p h d", h=4), func=AF.Copy),
                         reads=[rps], writes=[R_dfv[n]])
                    yield
                    continue
                qi = pcount[0] % 2
                qt = qtok[qi]
                Rq = R_qtok[qi]
                P.op(S, lambda e, ps=ps, qt=qt: e.activation(out=qt, in_=ps[:, :], func=AF.Copy), reads=[rps], writes=[Rq])
                yield
                ps3 = ps[:, :].rearrange("p (g d) -> p g d", g=8)
                qt3 = qt.rearrange("p (g d) -> p g d", g=8)
                cb = cosT[:, n, :].unsqueeze(1).to_broadcast([128, 8, 8])
                sb = sinT[:, n, :].unsqueeze(1).to_broadcast([128, 8, 8])
                x1 = ps3[:, :, 0:8]
                x2 = ps3[:, :, 8:16]
                P.op(V, lambda e, x1=x1, cb=cb: e.tensor_tensor(out=rt[0], in0=x1, in1=cb, op=ALU.mult), reads=[rps, R_const], writes=[R_rt])
                P.op(V, lambda e, x2=x2, sb=sb: e.tensor_tensor(out=rt[1], in0=x2, in1=sb, op=ALU.mult), reads=[rps, R_const], writes=[R_rt])
                yield
                P.op(V, lambda e, x2=x2, cb=cb: e.tensor_tensor(out=rt[2], in0=x2, in1=cb, op=ALU.mult), reads=[rps, R_const], writes=[R_rt])
                P.op(V, lambda e, x1=x1, sb=sb: e.tensor_tensor(out=rt[3], in0=x1, in1=sb, op=ALU.mult), reads=[rps, R_const], writes=[R_rt])
                yield
                P.op(V, lambda e, qt3=qt3: e.tensor_tensor(out=qt3[:, :, 0:8], in0=rt[0], in1=rt[1], op=ALU.subtract), reads=[R_rt], writes=[Rq])
                P.op(V, lambda e, qt3=qt3: e.tensor_tensor(out=qt3[:, :, 8:16], in0=rt[2], in1=rt[3], op=ALU.add), reads=[R_rt], writes=[Rq])
                yield
                for h in range(4):
                    P.op(T, lambda e, h=h, qt=qt: e.transpose(out=ptr[:, h * 128:(h + 1) * 128], in_=qt[:, h * 128:(h + 1) * 128], identity=ident_bf[:]),
                         reads=[Rq, R_const], writes=[rptr])
                if kind == "q":
                    P.op(V, lambda e: e.tensor_copy(out=dfq0[0:64, :, n * 128:(n + 1) * 128], in_=ptr[0:64, 0:512].rearrange("p (h t) -> p h t", h=4)),
                         reads=[rptr], writes=[R_dfq[n]])
                    P.op(V, lambda e: e.tensor_copy(out=dfq1[64:128, :, n * 128:(n + 1) * 128], in_=ptr[64:128, 0:512].rearrange("p (h t) -> p h t", h=4)),
                         reads=[rptr], writes=[R_dfq[n]])
                else:
                    P.op(V, lambda e: e.tensor_copy(out=dfkT[:, :, n * 128:(n + 1) * 128], in_=ptr[:, 0:512].rearrange("p (h t) -> p h t", h=4)),
                         reads=[rptr], writes=[R_dfk[n]])
                yield

        run_rr([genA(0)], use_bg=True)
        for n in range(NT):
            run_rr([genA(n + 1) if n + 1 < NT else None, genDF(n)], use_bg=True)
        assert main.off <= rst_off, (main.off, rst_off)
        main.reset(off_A)
        P.barrier()

        if early('DFproj'):
            return nc
        PT = [main.alloc([2, 256], BF16) for _ in range(2)]
        R_PT = [Res(), Res()]
        dtmp = main.alloc([128], F32)
        R_dt = Res()
        mxb = main.alloc([128], BF16)
        R_mxb = Res()
        fst = main.alloc([16], F32)
        R_fst = Res()
        stb = [pb[0], pb[1]]
        rstb = [rpb[0], rpb[1]]
        accp = [[pb[2], pb[3]], [pb[4], pb[5]]]
        raccp = [[rpb[2], rpb[3]], [rpb[4], rpb[5]]]
        dfq = [dfq0, dfq1]
        NQG = NT // 2
        iters = [(h, qg, kb) for h in range(4) for qg in range(NQG) for kb in range(2 * qg + 2)]

        def df_ST(j):
            h, qg, kb = iters[j]
            bufi = j % 2
            i0 = max(0, kb - 2 * qg)
            c0 = i0 * 128
            for m in range(2):
                P.op(T, lambda e, m=m: e.matmul(
                    stb[bufi][:, m * 256 + c0:(m + 1) * 256], lhsT=dfkT[:, h, kb * 128:(kb + 1) * 128],
                    rhs=dfq[m][:, h, qg * 256 + c0:(qg + 1) * 256], start=True, stop=True),
                    reads=[R_dfk[kb]] + R_dfq[qg * 2 + i0:qg * 2 + 2], writes=[rstb[bufi]])

        def df_EXP(j):
            h, qg, kb = iters[j]
            bufi = j % 2
            c0 = max(0, kb - 2 * qg) * 128
            P.op(S, lambda e: e.activation(out=PT[bufi][:, :, c0:256], in_=stb[bufi][:, :].rearrange("p (m q) -> p m q", m=2)[:, :, c0:256],
                                           func=AF.Exp, scale=0.125), reads=[rstb[bufi]], writes=[R_PT[bufi]])
            if kb >= 2 * qg:
                for m in range(2):
                    P.op(G, lambda e, m=m: e.tensor_tensor(out=PT[bufi][:, m, c0:c0 + 128], in0=PT[bufi][:, m, c0:c0 + 128], in1=UI_bf[:], op=ALU.mult),
                         reads=[R_PT[bufi], R_const], writes=[R_PT[bufi]])

        NRING = 4
        stg = [main.alloc([2, 130], F32) for _ in range(NRING)]
        dring = [main.alloc([128], F32) for _ in range(NRING)]
        mring = [main.alloc([128], BF16) for _ in range(NRING)]
        fring = [main.alloc([8], F32) for _ in range(NRING)]
        R_ring = [Res() for _ in range(NRING)]
        fin_cnt = [0]
        pending = []

        def df_FIN(h, n, i, j):
            r = fin_cnt[0] % NRING
            fin_cnt[0] += 1
            sg, dd, mm_, ff, Rr = stg[r], dring[r], mring[r], fring[r], R_ring[r]
            for m in range(2):
                P.op(V, lambda e, m=m: e.tensor_copy(out=sg[:, m, :], in_=accp[m][i][:, 0:130]), reads=[raccp[m][i]], writes=[Rr])

            def part2():
                P.op(V, lambda e: e.reciprocal(out=ff[:, 0:2], in_=sg[:, :, 128:129].rearrange("p m o -> p (m o)")), reads=[Rr], writes=[Rr])
                P.op(V, lambda e: e.tensor_tensor(out=ff[:, 2:3], in0=ff[:, 1:2], in1=LAMN, op=ALU.mult), reads=[Rr, R_const], writes=[Rr])
                P.op(V, lambda e: e.tensor_scalar(out=dd, in0=sg[:, 0, 0:128], scalar1=ff[:, 0:1], scalar2=None, op0=ALU.mult), reads=[Rr], writes=[Rr])
                P.op(V, lambda e: e.scalar_tensor_tensor(out=dd, in0=sg[:, 1, 0:128], scalar=ff[:, 2:3], in1=dd, op0=ALU.mult, op1=ALU.add), reads=[Rr], writes=[Rr])

            def part3():
                P.op(S, lambda e: e.activation(out=mm_, in_=dd, func=AF.Square, accum_out=ff[:, 3:4]), reads=[Rr], writes=[Rr])
                P.op(S, lambda e: e.activation(out=ff[:, 4:5], in_=ff[:, 3:4], func=AF.Ln, scale=1.0 / 128, bias=EPSC), reads=[Rr, R_const], writes=[Rr])
                P.op(S, lambda e: e.activation(out=ff[:, 5:6], in_=ff[:, 4:5], func=AF.Exp, scale=-0.5), reads=[Rr], writes=[Rr])

            def part4():
                P.op(V, lambda e: e.tensor_scalar(out=mm_, in0=dd, scalar1=ff[:, 5:6], scalar2=None, op0=ALU.mult), reads=[Rr], writes=[Rr])
                P.op(T, lambda e: e.transpose(out=ptr[:, 0:128], in_=mm_, identity=ident_bf[:]), reads=[Rr, R_const], writes=[rptr])
                P.op(V, lambda e: e.tensor_copy(out=mixT[:, 4 + h, n * 128:(n + 1) * 128], in_=ptr[:, 0:128]), reads=[rptr], writes=[R_mixT[4 + h][n]])
            pending.append((j + 1, part2))
            pending.append((j + 2, part3))
            pending.append((j + 3, part4))

        def df_flush(j):
            keep = []
            for (due, fn) in pending:
                if due <= j:
                    fn()
                else:
                    keep.append((due, fn))
            pending[:] = keep

        def df_PV(j):
            h, qg, kb = iters[j]
            bufi = j % 2
            i0 = max(0, kb - 2 * qg)
            for m in range(2):
                for i in range(i0, 2):
                    P.op(T, lambda e, m=m, i=i: e.matmul(
                        accp[m][i][:, 0:129], lhsT=PT[bufi][:, m, i * 128:(i + 1) * 128], rhs=dfv[:, kb, h, 0:129],
                        start=(kb == 0), stop=(kb == 2 * qg + i)),
                        reads=[R_PT[bufi], R_dfv[kb]], writes=[raccp[m][i]])
            if kb >= 2 * qg:
                df_FIN(h, kb, kb - 2 * qg, j)

        wz = wblk[0]
        R_wz = R_wblk[0]
        wba = main.alloc([8, 8], BF16)
        R_wba = Res()
        etmp = main.alloc([512], F32)
        R_et = Res()

        def dn_prolog():
            P.dma("sync", lambda e: e.dma_start(out=wz, in_=win_s[:, :, 1536:2048]), reads=[R_win], writes=[R_wz])
            P.dma("sync", lambda e: e.dma_start(out=wba, in_=win_s[:, :, 2048:2056]), reads=[R_win], writes=[R_wba])
            yield
            for n in range(NT):
                for k in range(8):
                    P.op(T, lambda e, k=k, n=n: e.matmul(pb[6][:, :], lhsT=hnT[:, k, n * 128:(n + 1) * 128], rhs=wz[:, k, :], start=(k == 0), stop=(k == 7)),
                         reads=[R_hnT[n], R_wz], writes=[rpb[6]])
                yield
                P.op(S, lambda e: e.activation(out=etmp, in_=pb[6][:, :], func=AF.Exp, scale=-1.0), reads=[rpb[6]], writes=[R_et])
                yield
                P.op(V, lambda e: e.tensor_scalar(out=etmp, in0=etmp, scalar1=1.0, scalar2=None, op0=ALU.add), reads=[R_et], writes=[R_et])
                P.op(V, lambda e: e.reciprocal(out=etmp, in_=etmp), reads=[R_et], writes=[R_et])
                yield
                P.op(V, lambda e, n=n: e.tensor_tensor(out=siluz[:, n, :], in0=pb[6][:, :], in1=etmp, op=ALU.mult), reads=[rpb[6], R_et], writes=[R_sz[n]])
                yield
                for k in range(8):
                    P.op(T, lambda e, k=k, n=n: e.matmul(pb[6][:, 0:8], lhsT=hnT[:, k, n * 128:(n + 1) * 128], rhs=wba[:, k, :], start=(k == 0), stop=(k == 7)),
                         reads=[R_hnT[n], R_wba], writes=[rpb[6]])
                P.op(V, lambda e, n=n: e.tensor_copy(out=ba[:, n, :], in_=pb[6][:, 0:8]), reads=[rpb[6]], writes=[R_ba])
                yield
            bb = ba[:, :, 0:4]
            aa = ba[:, :, 4:8]
            dtb_b = DTB.unsqueeze(1).to_broadcast([128, NT, 4])
            nega_b = NEGA.unsqueeze(1).to_broadcast([128, NT, 4])

            def smop(eng, fn, extra=()):
                P.op(eng, fn, reads=[R_sm, R_ba, R_const] + list(extra), writes=[R_sm])
            smop(S, lambda e: e.activation(out=sm[:, 0], in_=bb, func=AF.Abs))
            yield
            smop(S, lambda e: e.activation(out=sm[:, 1], in_=sm[:, 0], func=AF.Exp, scale=-1.0))
            yield
            smop(S, lambda e: e.activation(out=sm[:, 2], in_=sm[:, 1], func=AF.Ln, bias=ONEC))
            yield
            smop(V, lambda e: e.scalar_tensor_tensor(out=sm[:, 3], in0=bb, scalar=0.0, in1=sm[:, 2], op0=ALU.min, op1=ALU.subtract))
            smop(V, lambda e: e.tensor_tensor(out=sm[:, 4], in0=aa, in1=dtb_b, op=ALU.add))
            yield
            smop(S, lambda e: e.activation(out=sm[:, 0], in_=sm[:, 4], func=AF.Abs))
            yield
            smop(S, lambda e: e.activation(out=sm[:, 1], in_=sm[:, 0], func=AF.Exp, scale=-1.0))
            yield
            smop(S, lambda e: e.activation(out=sm[:, 2], in_=sm[:, 1], func=AF.Ln, bias=ONEC))
            yield
            smop(V, lambda e: e.scalar_tensor_tensor(out=sm[:, 5], in0=sm[:, 4], scalar=0.0, in1=sm[:, 2], op0=ALU.max, op1=ALU.add))
            smop(V, lambda e: e.tensor_tensor(out=sm[:, 6], in0=sm[:, 5], in1=nega_b, op=ALU.mult))
            yield
            gflat = sm[:, 6].rearrange("p n h -> p (n h)")
            P.op(T, lambda e: e.matmul(pb[6][:, 0:NT * 4], lhsT=UI[:], rhs=gflat, start=True, stop=True), reads=[R_sm, R_const], writes=[rpb[6]])
            smop(V, lambda e: e.tensor_copy(out=sm[:, 7].rearrange("p n h -> p (n h)"), in_=pb[6][:, 0:NT * 4]), extra=[rpb[6]])
            yield
            P.op(T, lambda e: e.matmul(pb[6][:, 0:NT * 4], lhsT=ones_f[:], rhs=gflat, start=True, stop=True), reads=[R_sm, R_const], writes=[rpb[6]])
            smop(V, lambda e: e.tensor_copy(out=sm[:, 8].rearrange("p n h -> p (n h)"), in_=pb[6][:, 0:NT * 4]), extra=[rpb[6]])
            yield
            smop(V, lambda e: e.tensor_tensor(out=sm[:, 9], in0=sm[:, 3], in1=sm[:, 7], op=ALU.add))
            smop(V, lambda e: e.tensor_tensor(out=sm[:, 10], in0=sm[:, 8], in1=sm[:, 7], op=ALU.subtract))
            yield
            smop(S, lambda e: e.activation(out=sm[:, 11], in_=sm[:, 9], func=AF.Exp))
            yield
            smop(S, lambda e: e.activation(out=sm[:, 10], in_=sm[:, 10], func=AF.Exp))
            yield
            smop(S, lambda e: e.activation(out=sm[:, 8], in_=sm[:, 8], func=AF.Exp))
            yield
            smop(S, lambda e: e.activation(out=sm[:, 3], in_=sm[:, 3], func=AF.Exp))
            yield

        bg2[0] = dn_prolog()
        df_ST(0)
        for j in range(len(iters)):
            if j + 1 < len(iters):
                df_ST(j + 1)
            df_EXP(j)
            df_PV(j)
            df_flush(j)
            bg_step()
            bg2_step()
        df_flush(10 ** 9)
        while bg[0] is not None:
            bg_step()

        while bg2[0] is not None:
            bg2_step()
        P.barrier()
        main.reset(base_off)
        GC, HC, C1, C2, DEC, BETA = sm[:, 7], sm[:, 9], sm[:, 11], sm[:, 10], sm[:, 8], sm[:, 3]

        NBt = NT // 4
        HB = []
        for _i in range(2):
            HB.append(dict(qT=main.alloc([L], BF16), kT=main.alloc([L], BF16), vT=main.alloc([L], BF16),
                           kbg=main.alloc([NT, 128], BF16), ksc=main.alloc([NT, 128], BF16), vb=main.alloc([NT, 128], BF16),
                           wqkv=main.alloc([3, 8, 128], BF16),
                           R_q=Res(), R_k=Res(), R_v=Res(), R_tok=Res(), R_w=Res()))
        xc = main.alloc([L + 4], F32)
        R_xc = Res()
        yv = main.alloc([L], F32)
        R_yv = Res()
        sq = main.alloc([L], BF16)
        R_sq = Res()
        rn = main.alloc([512], F32)
        R_rn = Res()
        dA = main.alloc([4, 128], F32)
        dB = main.alloc([4, 128], F32)
        dC = main.alloc([4, 128], F32)
        dD = main.alloc([4, 128], F32)
        R_dA, R_dB, R_dC, R_dD = Res(), Res(), Res(), Res()
        Yb = [main.alloc([4, 128], F32) for _ in range(2)]
        Zb = [main.alloc([4, 128], F32) for _ in range(2)]
        R_Y = [Res(), Res()]
        R_Z = [Res(), Res()]
        Nm = main.alloc([4, 128], F32)
        R_N = Res()
        PB = []
        for _i in range(2):
            PB.append(dict(TTb=main.alloc([4, 128], BF16), nwT=main.alloc([4, 128], BF16), qsT=main.alloc([4, 128], BF16),
                           QKm=main.alloc([4, 128], BF16), R_TT=Res(), R_nw=Res(), R_qs=Res(), R_QK=Res()))
        vnb = main.alloc([128], BF16)
        R_vn = Res()
        Sf = main.alloc([128], F32)
        Sb = main.alloc([128], BF16)
        R_Sf, R_Sb = Res(), Res()
        mxd = main.alloc([128], BF16)
        R_mxd = Res()
        fs = main.alloc([8], F32)
        R_fs = Res()
        jk = main.alloc([128], BF16)
        R_jk = Res()
        P.op(V, lambda e: e.memset(xc[:, 0:4], 0.0), writes=[R_xc])
        idb = ident_f[:].unsqueeze(1).to_broadcast([128, 4, 128])
        negSLb = negSL[:].unsqueeze(1).to_broadcast([128, 4, 128])
        UIb = UI[:].unsqueeze(1).to_broadcast([128, 4, 128])
        BK_PREP = 0
        BK_G = 0
        BK_KQ = 0
        HBK = [(1, 2, 3), (4, 5, 6)]
        ptrf = ptr[:].bitcast(F32)

        qk_done = [False]

        def dn_prep(h):
            hb = HB[h % 2]
            qk_done[0] = False
            for c in range(3):
                P.dma("sync", lambda e, c=c: e.dma_start(out=hb["wqkv"][:, c], in_=win_s[:, :, c * 512 + h * 128:c * 512 + (h + 1) * 128]),
                      reads=[R_win], writes=[hb["R_w"]])
            yield
            for c in range(3):
                cc = c * 4 + h
                if c == 2:
                    qk_done[0] = True
                for grp in range(NG):
                    for k in range(8):
                        P.op(T, lambda e, c=c, k=k, grp=grp: e.matmul(pb[BK_PREP][:, :], lhsT=hb["wqkv"][:, c, k, :], rhs=hnT[:, k, grp * 512:(grp + 1) * 512],
                                                                      start=(k == 0), stop=(k == 7)),
                             reads=[hb["R_w"]] + R_hnT[grp * 4:grp * 4 + 4], writes=[rpb[BK_PREP]])
                        if k == 3:
                            yield
                    P.op(S, lambda e, grp=grp: e.activation(out=xc[:, 3 + grp * 512:3 + (grp + 1) * 512], in_=pb[BK_PREP][:, :], func=AF.Copy),
                         reads=[rpb[BK_PREP]], writes=[R_xc])
                    yield
                P.op(V, lambda e, cc=cc: e.tensor_scalar(out=yv, in0=xc[:, 3:3 + L], scalar1=convw[:, cc, 3:4], scalar2=None, op0=ALU.mult),
                     reads=[R_xc, R_const], writes=[R_yv])
                yield
                for j in (2, 1, 0):
                    P.op(V, lambda e, cc=cc, j=j: e.scalar_tensor_tensor(out=yv, in0=xc[:, j:j + L], scalar=convw[:, cc, j:j + 1], in1=yv,
                                                                        op0=ALU.mult, op1=ALU.add), reads=[R_xc, R_const, R_yv], writes=[R_yv])
                    yield
                if c == 2:
                    P.op(S, lambda e: e.activation(out=hb["vT"], in_=yv, func=AF.Silu), reads=[R_yv], writes=[hb["R_v"]])
                    yield
                    continue
                P.op(S, lambda e: e.activation(out=yv, in_=yv, func=AF.Silu), reads=[R_yv], writes=[R_yv])
                yield
                P.op(G, lambda e: e.tensor_tensor(out=sq, in0=yv, in1=yv, op=ALU.mult), reads=[R_yv], writes=[R_sq])
                yield
                dstT, Rd, qscale = (hb["qT"], hb["R_q"], 128 ** -0.5) if c == 0 else (hb["kT"], hb["R_k"], 1.0)
                for grp in range(NG):
                    gs_ = slice(grp * 512, (grp + 1) * 512)
                    P.op(T, lambda e, gs_=gs_: e.matmul(pb[BK_PREP][:, :], lhsT=ones_bf[:], rhs=sq[:, gs_], start=True, stop=True),
                         reads=[R_sq, R_const], writes=[rpb[BK_PREP]])
                    P.op(S, lambda e: e.activation(out=rn, in_=pb[BK_PREP][:, :], func=AF.Ln, bias=EPSL2), reads=[rpb[BK_PREP], R_const], writes=[R_rn])
                    yield
                    P.op(S, lambda e: e.activation(out=rn, in_=rn, func=AF.Exp, scale=-0.5), reads=[R_rn], writes=[R_rn])
                    yield
                    P.op(V, lambda e, gs_=gs_, dstT=dstT, qscale=qscale: e.scalar_tensor_tensor(out=dstT[:, gs_], in0=yv[:, gs_], scalar=qscale, in1=rn,
                                                                                               op0=ALU.mult, op1=ALU.mult),
                         reads=[R_yv, R_rn], writes=[Rd])
                    yield
            for n in range(NT):
                ts_ = slice(n * 128, (n + 1) * 128)
                P.op(T, lambda e, ts_=ts_: e.transpose(out=ptr[:, 0:128], in_=hb["kT"][:, ts_], identity=ident_bf[:]), reads=[hb["R_k"], R_const], writes=[rptr])
                P.op(T, lambda e, ts_=ts_: e.transpose(out=ptr[:, 128:256], in_=hb["vT"][:, ts_], identity=ident_bf[:]), reads=[hb["R_v"], R_const], writes=[rptr])
                P.op(S, lambda e, n=n: e.activation(out=hb["kbg"][:, n, :], in_=ptr[:, 0:128], func=AF.Copy, scale=C1[:, n, h:h + 1]),
                     reads=[rptr, R_sm], writes=[hb["R_tok"]])
                P.op(V, lambda e, n=n: e.tensor_scalar(out=hb["ksc"][:, n, :], in0=ptr[:, 0:128], scalar1=C2[:, n, h:h + 1], scalar2=None, op0=ALU.mult),
                     reads=[rptr, R_sm], writes=[hb["R_tok"]])
                P.op(S, lambda e, n=n: e.activation(out=hb["vb"][:, n, :], in_=ptr[:, 128:256], func=AF.Copy, scale=BETA[:, n, h:h + 1]),
                     reads=[rptr, R_sm], writes=[hb["R_tok"]])
                yield

        R_dAh = [Res(), Res()]
        R_dBh = [Res(), Res()]
        R_dCh = [Res(), Res()]
        R_dDh = [Res(), Res()]
        R_Yh = [[Res(), Res()], [Res(), Res()]]
        R_Zh = [[Res(), Res()], [Res(), Res()]]
        R_Nh = [Res(), Res()]

        def dn_pre(h, jb, hf):
            hb = HB[h % 2]
            pbf = PB[(h * NBt + jb) % 2]
            BY, BZ, BN = HBK[hf]
            t0_ = hf * 2
            n0 = jb * 4 + t0_
            tv = slice(t0_, t0_ + 2)
            sl = slice(n0 * 128, (n0 + 2) * 128)
            gcb = GC[:, n0:n0 + 2, h:h + 1].to_broadcast([128, 2, 128])
            hcb = HC[:, n0:n0 + 2, h:h + 1].to_broadcast([128, 2, 128])
            idb2 = ident_f[:].unsqueeze(1).to_broadcast([128, 2, 128])
            nsl2 = negSL[:].unsqueeze(1).to_broadcast([128, 2, 128])
            ui2 = UI[:].unsqueeze(1).to_broadcast([128, 2, 128])
            kT, qT = hb["kT"], hb["qT"]
            Rpk = [pbf["R_TT"], pbf["R_nw"], pbf["R_qs"], pbf["R_QK"]]
            cols = slice(t0_ * 128, (t0_ + 2) * 128)

            def bv(bank):
                return pb[bank][:, cols].rearrange("p (i t) -> p i t", i=2)

            def mm2(bank, lhs_fn, rhs_fn, reads):
                for i in range(2):
                    P.op(T, lambda e, i=i: e.matmul(pb[bank][:, (t0_ + i) * 128:(t0_ + i + 1) * 128], lhsT=lhs_fn(i), rhs=rhs_fn(i), start=True, stop=True),
                         reads=reads, writes=[rpb[bank]])
            dAv, dBv, dCv, dDv, Nv = dA[:, tv, :], dB[:, tv, :], dC[:, tv, :], dD[:, tv, :], Nm[:, tv, :]
            Yv = [Yb[0][:, tv, :], Yb[1][:, tv, :]]
            Zv = [Zb[0][:, tv, :], Zb[1][:, tv, :]]
            RA, RB, RC, RD, RN = R_dAh[hf], R_dBh[hf], R_dCh[hf], R_dDh[hf], R_Nh[hf]
            RY, RZ = R_Yh[hf], R_Zh[hf]
            P.op(V, lambda e: e.tensor_tensor(out=dAv, in0=idb2, in1=gcb, op=ALU.mult), reads=[R_const, R_sm], writes=[RA])
            yield
            mm2(BN, lambda i: ones_f[:], lambda i: dAv[:, i, :], [RA, R_const])
            P.op(V, lambda e: e.tensor_tensor(out=dBv, in0=bv(BN), in1=hcb, op=ALU.subtract), reads=[rpb[BN], R_sm], writes=[RB])
            P.op(V, lambda e: e.tensor_tensor(out=dCv, in0=bv(BN), in1=gcb, op=ALU.subtract), reads=[rpb[BN], R_sm], writes=[RC])
            yield
            P.op(S, lambda e: e.activation(out=dBv, in_=dBv, func=AF.Exp, scale=-1.0), reads=[RB], writes=[RB])
            yield
            P.op(S, lambda e: e.activation(out=dCv, in_=dCv, func=AF.Exp), reads=[RC], writes=[RC])
            P.op(S, lambda e: e.activation(out=dDv, in_=bv(BN), func=AF.Exp), reads=[rpb[BN]], writes=[RD])
            yield
            P.op(V, lambda e: e.scalar_tensor_tensor(out=dBv, in0=dBv, scalar=1.0, in1=nsl2, op0=ALU.min, op1=ALU.mult), reads=[RB, R_const], writes=[RB])
            yield
            P.op(V, lambda e: e.scalar_tensor_tensor(out=dCv, in0=dCv, scalar=1.0, in1=ui2, op0=ALU.min, op1=ALU.mult), reads=[RC, R_const], writes=[RC])
            yield
            P.op(G, lambda e: e.tensor_tensor(out=pbf["qsT"][:, tv, :], in0=qT[:, sl].rearrange("p (i t) -> p i t", i=2), in1=dDv, op=ALU.mult),
                 reads=[hb["R_q"], RD], writes=[pbf["R_qs"]])
            yield
            mm2(BN, lambda i: kT[:, (n0 + i) * 128:(n0 + i + 1) * 128], lambda i: kT[:, (n0 + i) * 128:(n0 + i + 1) * 128], [hb["R_k"]])
            P.op(V, lambda e: e.tensor_tensor(out=Yv[0], in0=bv(BN), in1=dBv, op=ALU.mult), reads=[rpb[BN], RB], writes=[RY[0]])
            yield
            mm2(BN, lambda i: kT[:, (n0 + i) * 128:(n0 + i + 1) * 128], lambda i: qT[:, (n0 + i) * 128:(n0 + i + 1) * 128], [hb["R_k"], hb["R_q"]])
            P.op(V, lambda e: e.tensor_tensor(out=pbf["QKm"][:, tv, :], in0=bv(BN), in1=dCv, op=ALU.mult), reads=[rpb[BN], RC], writes=[pbf["R_QK"]])
            yield
            for i in range(2):
                P.op(T, lambda e, i=i: e.transpose(out=pb[BZ][:, (t0_ + i) * 128:(t0_ + i + 1) * 128], in_=Yv[0][:, i, :], identity=ident_f[:]),
                     reads=[RY[0], R_const], writes=[rpb[BZ]])
            yield
            P.op(S, lambda e: e.activation(out=Zv[0], in_=bv(BZ), func=AF.Copy), reads=[rpb[BZ]], writes=[RZ[0]])
            yield
            P.op(V, lambda e: e.tensor_tensor(out=Nv, in0=Zv[0], in1=idb2, op=ALU.add), reads=[RZ[0], R_const], writes=[RN])
            yield
            cur = 0
            for lv in range(1, 7):
                nx = 1 - cur
                mm2(BY, lambda i, cur=cur: Zv[cur][:, i, :], lambda i, cur=cur: Yv[cur][:, i, :], [RZ[cur], RY[cur]])
                yield
                if lv <= 5:
                    mm2(BZ, lambda i, cur=cur: Yv[cur][:, i, :], lambda i, cur=cur: Zv[cur][:, i, :], [RZ[cur], RY[cur]])
                    yield
                P.op(S, lambda e, nx=nx: e.activation(out=Yv[nx], in_=bv(BY), func=AF.Copy), reads=[rpb[BY]], writes=[RY[nx]])
                yield
                if lv <= 5:
                    P.op(V, lambda e, nx=nx: e.tensor_copy(out=Zv[nx], in_=bv(BZ)), reads=[rpb[BZ]], writes=[RZ[nx]])
                    yield
                mm2(BN, lambda i, nx=nx: Yv[nx][:, i, :], lambda i: Nv[:, i, :], [RY[nx], RN])
                yield
                P.op(V, lambda e: e.tensor_tensor(out=Nv, in0=Nv, in1=bv(BN), op=ALU.add), reads=[RN, rpb[BN]], writes=[RN])
                yield
                cur = nx
            P.op(S, lambda e: e.activation(out=pbf["TTb"][:, tv, :], in_=Nv, func=AF.Copy), reads=[RN], writes=[pbf["R_TT"]])
            yield
            mm2(BN, lambda i: hb["kbg"][:, n0 + i, :], lambda i: pbf["TTb"][:, t0_ + i, :], [hb["R_tok"], pbf["R_TT"]])
            yield
            P.op(S, lambda e: e.activation(out=pbf["nwT"][:, tv, :], in_=bv(BN), func=AF.Copy, scale=-1.0), reads=[rpb[BN]], writes=[pbf["R_nw"]])
            yield

        def dn_scan(h, jb):
            hb = HB[h % 2]
            pbf = PB[(h * NBt + jb) % 2]
            n0 = jb * 4
            VN = ptrf[:, 256:384]
            OO = ptrf[:, 384:512]
            DS = ptrf[:, 256:384]
            rS = rptr
            if jb == 0:
                P.op(V, lambda e: e.memset(Sf, 0.0), writes=[R_Sf])
                P.op(V, lambda e: e.memset(Sb, 0.0), writes=[R_Sb])
                yield
            for i in range(4):
                n = n0 + i
                P.op(T, lambda e, i=i, n=n: e.matmul(VN, lhsT=pbf["TTb"][:, i, :], rhs=hb["vb"][:, n, :], start=True, stop=False), reads=[pbf["R_TT"], hb["R_tok"]], writes=[rS])
                P.op(T, lambda e, i=i: e.matmul(VN, lhsT=pbf["nwT"][:, i, :], rhs=Sb, start=False, stop=True), reads=[pbf["R_nw"], R_Sb], writes=[rS])
                yield
                P.op(S, lambda e: e.activation(out=vnb, in_=VN, func=AF.Copy), reads=[rS], writes=[R_vn])
                yield
                P.op(T, lambda e, i=i: e.matmul(OO, lhsT=pbf["qsT"][:, i, :], rhs=Sb, start=True, stop=False), reads=[pbf["R_qs"], R_Sb], writes=[rS])
                P.op(T, lambda e, i=i: e.matmul(OO, lhsT=pbf["QKm"][:, i, :], rhs=vnb, start=False, stop=True), reads=[pbf["R_QK"], R_vn], writes=[rS])
                P.op(T, lambda e, n=n: e.matmul(DS, lhsT=hb["ksc"][:, n, :], rhs=vnb, start=True, stop=True), reads=[hb["R_tok"], R_vn], writes=[rS])
                yield
                P.op(V, lambda e, n=n: e.scalar_tensor_tensor(out=Sf, in0=Sf, scalar=DEC[:, n, h:h + 1], in1=DS, op0=ALU.mult, op1=ALU.add),
                     reads=[R_Sf, R_sm, rS], writes=[R_Sf])
                yield
                P.op(S, lambda e: e.activation(out=Sb, in_=Sf, func=AF.Copy), reads=[R_Sf], writes=[R_Sb])
                yield
                P.op(S, lambda e: e.activation(out=jk, in_=OO, func=AF.Square, accum_out=fs[:, 0:1]), reads=[rS], writes=[R_jk, R_fs])
                yield
                P.op(S, lambda e: e.activation(out=fs[:, 1:2], in_=fs[:, 0:1], func=AF.Ln, scale=1.0 / 128, bias=EPSC), reads=[R_fs, R_const], writes=[R_fs])
                yield
                P.op(S, lambda e: e.activation(out=fs[:, 2:3], in_=fs[:, 1:2], func=AF.Exp, scale=-0.5), reads=[R_fs], writes=[R_fs])
                yield
                P.op(V, lambda e, n=n: e.scalar_tensor_tensor(out=mxd, in0=OO, scalar=fs[:, 2:3], in1=siluz[:, n, h * 128:(h + 1) * 128],
                                                             op0=ALU.mult, op1=ALU.mult), reads=[rS, R_fs, R_sz[n]], writes=[R_mxd])
                yield
                P.op(T, lambda e: e.transpose(out=ptr[:, 256:384], in_=mxd, identity=ident_bf[:]), reads=[R_mxd, R_const], writes=[rptr])
                yield
                P.op(V, lambda e, n=n: e.tensor_copy(out=mixT[:, h, n * 128:(n + 1) * 128], in_=ptr[:, 256:384]), reads=[rptr], writes=[R_mixT[h][n]])
                yield

        g0_ = dn_prep(0)
        while not qk_done[0]:
            next(g0_)
        run_rr([dn_pre(0, 0, 0), dn_pre(0, 0, 1), g0_])
        for h in range(4):
            bg[0] = dn_prep(h + 1) if h + 1 < 4 else None
            for jb in range(NBt):
                if jb + 1 < NBt:
                    nxt = (dn_pre(h, jb + 1, 0), dn_pre(h, jb + 1, 1))
                elif h + 1 < 4:
                    while bg[0] is not None:
                        bg_step()
                    nxt = (dn_pre(h + 1, 0, 0), dn_pre(h + 1, 0, 1))
                else:
                    nxt = (None, None)
                run_rr([nxt[0], nxt[1], dn_scan(h, jb)], use_bg=True)
            while bg[0] is not None:
                bg_step()

        if dbg:
            P.barrier()
            main.reset(base_off)
            dbo = main.alloc([8 * L], F32)
            R_dbo = Res()
            P.op(V, lambda e: e.tensor_copy(out=dbo, in_=mixT.rearrange("p a b -> p (a b)")), writes=[R_dbo])
            P.dma("sync", lambda e: e.dma_start(out=dbg_d[:, :], in_=dbo), reads=[R_dbo], is_output=True)
            P.barrier()

        P.barrier()
        if L >= 2048:
            main.reset(0)
            wu = [main.alloc([8, 1024], BF16) for _ in range(2)]
            main.reset(base_off0)
        else:
            main.reset(base_off0)
            wu = [main.alloc([8, 1024], BF16) for _ in range(2)]
        wo = main.alloc([8, D], BF16)
        R_wo = Res()
        P.dma("sync", lambda e: e.dma_start(out=wo, in_=wout_s[:, :, :]), reads=[R_wrest], writes=[R_wo])
        x1b = [main.alloc([4, D], F32) for _ in range(2)]
        R_x1b = [[Res() for _ in range(4)] for _ in range(2)]
        hmTb = [main.alloc([8, 512], BF16) for _ in range(2)]
        R_hmb2 = [[Res() for _ in range(4)] for _ in range(2)]
        hT = main.alloc([8, 512], BF16)
        R_hT = [Res() for _ in range(8)]
        wd = [main.alloc([8, 1024], BF16) for _ in range(2)]
        R_wu = [Res(), Res()]
        R_wd = [Res(), Res()]
        xr = [main.alloc([D], F32) for _ in range(2)]
        R_xr = [Res(), Res()]
        xo = [main.alloc([D], F32) for _ in range(2)]
        R_xo = [Res(), Res()]
        hmb = main.alloc([D], BF16)
        R_hmb = Res()
        rl = [main.alloc([512], BF16) for _ in range(2)]
        R_rl = [Res(), Res()]
        mstb = [main.alloc([4, 8], F32) for _ in range(2)]
        R_mstb = [Res(), Res()]
        jkp = main.alloc([D], BF16)
        jke = main.alloc([D], BF16)
        R_jkp, R_jke = Res(), Res()
        NBLK = L // 512
        wq_cnt = [0]

        def mlp_pro(blk):
            pb_ = blk % 2
            x1, hmT, mst, R_x1, R_hm, R_mst = x1b[pb_], hmTb[pb_], mstb[pb_], R_x1b[pb_], R_hmb2[pb_], R_mstb[pb_]
            for i in range(4):
                n = blk * 4 + i
                ts_ = slice(n * 128, (n + 1) * 128)
                xi = n % 2
                P.dma("scalar", lambda e, xi=xi, ts_=ts_, b=b: e.dma_start(out=xr[xi], in_=x_d[b, ts_, :]), writes=[R_xr[xi]])
                for c2 in range(2):
                    for k in range(8):
                        P.op(T, lambda e, k=k, c2=c2, ts_=ts_: e.matmul(pb[c2][:, :], lhsT=mixT[:, k, ts_], rhs=wo[:, k, c2 * 512:(c2 + 1) * 512],
                                                                       start=(k == 0), stop=(k == 7)),
                             reads=[R_mixT[k][n], R_wo], writes=[rpb[c2]])
                    yield
                    P.op(V, lambda e, c2=c2, i=i, xi=xi: e.tensor_tensor(out=x1[:, i, c2 * 512:(c2 + 1) * 512], in0=pb[c2][:, :],
                                                                          in1=xr[xi][:, c2 * 512:(c2 + 1) * 512], op=ALU.add),
                         reads=[rpb[c2], R_xr[xi]], writes=[R_x1[i]])
                    yield
                P.op(S, lambda e, i=i: e.activation(out=jkp, in_=x1[:, i, :], func=AF.Square, accum_out=mst[:, i, 0:1]), reads=[R_x1[i]], writes=[R_jkp, R_mst])
                yield
                P.op(S, lambda e, i=i: e.activation(out=mst[:, i, 1:2], in_=mst[:, i, 0:1], func=AF.Ln, scale=1.0 / D, bias=EPSC), reads=[R_mst, R_const], writes=[R_mst])
                yield
                P.op(S, lambda e, i=i: e.activation(out=mst[:, i, 2:3], in_=mst[:, i, 1:2], func=AF.Exp, scale=-0.5), reads=[R_mst], writes=[R_mst])
                yield
                P.op(V, lambda e, i=i: e.tensor_scalar(out=hmb, in0=x1[:, i, :], scalar1=mst[:, i, 2:3], scalar2=None, op0=ALU.mult),
                     reads=[R_x1[i], R_mst], writes=[R_hmb])
                yield
                for k in range(8):
                    P.op(T, lambda e, k=k: e.transpose(out=ptr[:, k * 128:(k + 1) * 128], in_=hmb[:, k * 128:(k + 1) * 128], identity=ident_bf[:]),
                         reads=[R_hmb, R_const], writes=[rptr])
                P.op(S, lambda e, i=i: e.activation(out=hmT[:, :, i * 128:(i + 1) * 128], in_=ptr[:].rearrange("p (k t) -> p k t", k=8), func=AF.Copy),
                     reads=[rptr], writes=[R_hm[i]])
                yield

        def mlp_main(blk):
            pb_ = blk % 2
            x1, hmT, R_x1, R_hm = x1b[pb_], hmTb[pb_], R_x1b[pb_], R_hmb2[pb_]
            for fq in range(4):
                wi = wq_cnt[0] % 2
                wq_cnt[0] += 1
                P.dma("sync", lambda e, wi=wi, fq=fq: e.dma_start(out=wu[wi], in_=wup_s[:, :, fq * 1024:(fq + 1) * 1024]), reads=[R_wrest], writes=[R_wu[wi]])
                P.dma("sync", lambda e, wi=wi, fq=fq: e.dma_start(out=wd[wi], in_=wdn_s[:, fq * 8:(fq + 1) * 8, :]), reads=[R_wrest], writes=[R_wd[wi]])
                for fc in range(8):
                    pz = 2 + fc % 2
                    for k in range(8):
                        P.op(T, lambda e, k=k, fc=fc, wi=wi, pz=pz: e.matmul(pb[pz][:, :], lhsT=wu[wi][:, k, fc * 128:(fc + 1) * 128], rhs=hmT[:, k, :],
                                                                          start=(k == 0), stop=(k == 7)),
                             reads=[R_wu[wi]] + R_hm, writes=[rpb[pz]])
                    ri = fc % 2
                    P.op(S, lambda e, pz=pz, ri=ri: e.activation(out=rl[ri], in_=pb[pz][:, :], func=AF.Relu), reads=[rpb[pz]], writes=[R_rl[ri]])
                    P.op(G, lambda e, fc=fc, ri=ri: e.tensor_tensor(out=hT[:, fc, :], in0=rl[ri], in1=rl[ri], op=ALU.mult), reads=[R_rl[ri]], writes=[R_hT[fc]])
                    yield
                for i in range(4):
                    for c2 in range(2):
                        pz = 4 + c2
                        for fc in range(8):
                            P.op(T, lambda e, fc=fc, i=i, c2=c2, wi=wi, pz=pz: e.matmul(pb[pz][:, :], lhsT=hT[:, fc, i * 128:(i + 1) * 128],
                                                                                      rhs=wd[wi][:, fc, c2 * 512:(c2 + 1) * 512],
                                                                                      start=(fc == 0), stop=(fc == 7)),
                                 reads=[R_hT[fc], R_wd[wi]], writes=[rpb[pz]])
                        P.op(V, lambda e, i=i, c2=c2, pz=pz: e.tensor_tensor(out=x1[:, i, c2 * 512:(c2 + 1) * 512], in0=x1[:, i, c2 * 512:(c2 + 1) * 512],
                                                                           in1=pb[pz][:, :], op=ALU.add),
                             reads=[R_x1[i], rpb[pz]], writes=[R_x1[i]])
                        yield

        def mlp_epi(blk):
            pb_ = blk % 2
            x1, mst, R_x1, R_mst = x1b[pb_], mstb[pb_], R_x1b[pb_], R_mstb[pb_]
            for i in range(4):
                n = blk * 4 + i
                xi = n % 2
                P.op(S, lambda e, i=i: e.activation(out=jke, in_=x1[:, i, :], func=AF.Square, accum_out=mst[:, i, 3:4]), reads=[R_x1[i]], writes=[R_jke, R_mst])
                yield
                P.op(S, lambda e, i=i: e.activation(out=mst[:, i, 4:5], in_=mst[:, i, 3:4], func=AF.Ln, scale=1.0 / D, bias=EPSC), reads=[R_mst, R_const], writes=[R_mst])
                yield
                P.op(S, lambda e, i=i: e.activation(out=mst[:, i, 5:6], in_=mst[:, i, 4:5], func=AF.Exp, scale=-0.5), reads=[R_mst], writes=[R_mst])
                yield
                P.op(V, lambda e, i=i, xi=xi: e.scalar_tensor_tensor(out=xo[xi], in0=x1[:, i, :], scalar=mst[:, i, 5:6], in1=fnw_bc[:], op0=ALU.mult, op1=ALU.mult),
                     reads=[R_x1[i], R_mst, R_const], writes=[R_xo[xi]])
                P.dma("scalar", lambda e, xi=xi, n=n, b=b: e.dma_start(out=y_d[b, n * 128:(n + 1) * 128, :], in_=xo[xi]), reads=[R_xo[xi]], is_output=True)
                yield

        def seq_gens(*gens):
            for g in gens:
                if g is not None:
                    yield from g

        run_rr([mlp_pro(0)])
        for blk in range(NBLK):
            side = seq_gens(mlp_epi(blk - 1) if blk >= 1 else None, mlp_pro(blk + 1) if blk + 1 < NBLK else None)
            run_rr([side, mlp_main(blk)])
        run_rr([mlp_epi(NBLK - 1)])
        P.barrier()

    P.finish()
    P.emit()
    P.close()
    return nc


_NC_CACHE = {}


def kernel(x, positions, attn_norm_w, w_in, conv_w, a_log, dt_bias, dn_norm_w, lambda_q1, lambda_k1,
           lambda_q2, lambda_k2, diff_norm_w, group_scale, w_out, mlp_norm_w, w_up, w_down, final_norm_w):
    x = np.asarray(x, dtype=np.float32)
    B, L, _ = x.shape
    NB = B // NCORES
    key = (NB, L)
    if key not in _NC_CACHE:
        _NC_CACHE[key] = build(NB, L)
    nc = _NC_CACHE[key]

    def f(a):
        return np.ascontiguousarray(np.asarray(a, dtype=np.float32))
    shared = {
        "positions": np.ascontiguousarray(np.asarray(positions, dtype=np.int32)),
        "attn_norm_w": f(np.asarray(attn_norm_w)[0]), "w_in": f(np.asarray(w_in)[0]), "conv_w": f(np.asarray(conv_w)[0]),
        "a_log": f(np.asarray(a_log)[0]), "dt_bias": f(np.asarray(dt_bias)[0]), "dn_norm_w": f(np.asarray(dn_norm_w)[0]),
        "lambda_q1": f(np.asarray(lambda_q1)[0]), "lambda_k1": f(np.asarray(lambda_k1)[0]),
        "lambda_q2": f(np.asarray(lambda_q2)[0]), "lambda_k2": f(np.asarray(lambda_k2)[0]),
        "diff_norm_w": f(np.asarray(diff_norm_w)[0]), "group_scale": f(np.asarray(group_scale)[0]),
        "w_out": f(np.asarray(w_out)[0]), "mlp_norm_w": f(np.asarray(mlp_norm_w)[0]), "w_up": f(np.asarray(w_up)[0]),
        "w_down": f(np.asarray(w_down)[0]), "final_norm_w": f(final_norm_w),
    }
    in_maps = []
    for c in range(NCORES):
        m = dict(shared)
        m["x"] = np.ascontiguousarray(x[c * NB:(c + 1) * NB])
        in_maps.append(m)
    res = run_bass_kernel_spmd(nc, in_maps, core_ids=list(range(NCORES)))
    return np.concatenate([np.asarray(r["y"]) for r in res.results], axis=0).astype(np.float32)
```
